# Optimizing a Trainium2 kernel written in Bass

```python
import jax, jax.numpy as jnp
from jax import lax
import numpy as np

D_MODEL = 2048
BATCH = 2
SEQ = 4096
DEPTH = 1
DEC_BATCH = 128
DEC_SEQ = 4
PAST_LEN = 16384
PAGE_SIZE = 128

POOL_WIDTH = D_MODEL // 2
POOL_WINDOWS = (2, 4, 8, 16)
N_POOL_GROUPS = len(POOL_WINDOWS)
POOL_GROUP = POOL_WIDTH // N_POOL_GROUPS
POOL_HIST = max(POOL_WINDOWS) - 1
HEAD_DIM = 64
N_HEADS = (D_MODEL - POOL_WIDTH) // HEAD_DIM
N_KV_HEADS = 4
GQA_GROUP = N_HEADS // N_KV_HEADS
ATTN_WIDTH = N_HEADS * HEAD_DIM
KV_WIDTH = N_KV_HEADS * HEAD_DIM
IN_WIDTH = POOL_WIDTH + ATTN_WIDTH + 2 * KV_WIDTH
MIX_WIDTH = POOL_WIDTH + ATTN_WIDTH
WINDOW = 128
BLOCK = 128
ROPE_DIM = HEAD_DIM // 4
ROPE_THETA = 500000.0
D_FF = ((8 * D_MODEL // 3 + 255) // 256) * 256
EPS = 1e-5

kernel_name = "hybrid_pool_swa_sink_decoder_step"


def rms_norm(x, g):
    xf = x.astype(jnp.float32)
    y = xf * lax.rsqrt(jnp.mean(xf * xf, axis=-1, keepdims=True) + EPS)
    return (y * g.astype(jnp.float32)).astype(x.dtype)


def rope(x, pos):
    half = ROPE_DIM // 2
    inv = ROPE_THETA ** (-jnp.arange(0, ROPE_DIM, 2, dtype=jnp.float32) / ROPE_DIM)
    ang = pos.astype(jnp.float32)[:, None] * inv[None, :]
    cos = jnp.cos(ang)[:, None, :]
    sin = jnp.sin(ang)[:, None, :]
    xr = x[..., :ROPE_DIM].astype(jnp.float32)
    x1, x2 = xr[..., :half], xr[..., half:]
    rot = jnp.concatenate([x1 * cos - x2 * sin, x2 * cos + x1 * sin], axis=-1).astype(x.dtype)
    return jnp.concatenate([rot, x[..., ROPE_DIM:]], axis=-1)


def pool_mix(u, prev, start_pos, w_pool, pool_scale):
    b, t, c = u.shape
    ext = jnp.concatenate([prev.astype(u.dtype), u], axis=1)
    extf = ext.astype(jnp.float32)
    cs = jnp.concatenate([jnp.zeros((b, 1, c), jnp.float32), jnp.cumsum(extf, axis=1)], axis=1)
    end = cs[:, POOL_HIST + 1:POOL_HIST + 1 + t]
    pos = start_pos + jnp.arange(t)
    means = []
    for gi, w in enumerate(POOL_WINDOWS):
        sl = slice(gi * POOL_GROUP, (gi + 1) * POOL_GROUP)
        begin = cs[:, POOL_HIST + 1 - w:POOL_HIST + 1 - w + t, sl]
        cnt = jnp.minimum(pos + 1, w).astype(jnp.float32)[:, None]
        means.append((end[..., sl] - begin) / cnt)
    pooled = jnp.concatenate(means, axis=-1)
    d = (pooled - u.astype(jnp.float32)).astype(u.dtype).reshape(b, t, N_POOL_GROUPS, POOL_GROUP)
    out = jnp.einsum('btgc,gcd->btgd', d, w_pool).reshape(b, t, c) * pool_scale
    return out.astype(u.dtype), ext[:, -POOL_HIST:]


def sink_attention(q, k, v, mask, sinks):
    s = jnp.einsum('...qkgd,...skd->...kgqs', q, k).astype(jnp.float32) * (HEAD_DIM ** -0.5)
    s = jnp.where(mask, s, -jnp.inf)
    sink = sinks.astype(jnp.float32).reshape(N_KV_HEADS, GQA_GROUP, 1, 1)
    m = jnp.maximum(jnp.max(s, axis=-1, keepdims=True), sink)
    p = jnp.exp(s - m)
    p = p / (jnp.sum(p, axis=-1, keepdims=True) + jnp.exp(sink - m))
    return jnp.einsum('...kgqs,...skd->...qkgd', p.astype(v.dtype), v)


def swa_prompt(q, k, v, sinks):
    b, s = q.shape[0], q.shape[1]
    nb = s // BLOCK
    qb = q.reshape(b, nb, BLOCK, N_KV_HEADS, GQA_GROUP, HEAD_DIM)
    zpad = jnp.zeros((b, BLOCK, N_KV_HEADS, HEAD_DIM), k.dtype)
    kp = jnp.concatenate([zpad, k], axis=1).reshape(b, nb + 1, BLOCK, N_KV_HEADS, HEAD_DIM)
    vp = jnp.concatenate([zpad, v], axis=1).reshape(b, nb + 1, BLOCK, N_KV_HEADS, HEAD_DIM)
    kb = jnp.concatenate([kp[:, :-1], kp[:, 1:]], axis=2)
    vb = jnp.concatenate([vp[:, :-1], vp[:, 1:]], axis=2)
    qi = jnp.arange(BLOCK)[:, None] + BLOCK
    kj = jnp.arange(2 * BLOCK)[None, :]
    diff = qi - kj
    n = jnp.arange(nb)[:, None, None]
    valid = (diff >= 0) & (diff <= WINDOW) & ((n * BLOCK - BLOCK + kj) >= 0)
    mask = valid[:, None, None, :, :]
    o = sink_attention(qb, kb, vb, mask, sinks)
    return o.reshape(b, s, ATTN_WIDTH)


def swa_sample(q, k_new, v_new, k_cache, v_cache, sinks):
    bd, t = q.shape[0], q.shape[1]
    w = k_cache.shape[1]
    k_all = jnp.concatenate([k_cache.astype(k_new.dtype), k_new], axis=1)
    v_all = jnp.concatenate([v_cache.astype(v_new.dtype), v_new], axis=1)
    qpos = PAST_LEN + jnp.arange(t)
    kpos = PAST_LEN - w + jnp.arange(w + t)
    diff = qpos[:, None] - kpos[None, :]
    mask = (diff >= 0) & (diff <= WINDOW) & (kpos[None, :] >= 0)
    qg = q.reshape(bd, t, N_KV_HEADS, GQA_GROUP, HEAD_DIM)
    o = sink_attention(qg, k_all, v_all, mask, sinks)
    return o.reshape(bd, t, ATTN_WIDTH), k_all[:, -w:], v_all[:, -w:]


def project(x, g_mix, w_in, pos):
    b, t, _ = x.shape
    h = rms_norm(x, g_mix)
    proj = h @ w_in
    u = proj[..., :POOL_WIDTH]
    q = proj[..., POOL_WIDTH:POOL_WIDTH + ATTN_WIDTH].reshape(b, t, N_HEADS, HEAD_DIM)
    k = proj[..., POOL_WIDTH + ATTN_WIDTH:POOL_WIDTH + ATTN_WIDTH + KV_WIDTH].reshape(b, t, N_KV_HEADS, HEAD_DIM)
    v = proj[..., POOL_WIDTH + ATTN_WIDTH + KV_WIDTH:].reshape(b, t, N_KV_HEADS, HEAD_DIM)
    return u, rope(q, pos), rope(k, pos), v


def finish(x, pool_out, attn_out, w_out, g_ffn, w_gate, w_up, w_down):
    x = x + jnp.concatenate([pool_out, attn_out], axis=-1) @ w_out
    h = rms_norm(x, g_ffn)
    return x + (jax.nn.silu(h @ w_gate) * (h @ w_up)) @ w_down


def setup_inputs(seed: int = 0) -> dict:
    key = jax.random.key(seed)
    ks = jax.random.split(key, 18)
    f32 = jnp.float32
    win_buf = min(WINDOW, PAST_LEN)
    nrm = lambda k, shape, scale: jax.random.normal(k, shape, f32) * scale
    return {
        "x_prompt": nrm(ks[0], (BATCH, SEQ, D_MODEL), 1.0),
        "x_sample": nrm(ks[1], (DEC_BATCH, DEC_SEQ, D_MODEL), 1.0),
        "state_pool": nrm(ks[2], (DEPTH, DEC_BATCH, POOL_HIST, POOL_WIDTH), 1.0),
        "cache_k_win": nrm(ks[3], (DEPTH, DEC_BATCH, win_buf, N_KV_HEADS, HEAD_DIM), 1.0),
        "cache_v_win": nrm(ks[4], (DEPTH, DEC_BATCH, win_buf, N_KV_HEADS, HEAD_DIM), 1.0),
        "g_mix": 1.0 + nrm(ks[5], (DEPTH, D_MODEL), 0.05),
        "w_in": nrm(ks[6], (DEPTH, D_MODEL, IN_WIDTH), D_MODEL ** -0.5),
        "w_pool": nrm(ks[7], (DEPTH, N_POOL_GROUPS, POOL_GROUP, POOL_GROUP), POOL_GROUP ** -0.5),
        "pool_scale": 1.0 + nrm(ks[8], (DEPTH, POOL_WIDTH), 0.1),
        "attn_sinks": nrm(ks[9], (DEPTH, N_HEADS), 1.0),
        "w_out": nrm(ks[10], (DEPTH, MIX_WIDTH, D_MODEL), MIX_WIDTH ** -0.5),
        "g_ffn": 1.0 + nrm(ks[11], (DEPTH, D_MODEL), 0.05),
        "w_gate": nrm(ks[12], (DEPTH, D_MODEL, D_FF), D_MODEL ** -0.5),
        "w_up": nrm(ks[13], (DEPTH, D_MODEL, D_FF), D_MODEL ** -0.5),
        "w_down": nrm(ks[14], (DEPTH, D_FF, D_MODEL), D_FF ** -0.5),
        "g_final": 1.0 + nrm(ks[15], (D_MODEL,), 0.05),
    }


def reference(x_prompt, x_sample, state_pool, cache_k_win, cache_v_win, g_mix, w_in, w_pool,
              pool_scale, attn_sinks, w_out, g_ffn, w_gate, w_up, w_down, g_final):
    xp, xs = x_prompt, x_sample
    pos_p = jnp.arange(xp.shape[1])
    pos_s = PAST_LEN + jnp.arange(xs.shape[1])
    win_p = min(WINDOW, xp.shape[1])
    pool_p, kw_p, vw_p, pool_s, kw_s, vw_s = [], [], [], [], [], []
    for l in range(DEPTH):
        u, q, k, v = project(xp, g_mix[l], w_in[l], pos_p)
        zeros_hist = jnp.zeros((xp.shape[0], POOL_HIST, POOL_WIDTH), u.dtype)
        po, ps = pool_mix(u, zeros_hist, 0, w_pool[l], pool_scale[l])
        ao = swa_prompt(q, k, v, attn_sinks[l])
        xp = finish(xp, po, ao, w_out[l], g_ffn[l], w_gate[l], w_up[l], w_down[l])
        pool_p.append(ps)
        kw_p.append(k[:, -win_p:])
        vw_p.append(v[:, -win_p:])
        u, q, k, v = project(xs, g_mix[l], w_in[l], pos_s)
        po, ps = pool_mix(u, state_pool[l], PAST_LEN, w_pool[l], pool_scale[l])
        ao, kn, vn = swa_sample(q, k, v, cache_k_win[l], cache_v_win[l], attn_sinks[l])
        xs = finish(xs, po, ao, w_out[l], g_ffn[l], w_gate[l], w_up[l], w_down[l])
        pool_s.append(ps)
        kw_s.append(kn)
        vw_s.append(vn)
    y_prompt = rms_norm(xp, g_final)
    y_sample = rms_norm(xs, g_final)
    return (y_prompt, y_sample, jnp.stack(pool_p), jnp.stack(kw_p), jnp.stack(vw_p),
            jnp.stack(pool_s), jnp.stack(kw_s), jnp.stack(vw_s))
```

```python
import contextlib
import numpy as np
import ml_dtypes
import concourse.bass as bass
import concourse.mybir as mybir
from concourse.bass_utils import run_bass_kernel_spmd

F32 = mybir.dt.float32
BF16 = mybir.dt.bfloat16
AF = mybir.ActivationFunctionType
ALU = mybir.AluOpType
AX = mybir.AxisListType

D = 2048
DFF = 5632
NCORE = 8
PAST = 16384
EPS = 1e-5
NEG = -30000.0
PERM = [0, 4, 1, 5, 2, 6, 3, 7, 8, 12, 9, 13, 10, 14, 11, 15]
ENGS = ['pe', 'act', 'dve', 'pool', 'sp']
NORM_ENG = 'pool'
NDSEM = {'sp': 12, 'pool': 6, 'act': 4, 'pe': 1, 'dve': 1}

C_GMIX, C_GFFN, C_PSC, C_SINK, C_NSINK = 0, 16, 32, 40, 56
C_MN, C_MF, C_MS, C_ROPE, C_INVC = 72, 328, 584, 840, 1160
CW = 1288


class Op:
    pass


class Planner:
    def __init__(self):
        self.ops = []
        self.lw = {}
        self.rd = {}
        self.cnt = {e: 0 for e in ENGS}
        self.rr = {e: 0 for e in ENGS}
        self.dlast = {}
        self.dcnt = {}
        self.reg_last = {}

    def op(self, eng, fn, R=(), W=(), dma=False, extra=(), reg=None):
        o = Op()
        o.eng, o.fn, o.dma = eng, fn, dma
        deps = {}
        for k in R:
            w = self.lw.get(k)
            if w is not None:
                deps[w] = True
        for k in W:
            w = self.lw.get(k)
            if w is not None:
                deps.setdefault(w, False)
            for r in self.rd.get(k, ()):
                deps.setdefault(r, False)
        for e in extra:
            if e is not None:
                deps[e] = True
        if dma:
            si = self.rr[eng]
            self.rr[eng] = (si + 1) % NDSEM[eng]
            prev = self.dlast.get((eng, si))
            if prev is not None:
                deps[prev] = True
            self.dlast[(eng, si)] = o
            n = self.dcnt.get((eng, si), 0) + 1
            self.dcnt[(eng, si)] = n
            o.dsem, o.dval = (eng, si), 16 * n
        o.deps = deps
        for k in W:
            self.lw[k] = o
            self.rd[k] = []
        for k in R:
            self.rd.setdefault(k, []).append(o)
        o.pos = self.cnt[eng]
        self.cnt[eng] += 1
        o.mark = False
        o.ms = 0
        self.ops.append(o)
        if reg is not None:
            for r in (reg if isinstance(reg, (list, tuple)) else [reg]):
                if dma:
                    self.reg_last.setdefault(r, {}).setdefault('dmas', []).append(o)
                else:
                    self.reg_last.setdefault(r, {})[eng] = o
        return o

    def fence(self, *regs):
        out = []
        for r in regs:
            for k, v in self.reg_last.get(r, {}).items():
                out += v if k == 'dmas' else [v]
        return out

    def all_last(self):
        last = {}
        dmas = []
        for o in self.ops:
            if o.dma:
                dmas.append(o)
            else:
                last[o.eng] = o
        return list(last.values()) + dmas

    def resolve(self):
        waited = {e: {} for e in ENGS}
        for o in self.ops:
            o.waits = []
            best = {}
            for a, raw in o.deps.items():
                if a is o:
                    continue
                if a.dma:
                    key = ('d',) + a.dsem
                    if waited[o.eng].get(key, 0) < a.dval:
                        waited[o.eng][key] = a.dval
                        o.waits.append(a)
                    continue
                if a.eng == o.eng and not o.dma:
                    if o.eng == 'pe':
                        continue
                if a.eng not in best or best[a.eng].pos < a.pos:
                    best[a.eng] = a
            for e, a in best.items():
                if waited[o.eng].get(e, -1) < a.pos:
                    waited[o.eng][e] = a.pos
                    a.mark = True
                    o.waits.append(a)
        c = {e: 0 for e in ENGS}
        for o in self.ops:
            if o.mark:
                c[o.eng] += 1
                o.ms = c[o.eng]

    def emit(self, block, sems, dsems):
        deco = {'pe': block.tensor, 'act': block.scalar, 'dve': block.vector,
                'pool': block.gpsimd, 'sp': block.sync}
        for eng in ENGS:
            ops = [o for o in self.ops if o.eng == eng]

            def body(e, ops=ops):
                for o in ops:
                    for a in o.waits:
                        if a.dma:
                            e.wait_ge(dsems[a.dsem], a.dval)
                        else:
                            e.wait_ge(sems[a.eng], a.ms)
                    if o.fn is None:
                        continue
                    ins = o.fn(e)
                    if o.dma:
                        ins.then_inc(dsems[o.dsem], 16)
                    elif o.mark:
                        ins.then_inc(sems[o.eng], 1)
            deco[eng](body)


class _Stop(Exception):
    pass


def build_program(stop_at=None):
    nc = bass.Bass("TRN2", target_bir_lowering=False)

    def din(name, shape, dt=F32):
        return nc.dram_tensor(name, list(shape), dt, kind="ExternalInput").ap()

    def dout(name, shape):
        return nc.dram_tensor(name, list(shape), F32, kind="ExternalOutput").ap()

    xp = din("xp", [1024, D]); xh = din("xh", [128, D]); xs = din("xs", [64, D])
    spd = din("sp", [16, 15, 1024]); ckd = din("ck", [16, 128, 256]); cvd = din("cv", [16, 128, 256])
    w_in = din("w_in", [D, 2560]); w_pool = din("w_pool", [4, 256, 256]); w_out = din("w_out", [D, D])
    w_gate = din("w_gate", [D, DFF]); w_up = din("w_up", [D, DFF]); w_down = din("w_down", [DFF, D])
    cst_d = din("cst", [128, CW]); gfin_d = din("gfin", [128, D]); bm_d = din("bm", [128, 1024], BF16)
    yp = dout("yp", [1024, D]); ys = dout("ys", [64, D])
    pool_p = dout("pool_p", [15, 1024]); k_p = dout("k_p", [128, 256]); v_p = dout("v_p", [128, 256])
    pool_s = dout("pool_s", [16, 15, 1024]); k_s = dout("k_s", [16, 128, 256]); v_s = dout("v_s", [16, 128, 256])

    P = Planner()
    stack = contextlib.ExitStack()
    with stack:
        def sb(name, shape, dt):
            return stack.enter_context(nc.sbuf_tensor(name, shape, dt))
        acts = sb("acts", [128, 16, 1216], BF16)
        ring = sb("ring", [128, 8, 4096], BF16)
        flex = sb("flex", [128, 18432], F32)
        scr = sb("scr", [128, 6144], F32)
        cst = sb("cst_sb", [128, CW], F32)
        bm = sb("bm_sb", [128, 16, 64], BF16)
        identf = sb("identf", [128, 128], F32)
        identb = sb("identb", [128, 128], BF16)
        stat = sb("stat", [128, 128], F32)
        small = sb("small", [128, 4, 64], F32)
        ps = stack.enter_context(nc.psum_tensor("ps", [128, 8, 512], F32))
        sems = {e: stack.enter_context(nc.semaphore("sem_" + e)) for e in ENGS}
        dsems = {}
        for e in ('sp', 'pool', 'act'):
            for i in range(NDSEM[e]):
                dsems[(e, i)] = stack.enter_context(nc.semaphore("dsem_%s_%d" % (e, i)))
        block = stack.enter_context(nc.Block())

        psb = ps[:].bitcast(BF16)
        ringf = ring[:].bitcast(F32)
        flexb = flex[:].bitcast(BF16)
        scrb = scr[:].bitcast(BF16)

        gmixT = cst[:, C_GMIX:C_GMIX + 16]; gffnT = cst[:, C_GFFN:C_GFFN + 16]
        pscT = cst[:, C_PSC:C_PSC + 8]
        sinkv = cst[:, C_SINK:C_SINK + 16]; nsinkv = cst[:, C_NSINK:C_NSINK + 16]
        mask_n = cst[:, C_MN:C_MN + 256]; mask_f = cst[:, C_MF:C_MF + 256]; mask_s = cst[:, C_MS:C_MS + 256]
        rope = cst[:, C_ROPE:C_ROPE + 320].rearrange("p (b c) -> p b c", b=10)
        invc = cst[:, C_INVC:C_INVC + 128].rearrange("p (c t) -> p c t", c=8)
        epst = stat[:, 127:128]
        ringhi = ring[:, 4:8, :].rearrange("p s e -> p (s e)")
        QT = ringhi[:, 0:8704].rearrange("p (i t) -> p i t", i=8)
        KT = ringhi[:, 8704:11136].rearrange("p (i t) -> p i t", i=2)
        Vb = ringhi[:, 11136:13696].rearrange("p (b c) -> p b c", b=10)
        kvf = ringf[:, 7, 800:1824].rearrange("p (a c) -> p a c", a=4)
        uT = flex[:, 0:8832].rearrange("p (c t) -> p c t", c=8)
        dT = flexb[:, 17664:26368].rearrange("p (c t) -> p c t", c=8)
        wsA = flex[:, 13184:14224]; wsB = flex[:, 14224:15264]
        usT = flex[:, 15264:17696].rearrange("p (c s h) -> p c s h", c=8, s=16)
        xres = flex[:, :].rearrange("p (b c) -> p b c", b=9)
        ckb = flexb[:, 0:4096].rearrange("p (s c) -> p s c", s=16)
        Vc = flexb[:, 4096:8192].rearrange("p (s c) -> p s c", s=16)
        KcT = flexb[:, 8192:12288].rearrange("p (k s j) -> p k s j", k=2, s=16)
        QTm = flexb[:, 12288:20480].rearrange("p (i s r) -> p i s r", i=8, s=16)
        PTm = [flexb[:, 20480 + 4096 * q:20480 + 4096 * (q + 1)].rearrange("p (i s r) -> p i s r", i=4, s=16) for q in range(2)]
        xa = [scr[:, 0:2048], scr[:, 2048:4096], flex[:, 8832:10880], flex[:, 10880:12928]]
        xnb = [scrb[:, 8192:10240], scrb[:, 10240:12288]]
        S_sb = [scr[:, 1024 * q:1024 * (q + 1)].rearrange("p (i k) -> p i k", i=4) for q in range(2)]
        Pb = [scrb[:, 4096 + 1024 * q:4096 + 1024 * (q + 1)].rearrange("p (i k) -> p i k", i=4) for q in range(2)]
        PTs = [scrb[:, 6144 + 1024 * q:6144 + 1024 * (q + 1)] for q in range(2)]
        atm = [scrb[:, 8192 + 1024 * q:8192 + 1024 * (q + 1)] for q in range(2)]
        ostg = scr[:, 5120:6144]
        atm_s = scrb[:, 10240:11264]
        qf = [scr[:, 2048 + 512 * q:2048 + 512 * (q + 1)] for q in range(2)]
        qb = [scrb[:, 6144 + 512 * q:6144 + 512 * (q + 1)] for q in range(2)]
        rtA = scr[:, 3584:3840].rearrange("p (h c) -> p h c", c=16)[:, 0:8, :]
        rtB = scr[:, 3840:4096].rearrange("p (h c) -> p h c", c=16)[:, 0:8, :]
        actT = [scrb[:, 4352 * q:4352 * (q + 1)].rearrange("p (c t) -> p c t", c=4) for q in range(2)]
        sgt = [scrb[:, 8704 + 512 * q:8704 + 512 * (q + 1)] for q in range(3)]

        def rows_of(b):
            return 64 if b == 9 else 128

        def chk(name):
            if stop_at == name:
                raise _Stop()

        try:
            P.op('sp', lambda e: e.dma_start(out=cst[:], in_=cst_d[:, :]), W=['cst'], dma=True)
            P.op('sp', lambda e: e.dma_start(out=bm[:].rearrange("p s r -> p (s r)"), in_=bm_d[:, :]), W=['bm'], dma=True)
            P.op('pool', lambda e: e.memset(identf[:], 0.0), W=['identf'])
            P.op('pool', lambda e: e.affine_select(out=identf[:], in_=identf[:], pattern=[[-1, 128]],
                                                   compare_op=ALU.not_equal, fill=1.0, base=0, channel_multiplier=1),
                 R=['identf'], W=['identf'])
            P.op('dve', lambda e: e.tensor_copy(out=identb[:], in_=identf[:]), R=['identf'], W=['identb'])
            P.op('dve', lambda e: e.memset(epst, EPS), W=['eps'])

            rs = {'n': 0, 'mod': 4}

            def next_slot():
                s = rs['n'] % rs['mod']
                rs['n'] += 1
                return s

            def load_unit(view_fn, src, eng='pool', extra=()):
                s = next_slot()
                o = P.op(eng, lambda e, s=s: e.dma_start(out=view_fn(s), in_=src), W=[('ring', s)], dma=True, extra=extra)
                return s

            def u16(s):
                return ring[:, s, :].rearrange("p (k c) -> p k c", k=16)

            def p16(s):
                return ring[:, s:s + 2, :].rearrange("p s e -> p (s e)").rearrange("p (k c) -> p k c", k=16)

            def load_pair(src, extra=()):
                if rs['n'] % 2:
                    rs['n'] += 1
                s0 = next_slot()
                s1 = next_slot()
                assert s1 == s0 + 1 and s0 % 2 == 0
                P.op('pool', lambda e: e.dma_start(out=p16(s0), in_=src), W=[('ring', s0), ('ring', s1)], dma=True, extra=extra)
                return s0

            def norm_T(b, src, srckeys, gT, sc, reg=None, extra=(), defer=None):
                rows = rows_of(b)
                q = b % 2
                c0 = b * 128
                P.op('act', lambda e: e.activation(out=xnb[q][:rows], in_=src, func=AF.Square, accum_out=stat[:rows, sc:sc + 1]),
                     R=srckeys, W=[('xnb', q), ('st', sc)], reg=reg, extra=extra)
                P.op('act', lambda e: e.activation(out=stat[:rows, sc + 10:sc + 11], in_=stat[:rows, sc:sc + 1], func=AF.Sqrt,
                                                   bias=epst[:rows], scale=1.0 / D),
                     R=[('st', sc), 'eps'], W=[('st', sc + 10)])
                P.op('dve', lambda e: e.reciprocal(out=stat[:rows, sc + 20:sc + 21], in_=stat[:rows, sc + 10:sc + 11]),
                     R=[('st', sc + 10)], W=[('st', sc + 20)])
                P.op('dve', lambda e: e.tensor_scalar(out=xnb[q][:rows], in0=src, scalar1=stat[:rows, sc + 20:sc + 21], scalar2=None, op0=ALU.mult),
                     R=srckeys + [('st', sc + 20)], W=[('xnb', q)], reg=reg)
                def back():
                  for h in range(2):
                    bank = 2 * q + h

                    def tp(e, h=h, bank=bank):
                        for j in range(8):
                            k = 8 * h + j
                            ins = e.transpose(out=psb[:, bank, j * 128:j * 128 + rows], in_=xnb[q][:rows, k * 128:(k + 1) * 128],
                                              identity=identb[:rows, :rows])
                        return ins
                    P.op('pe', tp, R=[('xnb', q), 'identb'], W=[('bank', bank)], reg=reg)
                    P.op('dve', lambda e, h=h, bank=bank: e.tensor_tensor(
                        out=acts[:, 8 * h:8 * h + 8, c0:c0 + rows],
                        in0=psb[:, bank, :].rearrange("p (j t) -> p j t", j=8)[:, :, :rows],
                        in1=gT[:, 8 * h:8 * h + 8].unsqueeze(2).to_broadcast([128, 8, rows]), op=ALU.mult),
                        R=[('bank', bank), 'cst'], W=[('acts', k, b) for k in range(8 * h, 8 * h + 8)], reg=reg)
                if defer is None:
                    back()
                else:
                    defer.append(back)

            xa_ops = {}
            for b in range(10):
                rows = rows_of(b)
                src = xh[:, :] if b == 0 else (xs[:, :] if b == 9 else xp[(b - 1) * 128:b * 128, :])
                q = b % 2
                q4 = b % 4
                xa_ops[b] = P.op('sp', lambda e, src=src, q4=q4, rows=rows: e.dma_start(out=xa[q4][:rows], in_=src), W=[('xa', q4)], dma=True, reg=['scr', 'flex', 'flexU'])
                norm_T(b, xa[q4][:rows], [('xa', q4)], gmixT, b, reg=['scr', 'flex', 'flexU'])

            ALLACT = [('acts', k, b) for k in range(16) for b in range(10)]
            chk('A')

            TT1 = [(112, 368), (480, 368), (848, 368)]
            for j in range(4):
                s = 4 + j
                P.op('pool', lambda e, s=s, j=j: e.dma_start(out=u16(s), in_=w_in[:, 256 * j:256 * j + 256].rearrange("(k p) c -> p k c", p=128)), W=[('ring', s)], dma=True, extra=([xa_ops[8]] if j == 0 else ()))
                for cc in range(2):
                    c = 2 * j + cc
                    b0 = 3 * (c % 2)
                    for tt, (t0, n) in enumerate(TT1):
                        def mm(e, s=s, cc=cc, tt=tt, t0=t0, n=n, b0=b0):
                            for k in range(16):
                                ins = e.matmul(ps[:, b0 + tt, :n], lhsT=u16(s)[:, k, cc * 128:(cc + 1) * 128], rhs=acts[:, k, t0:t0 + n],
                                               start=(k == 0), stop=(k == 15))
                            return ins
                        P.op('pe', mm, R=[('ring', s)] + [('acts', k, bb) for k in range(16) for bb in range(t0 // 128, (t0 + n - 1) // 128 + 1)], W=[('bank', b0 + tt)])
                        eng = 'act' if tt % 2 == 0 else 'dve'
                        if eng == 'act':
                            P.op('act', lambda e, c=c, tt=tt, t0=t0, n=n, b0=b0: e.copy(out=uT[:, c, t0 - 112:t0 - 112 + n], in_=ps[:, b0 + tt, :n]),
                                 R=[('bank', b0 + tt)], W=[('uT', c, tt)], reg=['flex', 'flexU'])
                        else:
                            P.op('dve', lambda e, c=c, tt=tt, t0=t0, n=n, b0=b0: e.tensor_copy(out=uT[:, c, t0 - 112:t0 - 112 + n], in_=ps[:, b0 + tt, :n]),
                                 R=[('bank', b0 + tt)], W=[('uT', c, tt)], reg=['flex', 'flexU'])
            UT = lambda c: [('uT', c, 0), ('uT', c, 1), ('uT', c, 2)]
            chk('B1')

            pool_work = []
            chk('B2o')
            s_sph = next_slot()
            sph = ringf[:, s_sph, :].rearrange("p (a c) -> p a c", a=2)
            P.op('sp', lambda e: e.dma_start(out=sph[:120], in_=spd.rearrange("(a s) h c -> (s h) a c", a=2)), W=[('ring', s_sph)], dma=True)
            for a in range(2):
                for g in range(2):
                    bank = 2 * a + g

                    def tp(e, a=a, g=g, bank=bank):
                        for cq in range(4):
                            c = 4 * g + cq
                            ins = e.transpose(out=ps[:, bank, cq * 120:(cq + 1) * 120], in_=sph[:120, a, c * 128:(c + 1) * 128], identity=identf[:120, :120])
                        return ins
                    P.op('pe', tp, R=[('ring', s_sph), 'identf'], W=[('bank', bank)])
                    P.op('act', lambda e, a=a, g=g, bank=bank: e.copy(
                        out=usT[:, 4 * g:4 * g + 4, 8 * a:8 * a + 8, 0:15],
                        in_=ps[:, bank, 0:480].rearrange("p (c s h) -> p c s h", c=4, s=8)),
                        R=[('bank', bank)], W=[('usT', 4 * g + cq, a) for cq in range(4)], reg='flex')
            P.op('dve', lambda e: e.tensor_copy(out=usT[:, :, :, 15:19], in_=uT[:, :, 1040:1104].rearrange("p c (t s) -> p c s t", t=4)),
                 R=[k for c in range(8) for k in UT(c)], W=[('usT', c, 2) for c in range(8)], reg=['flex', 'flexU'])
            USK = lambda c: [('usT', c, 0), ('usT', c, 1), ('usT', c, 2)]

            def tp_po(e):
                for c in range(8):
                    ins = e.transpose(out=ps[:16, 4 + c // 4, (c % 4) * 128:(c % 4 + 1) * 128], in_=uT[:, c, 1024:1040], identity=identf[:, :])
                return ins
            P.op('pe', tp_po, R=[k for c in range(8) for k in UT(c)] + ['identf'], W=[('bank', 4), ('bank', 5)], reg='flexU')
            P.op('act', lambda e: e.copy(out=ostg[:16, :], in_=ps[:16, 4:6, :].rearrange("p a c -> p (a c)")),
                 R=[('bank', 4), ('bank', 5)], W=['ostg'], reg='scr')
            P.op('sp', lambda e: e.dma_start(out=pool_p[:, :], in_=ostg[1:16, :]), R=['ostg'], W=['o_pp'], dma=True, reg='scr')

            def tp_pos(e):
                for c in range(8):
                    ins = e.transpose(out=ps[:64, 6 + c // 4, (c % 4) * 128:(c % 4 + 1) * 128], in_=uT[:, c, 1040:1104], identity=identf[:, :])
                return ins
            P.op('pe', tp_pos, R=[k for c in range(8) for k in UT(c)] + ['identf'], W=[('bank', 6), ('bank', 7)], reg='flexU')
            P.op('act', lambda e: e.copy(out=ostg[:64, :], in_=ps[:64, 6:8, :].rearrange("p a c -> p (a c)")),
                 R=[('bank', 6), ('bank', 7)], W=['ostg'], reg='scr')
            for t in range(4):
                P.op('sp', lambda e, t=t: e.dma_start(out=pool_s[:, 11 + t, :], in_=ostg[16 * t:16 * t + 16, :]), R=['ostg'], W=[('o_ps', t)], dma=True, reg='scr')

            L = 1040
            for c in range(8):
                gi = c // 2
                nlev = gi + 1
                w = 2 ** nlev
                ev = uT[:, c, 0:L]
                cur, curk = ev, None
                bufs = [(wsA, 'wsA'), (wsB, 'wsB')]
                for lev in range(nlev):
                    sh = 2 ** lev
                    lo = 2 * sh - 1
                    dst, dk = bufs[lev % 2]
                    pool_work.append((lambda cur=cur, dst=(dst if 'dst' in dir() else None), c=(c if 'c' in dir() else None), gi=(gi if 'gi' in dir() else None), w=(w if 'w' in dir() else None), sh=(sh if 'sh' in dir() else None), lo=(lo if 'lo' in dir() else None), curk=curk, dk=(dk if 'dk' in dir() else None), tmpb=(tmpb if 'tmpb' in dir() else None), tmpk=(tmpk if 'tmpk' in dir() else None), ci=(ci if 'ci' in dir() else None):
                        P.op('dve', lambda e, cur=cur, dst=dst, sh=sh, lo=lo: e.tensor_tensor(out=dst[:, lo:L], in0=cur[:, lo:L], in1=cur[:, lo - sh:L - sh], op=ALU.add),
                             R=(UT(c) if curk is None else [curk]), W=[dk], reg=['flex', 'flexU'])))
                    cur, curk = dst, dk
                pool_work.append((lambda cur=cur, dst=(dst if 'dst' in dir() else None), c=(c if 'c' in dir() else None), gi=(gi if 'gi' in dir() else None), w=(w if 'w' in dir() else None), sh=(sh if 'sh' in dir() else None), lo=(lo if 'lo' in dir() else None), curk=curk, dk=(dk if 'dk' in dir() else None), tmpb=(tmpb if 'tmpb' in dir() else None), tmpk=(tmpk if 'tmpk' in dir() else None), ci=(ci if 'ci' in dir() else None):
                    P.op('dve', lambda e, cur=cur, c=c, w=w: e.scalar_tensor_tensor(out=dT[:, c, 16:1024], in0=cur[:, 32:L], scalar=1.0 / w, in1=uT[:, c, 32:L],
                                                                                    op0=ALU.mult, op1=ALU.subtract),
                         R=[curk] + UT(c), W=[('dT', c, 1)], reg=['flex', 'flexU'])))
                tmpk = 'wsB' if curk == 'wsA' else 'wsA'
                tmpb = wsB if curk == 'wsA' else wsA
                pool_work.append((lambda cur=cur, dst=(dst if 'dst' in dir() else None), c=(c if 'c' in dir() else None), gi=(gi if 'gi' in dir() else None), w=(w if 'w' in dir() else None), sh=(sh if 'sh' in dir() else None), lo=(lo if 'lo' in dir() else None), curk=curk, dk=(dk if 'dk' in dir() else None), tmpb=(tmpb if 'tmpb' in dir() else None), tmpk=(tmpk if 'tmpk' in dir() else None), ci=(ci if 'ci' in dir() else None):
                    P.op('dve', lambda e, cur=cur, c=c, tmpb=tmpb: e.tensor_tensor(out=tmpb[:, 0:16], in0=cur[:, 16:32], in1=invc[:, c, :], op=ALU.mult),
                         R=[curk, 'cst'], W=[tmpk], reg=['flex', 'flexU'])))
                pool_work.append((lambda cur=cur, dst=(dst if 'dst' in dir() else None), c=(c if 'c' in dir() else None), gi=(gi if 'gi' in dir() else None), w=(w if 'w' in dir() else None), sh=(sh if 'sh' in dir() else None), lo=(lo if 'lo' in dir() else None), curk=curk, dk=(dk if 'dk' in dir() else None), tmpb=(tmpb if 'tmpb' in dir() else None), tmpk=(tmpk if 'tmpk' in dir() else None), ci=(ci if 'ci' in dir() else None):
                    P.op('dve', lambda e, c=c, tmpb=tmpb: e.tensor_tensor(out=dT[:, c, 0:16], in0=tmpb[:, 0:16], in1=uT[:, c, 16:32], op=ALU.subtract),
                         R=[tmpk] + UT(c), W=[('dT', c, 0)], reg=['flex', 'flexU'])))
            for gi in range(4):
                nlev = gi + 1
                w = 2 ** nlev
                ev = usT[:, 2 * gi:2 * gi + 2, :, :]
                vA = wsA[:, 0:608].rearrange("p (c s h) -> p c s h", c=2, s=16)
                vB = wsB[:, 0:608].rearrange("p (c s h) -> p c s h", c=2, s=16)
                bufs = [(vA, 'wsA'), (vB, 'wsB')]
                cur, curk = ev, None
                for lev in range(nlev):
                    sh = 2 ** lev
                    lo = 2 * sh - 1
                    dst, dk = bufs[lev % 2]
                    pool_work.append((lambda cur=cur, dst=(dst if 'dst' in dir() else None), c=(c if 'c' in dir() else None), gi=(gi if 'gi' in dir() else None), w=(w if 'w' in dir() else None), sh=(sh if 'sh' in dir() else None), lo=(lo if 'lo' in dir() else None), curk=curk, dk=(dk if 'dk' in dir() else None), tmpb=(tmpb if 'tmpb' in dir() else None), tmpk=(tmpk if 'tmpk' in dir() else None), ci=(ci if 'ci' in dir() else None):
                        P.op('dve', lambda e, cur=cur, dst=dst, sh=sh, lo=lo: e.tensor_tensor(out=dst[:, :, :, lo:19], in0=cur[:, :, :, lo:19], in1=cur[:, :, :, lo - sh:19 - sh], op=ALU.add),
                             R=(USK(2 * gi) + USK(2 * gi + 1) if curk is None else [curk]), W=[dk], reg=['flex', 'flexU'])))
                    cur, curk = dst, dk
                for ci in range(2):
                    pool_work.append((lambda cur=cur, dst=(dst if 'dst' in dir() else None), c=(c if 'c' in dir() else None), gi=(gi if 'gi' in dir() else None), w=(w if 'w' in dir() else None), sh=(sh if 'sh' in dir() else None), lo=(lo if 'lo' in dir() else None), curk=curk, dk=(dk if 'dk' in dir() else None), tmpb=(tmpb if 'tmpb' in dir() else None), tmpk=(tmpk if 'tmpk' in dir() else None), ci=(ci if 'ci' in dir() else None):
                        P.op('dve', lambda e, cur=cur, gi=gi, w=w, ci=ci: e.scalar_tensor_tensor(
                            out=dT[:, 2 * gi + ci, 1024:1088].rearrange("p (t s) -> p s t", t=4),
                            in0=cur[:, ci, :, 15:19], scalar=1.0 / w, in1=usT[:, 2 * gi + ci, :, 15:19], op0=ALU.mult, op1=ALU.subtract),
                            R=[curk] + USK(2 * gi + ci), W=[('dT', 2 * gi + ci, 2)], reg=['flex', 'flexU'])))
            def rope_ops(q, b, rows, nh):
                x = qf[q][:rows, 0:nh * 64].rearrange("p (h d) -> p h d", d=64)
                cc_ = rope[:rows, b, 0:16].unsqueeze(1).to_broadcast([rows, nh, 16])
                ss_ = rope[:rows, b, 16:32].unsqueeze(1).to_broadcast([rows, nh, 16])
                A = rtA[:rows, 0:nh, :]
                B = rtB[:rows, 0:nh, :]
                P.op('dve', lambda e: e.tensor_tensor(out=A, in0=x[:, :, 0:16], in1=cc_, op=ALU.mult), R=[('qf', q), 'cst'], W=['rtA'], reg='scr')
                P.op('dve', lambda e: e.tensor_tensor(out=B, in0=x[:, :, 0:16], in1=ss_, op=ALU.mult), R=[('qf', q), 'cst'], W=['rtB'], reg='scr')
                P.op('dve', lambda e: e.tensor_tensor(out=x[:, :, 0:8], in0=A[:, :, 0:8], in1=B[:, :, 8:16], op=ALU.subtract),
                     R=['rtA', 'rtB'], W=[('qf', q)], reg='scr')
                P.op('dve', lambda e: e.tensor_tensor(out=x[:, :, 8:16], in0=A[:, :, 8:16], in1=B[:, :, 0:8], op=ALU.add),
                     R=['rtA', 'rtB'], W=[('qf', q)], reg='scr')

            step = 0
            pend_b2 = []
            for pair in range(3):
                c0 = 2048 if pair == 0 else 1024 + 512 * (pair - 1)
                s0 = load_pair(w_in[:, c0:c0 + 512].rearrange("(k p) c -> p k c", p=128))
                for b in range(0 if pair == 0 else 1, 10):
                    rows = rows_of(b)
                    bank = step % 4
                    tb = 4 + step % 2
                    q = step % 2
                    step += 1

                    def mm(e, s0=s0, b=b, rows=rows, bank=bank):
                        for k in range(16):
                            ins = e.matmul(ps[:rows, bank, :], lhsT=acts[:, k, b * 128:b * 128 + rows],
                                           rhs=p16(s0)[:, k, :], start=(k == 0), stop=(k == 15))
                        return ins
                    P.op('pe', mm, R=[('ring', s0), ('ring', s0 + 1)] + [('acts', k, b) for k in range(16)], W=[('bank', bank)])
                    while pend_b2:
                        pend_b2.pop(0)()
                    if pair == 0:
                        P.op('act', lambda e, q=q, rows=rows, bank=bank: e.copy(out=qf[q][:rows, 0:256], in_=ps[:rows, bank, 0:256]),
                             R=[('bank', bank)], W=[('qf', q)], reg='scr')
                        P.op('act', lambda e, b=b, rows=rows, bank=bank: e.copy(out=Vb[:rows, b, :], in_=ps[:rows, bank, 256:512]),
                             R=[('bank', bank)], W=[('V', b)], reg='ringhi')
                        if b >= 8:
                            P.op('act', lambda e, b=b, rows=rows, bank=bank: e.copy(out=kvf[:rows, 2 + b - 8, :], in_=ps[:rows, bank, 256:512]),
                                 R=[('bank', bank)], W=[('vf', b)], reg='ringhi')
                        rope_ops(q, b, rows, 4)
                        for _ in range(3):
                            if pool_work:
                                pool_work.pop(0)()
                        if b >= 8:
                            P.op('act', lambda e, b=b, rows=rows, q=q: e.copy(out=kvf[:rows, b - 8, :], in_=qf[q][:rows, 0:256]),
                                 R=[('qf', q)], W=[('kf', b)], reg='ringhi')
                        P.op('act', lambda e, q=q, rows=rows: e.copy(out=qb[q][:rows, 0:256], in_=qf[q][:rows, 0:256]),
                             R=[('qf', q)], W=[('qb', q)], reg='scr')
                        nchunk = 2
                    else:
                        P.op('act', lambda e, q=q, rows=rows, bank=bank: e.copy(out=qf[q][:rows, :], in_=ps[:rows, bank, :]),
                             R=[('bank', bank)], W=[('qf', q)], reg='scr')
                        rope_ops(q, b, rows, 8)
                        for _ in range(3):
                            if pool_work:
                                pool_work.pop(0)()
                        P.op('act', lambda e, q=q, rows=rows: e.activation(out=qb[q][:rows, :], in_=qf[q][:rows, :], func=AF.Copy, scale=0.125),
                             R=[('qf', q)], W=[('qb', q)], reg='scr')
                        nchunk = 4

                    def finish(q=q, rows=rows, tb=tb, nchunk=nchunk, pair=pair, b=b):
                        def tp(e):
                            for j in range(nchunk):
                                ins = e.transpose(out=psb[:, tb, j * 128:j * 128 + rows], in_=qb[q][:rows, j * 128:(j + 1) * 128],
                                                  identity=identb[:rows, :rows])
                            return ins
                        P.op('pe', tp, R=[('qb', q), 'identb'], W=[('bank', tb)], reg='scr')
                        srcv = psb[:, tb, :].rearrange("p (j t) -> p j t", t=128)
                        if pair == 0:
                            P.op('dve', lambda e: e.tensor_copy(out=KT[:, :, b * 128:b * 128 + rows], in_=srcv[:, 0:2, :rows]),
                                 R=[('bank', tb)], W=[('KT', 0, b), ('KT', 1, b)], reg='ringhi')
                        else:
                            i0 = 4 * (pair - 1)
                            qc = (b - 1) * 128
                            P.op('dve', lambda e: e.tensor_copy(out=QT[:, i0:i0 + 4, qc:qc + rows], in_=srcv[:, 0:4, :rows]),
                                 R=[('bank', tb)], W=[('QT', i0 + ii, b) for ii in range(4)], reg='ringhi')
                    pend_b2.append(finish)

            while pend_b2:
                pend_b2.pop(0)()
            while pool_work:
                pool_work.pop(0)()
            chk('B2')
            P.op('sp', lambda e: e.dma_start(out=k_p[:, :], in_=kvf[:, 0, :]), R=[('kf', 8)], W=['o_kp'], dma=True, reg='ringhi')
            P.op('sp', lambda e: e.dma_start(out=v_p[:, :], in_=kvf[:, 2, :]), R=[('vf', 8)], W=['o_vp'], dma=True, reg='ringhi')
            for t in range(4):
                P.op('sp', lambda e, t=t: e.dma_start(out=k_s[:, 124 + t, :], in_=kvf[16 * t:16 * t + 16, 1, :]), R=[('kf', 9)], W=[('o_ks', t)], dma=True, reg='ringhi')
                P.op('sp', lambda e, t=t: e.dma_start(out=v_s[:, 124 + t, :], in_=kvf[16 * t:16 * t + 16, 3, :]), R=[('vf', 9)], W=[('o_vs', t)], dma=True, reg='ringhi')
            P.op('sp', lambda e: e.dma_start(out=k_s[:, 0:124, :], in_=ckd[:, 4:128, :]), W=['o_ks_c'], dma=True)
            P.op('sp', lambda e: e.dma_start(out=v_s[:, 0:124, :], in_=cvd[:, 4:128, :]), W=['o_vs_c'], dma=True)
            P.op('sp', lambda e: e.dma_start(out=pool_s[:, 0:11, :], in_=spd[:, 4:15, :]), W=['o_ps_c'], dma=True)

            s_wp = next_slot()
            wpv = ring[:, s_wp, 0:2048].rearrange("p (g d) -> p g d", g=8)
            P.op('pool', lambda e: e.dma_start(out=wpv, in_=w_pool.rearrange("g (kk p) d -> p (g kk) d", p=128)), W=[('ring', s_wp)], dma=True)
            TTP = [(0, 512, [1, 2, 3, 4]), (512, 512, [5, 6, 7, 8]), (1024, 64, [9])]
            pstep = 0
            for gi in range(4):
                for oc in range(2):
                    c = 2 * gi + oc
                    for (t0, n, blks) in TTP:
                        bank = pstep % 4
                        pstep += 1

                        def mm(e, gi=gi, oc=oc, t0=t0, n=n, bank=bank):
                            for kk in range(2):
                                ins = e.matmul(ps[:, bank, :n], lhsT=wpv[:, 2 * gi + kk, oc * 128:(oc + 1) * 128], rhs=dT[:, 2 * gi + kk, t0:t0 + n],
                                               start=(kk == 0), stop=(kk == 1))
                            return ins
                        P.op('pe', mm, R=[('ring', s_wp)] + [('dT', 2 * gi + kk, z) for kk in range(2) for z in range(3)], W=[('bank', bank)], reg='flex')
                        if pstep % 2 == 0:
                            P.op('act', lambda e, c=c, t0=t0, n=n, bank=bank: e.activation(out=acts[:, c, 128 + t0:128 + t0 + n], in_=ps[:, bank, :n], func=AF.Copy,
                                                                                          scale=pscT[:, c:c + 1]),
                                 R=[('bank', bank), 'cst'], W=[('acts', c, bb) for bb in blks])
                        else:
                            P.op('dve', lambda e, c=c, t0=t0, n=n, bank=bank: e.tensor_tensor(out=acts[:, c, 128 + t0:128 + t0 + n], in0=ps[:, bank, :n],
                                                                                             in1=pscT[:, c:c + 1].to_broadcast([128, n]), op=ALU.mult),
                                 R=[('bank', bank), 'cst'], W=[('acts', c, bb) for bb in blks])

            chk('POOL')

            CUR = {'sq': 0}

            def sm_of(par, rows):
                return small[:rows, CUR['sq'], :]

            def st_s1dve(par, rows, nk, maskv, hs, sbank):
                Sps = ps[:rows, sbank:sbank + 2, :].rearrange("p a (i k) -> p (a i) k", i=2)[:, :, :nk]
                Ss = S_sb[par][:rows, :, :nk]
                sm = sm_of(par, rows)
                nsinkh = nsinkv[:rows, hs]
                kS, km = ('Ssb', par), ('sm', CUR['sq'])
                P.op('dve', lambda e: e.tensor_tensor(out=Ss, in0=Sps, in1=maskv[:rows, :nk].unsqueeze(1).to_broadcast([rows, 4, nk]), op=ALU.add),
                     R=[('bank', sbank), ('bank', sbank + 1), 'cst'], W=[kS], reg='scr')
                P.op('dve', lambda e: e.tensor_reduce(out=sm[:, 0:4], in_=Ss, axis=AX.X, op=ALU.max), R=[kS], W=[(km, 0)])
                P.op('dve', lambda e: e.scalar_tensor_tensor(out=sm[:, 4:8], in0=sm[:, 0:4], scalar=-1.0, in1=nsinkh, op0=ALU.mult, op1=ALU.min),
                     R=[(km, 0), 'cst'], W=[(km, 1)])

            def st_s2a(par, rows, nk, hs, sbank):
                Ss = S_sb[par][:rows, :, :nk]
                Pv = Pb[par][:rows, :, :nk]
                sm = sm_of(par, rows)
                sinkh = sinkv[:rows, hs]
                kP, km = ('Pb', par), ('sm', CUR['sq'])

                def ex(e):
                    for ii in range(4):
                        ins = e.activation(out=Pv[:, ii, :], in_=Ss[:, ii, :], func=AF.Exp, bias=sm[:, 4 + ii:5 + ii], scale=1.0, accum_out=sm[:, 8 + ii:9 + ii])
                    return ins
                P.op('act', ex, R=[('Ssb', par), (km, 1)], W=[kP, (km, 2)], reg='scr')
                P.op('pool', lambda e: e.tensor_tensor(out=sm[:, 12:16], in0=sinkh, in1=sm[:, 4:8], op=ALU.add), R=[(km, 1), 'cst'], W=[(km, 3)])
                P.op('act', lambda e: e.activation(out=sm[:, 16:20], in_=sm[:, 12:16], func=AF.Exp), R=[(km, 3)], W=[(km, 4)])

            def st_s2b(par, rows, nk):
                Pv = Pb[par][:rows, :, :nk]
                sm = sm_of(par, rows)
                kP, km = ('Pb', par), ('sm', CUR['sq'])
                P.op('pool', lambda e: e.tensor_tensor(out=sm[:, 20:24], in0=sm[:, 8:12], in1=sm[:, 16:20], op=ALU.add), R=[(km, 2), (km, 4)], W=[(km, 5)])
                P.op('dve', lambda e: e.reciprocal(out=sm[:, 24:28], in_=sm[:, 20:24]), R=[(km, 5)], W=[(km, 6)])

            def make_prompt_item(idx, b, kc, j):
                par = idx % 2
                sbank, ptb = 2 * par, 4 + par
                bp = b % 2
                qc = (b - 1) * 128
                maskv = mask_f if b == 1 else mask_n
                kvh = 2 * kc + j
                hs = slice(8 * kc + j, 8 * kc + 8, 2)
                it = {}

                def s1pe():
                    def qk(e):
                        for ii in range(4):
                            i = 4 * kc + ii
                            ins = e.matmul(ps[:, sbank + ii // 2, (ii % 2) * 256:(ii % 2) * 256 + 256],
                                     lhsT=QT[64 * j:64 * j + 64, i, qc:qc + 128], rhs=KT[64 * j:64 * j + 64, kc, (b - 1) * 128:(b + 1) * 128],
                                     start=True, stop=True)
                        return ins
                    P.op('pe', qk, R=[('QT', 4 * kc + ii, b) for ii in range(4)] + [('KT', kc, b - 1), ('KT', kc, b)], W=[('bank', sbank), ('bank', sbank + 1)], reg='ringhi')
                it['s1pe'] = s1pe
                it['s1dve'] = lambda: st_s1dve(par, 128, 256, maskv, hs, sbank)
                it['s2a'] = lambda: st_s2a(par, 128, 256, hs, sbank)
                it['s2b'] = lambda: st_s2b(par, 128, 256)

                def s3():
                    def tpp(e):
                        for ii in range(4):
                            for kb in range(2):
                                ins = e.transpose(out=psb[:, ptb, (2 * ii + kb) * 128:(2 * ii + kb + 1) * 128], in_=Pb[par][:, ii, kb * 128:(kb + 1) * 128], identity=identb[:, :])
                        return ins
                    P.op('pe', tpp, R=[('Pb', par), 'identb'], W=[('bank', ptb)], reg='scr')
                    P.op('act', lambda e: e.copy(out=PTs[par][:, :], in_=psb[:, ptb, :]), R=[('bank', ptb)], W=[('PTs', par)], reg='scr')
                it['s3'] = s3

                def s4():
                    def pv(e):
                        for ii in range(4):
                            for kb in range(2):
                                ins = e.matmul(ps[:, 6, par * 256 + ii * 64:par * 256 + (ii + 1) * 64],
                                               lhsT=PTs[par][:, (2 * ii + kb) * 128:(2 * ii + kb + 1) * 128], rhs=Vb[:, b - 1 + kb, kvh * 64:(kvh + 1) * 64],
                                               start=(kb == 0), stop=(kb == 1))
                        return ins
                    P.op('pe', pv, R=[('PTs', par), ('V', b - 1), ('V', b)], W=[('bank', 6)], reg=['scr', 'ringhi'])
                    sq_ = idx % 4
                    P.op('dve', lambda e: e.tensor_tensor(
                        out=atm[bp][:, :].rearrange("p (i jj d) -> p i jj d", i=8, jj=2)[:, 4 * kc:4 * kc + 4, j, :],
                        in0=ps[:, 6, par * 256:(par + 1) * 256].rearrange("p (i d) -> p i d", i=4),
                        in1=small[:, sq_, 24:28].unsqueeze(2).to_broadcast([128, 4, 64]), op=ALU.mult),
                        R=[('bank', 6), (('sm', sq_), 6)], W=[('atm', bp, kc, j)], reg='scr')
                    if kc == 1 and j == 1:
                        def tpa(e):
                            for i in range(8):
                                ins = e.transpose(out=psb[:, 7, i * 128:(i + 1) * 128], in_=atm[bp][:, i * 128:(i + 1) * 128], identity=identb[:, :])
                            return ins
                        P.op('pe', tpa, R=[('atm', bp, kk, jj) for kk in range(2) for jj in range(2)] + ['identb'], W=[('bank', 7)], reg='scr')
                        P.op('dve', lambda e: e.tensor_copy(out=acts[:, 8:16, b * 128:(b + 1) * 128], in_=psb[:, 7, :].rearrange("p (i t) -> p i t", i=8)),
                             R=[('bank', 7)], W=[('acts', 8 + i, b) for i in range(8)])
                it['s4'] = s4
                return it

            def make_sample_item(idx, kc, j):
                par = idx % 2
                sbank, ptb = 2 * par, 4 + par
                bp = 1
                kvh = 2 * kc + j
                hs = slice(8 * kc + j, 8 * kc + 8, 2)
                it = {}

                def s1pe():
                    def qk(e):
                        for ii in range(4):
                            i = 4 * kc + ii
                            o0 = (ii % 2) * 256
                            for s in range(16):
                                e.matmul(ps[:64, sbank + ii // 2, o0:o0 + 128], lhsT=QTm[64 * j:64 * j + 64, i, s, :], rhs=KcT[64 * j:64 * j + 64, kc, s, :],
                                         start=(s == 0), stop=(s == 15))
                            ins = e.matmul(ps[:64, sbank + ii // 2, o0 + 128:o0 + 192], lhsT=QT[64 * j:64 * j + 64, i, 1024:1088], rhs=KT[64 * j:64 * j + 64, kc, 1152:1216],
                                           start=True, stop=True)
                        return ins
                    P.op('pe', qk, R=[('QTm', 4 * kc + ii) for ii in range(4)] + [('QT', 4 * kc + ii, 9) for ii in range(4)] + [('KcT', kc, 0), ('KcT', kc, 1), ('KT', kc, 9)],
                         W=[('bank', sbank), ('bank', sbank + 1)], reg=['flex', 'ringhi'])
                it['s1pe'] = s1pe
                it['s1dve'] = lambda: st_s1dve(par, 64, 192, mask_s, hs, sbank)
                it['s2a'] = lambda: st_s2a(par, 64, 192, hs, sbank)
                it['s2b'] = lambda: st_s2b(par, 64, 192)

                def s3():
                    def tpp(e):
                        for ii in range(4):
                            e.transpose(out=psb[:, ptb, ii * 64:(ii + 1) * 64], in_=Pb[par][:64, ii, 0:128], identity=identb[:64, :64])
                            ins = e.transpose(out=psb[:64, ptb, 256 + ii * 64:256 + (ii + 1) * 64], in_=Pb[par][:64, ii, 128:192], identity=identb[:64, :64])
                        return ins
                    P.op('pe', tpp, R=[('Pb', par), 'identb'], W=[('bank', ptb)], reg='scr')
                    P.op('act', lambda e: e.copy(out=PTs[par][:, 0:512], in_=psb[:, ptb, 0:512]), R=[('bank', ptb)], W=[('PTs', par)], reg='scr')
                    P.op('dve', lambda e: e.tensor_tensor(
                        out=PTm[par][:, :, :, :], in0=PTs[par][:, 0:256].rearrange("p (i r) -> p i r", i=4).unsqueeze(2).to_broadcast([128, 4, 16, 64]),
                        in1=bm[:, :, :].unsqueeze(1).to_broadcast([128, 4, 16, 64]), op=ALU.mult),
                        R=[('PTs', par), 'bm'], W=[('PTm', par)], reg=['flex', 'scr'])
                it['s3'] = s3

                def s4():
                    def pv(e):
                        for ii in range(4):
                            oo = par * 256 + ii * 64
                            for s in range(16):
                                e.matmul(ps[:64, 6, oo:oo + 64], lhsT=PTm[par][:, ii, s, :], rhs=Vc[:, s, kvh * 64:(kvh + 1) * 64], start=(s == 0), stop=False)
                            ins = e.matmul(ps[:64, 6, oo:oo + 64], lhsT=PTs[par][:64, 256 + ii * 64:256 + (ii + 1) * 64], rhs=Vb[:64, 9, kvh * 64:(kvh + 1) * 64],
                                           start=False, stop=True)
                        return ins
                    P.op('pe', pv, R=[('PTm', par), ('PTs', par), 'Vc', ('V', 9)], W=[('bank', 6)], reg=['flex', 'scr', 'ringhi'])
                    sq_ = idx % 4
                    P.op('dve', lambda e: e.tensor_tensor(
                        out=atm_s[:64, :].rearrange("p (i jj d) -> p i jj d", i=8, jj=2)[:, 4 * kc:4 * kc + 4, j, :],
                        in0=ps[:64, 6, par * 256:(par + 1) * 256].rearrange("p (i d) -> p i d", i=4),
                        in1=small[:64, sq_, 24:28].unsqueeze(2).to_broadcast([64, 4, 64]), op=ALU.mult),
                        R=[('bank', 6), (('sm', sq_), 6)], W=[('atm', 's', kc, j)], reg='scr', extra=P.fence('scr'))
                    if kc == 1 and j == 1:
                        def tpa9(e):
                            for i in range(8):
                                ins = e.transpose(out=psb[:, 7, i * 128:i * 128 + 64], in_=atm_s[:64, i * 128:(i + 1) * 128], identity=identb[:64, :64])
                            return ins
                        P.op('pe', tpa9, R=[('atm', 's', kk, jj) for kk in range(2) for jj in range(2)] + ['identb'], W=[('bank', 7)], reg='scr')
                        P.op('dve', lambda e: e.tensor_copy(out=acts[:, 8:16, 1152:1216], in_=psb[:, 7, :].rearrange("p (i t) -> p i t", i=8)[:, :, 0:64]),
                             R=[('bank', 7)], W=[('acts', 8 + i, 9) for i in range(8)], reg='flex')
                it['s4'] = s4
                return it

            fl = P.fence('flex')
            flU = P.fence('flexU')
            P.op('pool', lambda e: e.dma_start(out=ckb[:, :, :], in_=ckd.rearrange("s j c -> j s c")), W=['ckb'], dma=True, extra=flU, reg='flex')
            P.op('pool', lambda e: e.dma_start(out=Vc[:, :, :], in_=cvd.rearrange("s j c -> j s c")), W=['Vc'], dma=True, extra=flU, reg='flex')
            for kc in range(2):
                for hf in range(2):
                    bank = 2 * kc + hf

                    def tpk(e, kc=kc, hf=hf, bank=bank):
                        for sl in range(8):
                            s = 8 * hf + sl
                            ins = e.transpose(out=psb[:, bank, sl * 128:(sl + 1) * 128], in_=ckb[:, s, kc * 128:(kc + 1) * 128], identity=identb[:, :])
                        return ins
                    P.op('pe', tpk, R=['ckb', 'identb'], W=[('bank', bank)], reg='flex')
                    P.op('dve', lambda e, kc=kc, hf=hf, bank=bank: e.tensor_copy(out=KcT[:, kc, 8 * hf:8 * hf + 8, :], in_=psb[:, bank, :].rearrange("p (s j) -> p s j", s=8)),
                         R=[('bank', bank)], W=[('KcT', kc, hf)], extra=flU, reg='flex')
            for i in range(8):
                P.op('dve', lambda e, i=i: e.tensor_tensor(out=QTm[:, i, :, :], in0=QT[:, i, 1024:1088].unsqueeze(1).to_broadcast([128, 16, 64]), in1=bm[:, :, :], op=ALU.mult),
                     R=[('QT', i, 9), 'bm'], W=[('QTm', i)], extra=fl, reg=['flex', 'ringhi'])

            rs['mod'] = 8
            rs['n'] = 0
            wout_s = [load_pair(w_out[:, 512 * m:512 * m + 512].rearrange("(k p) c -> p k c", p=128)) for m in range(2)]
            items = []
            order = [('p', b, kc, j) for b in range(1, 9) for kc in range(2) for j in range(2)]
            samp = [('s', 0, kc, j) for kc in range(2) for j in range(2)]
            for pos_, sd in zip((1, 6, 11, 16), samp):
                order.insert(pos_, sd)
            for (kind_, b, kc, j) in order:
                if kind_ == 's':
                    items.append(make_sample_item(len(items), kc, j))
                else:
                    items.append(make_prompt_item(len(items), b, kc, j))
            NI = len(items)
            for t in range(NI + 4):
                def g(d):
                    n = t - d
                    if 0 <= n < NI:
                        return {k: (lambda f=f, n=n: (CUR.__setitem__('sq', n % 4), f())) for k, f in items[n].items()}
                    return None
                if g(0): g(0)['s1pe']()
                if g(4): g(4)['s4']()
                if g(3): g(3)['s3']()
                if g(2): g(2)['s2b']()
                if g(1): g(1)['s2a']()
                if g(0): g(0)['s1dve']()
                if t == 8:
                    chk('SATT')
            chk('ATT')
            flx = P.fence('flex')
            pend_nt = []
            scf = P.fence('scr')
            for b in range(1, 10):
                rows = rows_of(b)
                src = xs[:, :] if b == 9 else xp[(b - 1) * 128:b * 128, :]
                P.op('sp', lambda e, src=src, b=b, rows=rows: e.dma_start(out=xres[:rows, b - 1, :], in_=src), W=[('xres', b, m) for m in range(4)], dma=True, extra=flx)
            ostep = 0
            for m in range(4):
                s0 = wout_s[m] if m < 2 else load_pair(w_out[:, 512 * m:512 * m + 512].rearrange("(k p) c -> p k c", p=128), extra=P.fence('ringhi'))
                for b in range(1, 10):
                    rows = rows_of(b)
                    bank = 4 + ostep % 4
                    ostep += 1

                    def mm(e, s0=s0, b=b, rows=rows, bank=bank):
                        for k in range(16):
                            ins = e.matmul(ps[:rows, bank, :], lhsT=acts[:, k, b * 128:b * 128 + rows],
                                           rhs=p16(s0)[:, k, :], start=(k == 0), stop=(k == 15))
                        return ins
                    P.op('pe', mm, R=[('ring', s0), ('ring', s0 + 1)] + [('acts', k, b) for k in range(16)], W=[('bank', bank)])
                    P.op('dve', lambda e, b=b, m=m, rows=rows, bank=bank: e.tensor_tensor(out=xres[:rows, b - 1, 512 * m:512 * (m + 1)], in0=ps[:rows, bank, :],
                                                                                          in1=xres[:rows, b - 1, 512 * m:512 * (m + 1)], op=ALU.add),
                         R=[('bank', bank), ('xres', b, m)], W=[('xres', b, m)])
                    if m == 3 and b >= 2:
                        bb = b - 1
                        while pend_nt:
                            pend_nt.pop(0)()
                        norm_T(bb, xres[:rows_of(bb), bb - 1, :], [('xres', bb, mm_) for mm_ in range(4)], gffnT, 30 + bb, extra=scf, defer=pend_nt)
            while pend_nt:
                pend_nt.pop(0)()
            norm_T(9, xres[:64, 8, :], [('xres', 9, mm_) for mm_ in range(4)], gffnT, 39, extra=scf)
            chk('D')
            chk('D2')
            chk('D2')
            TT2 = [(128, 384, [1, 2, 3]), (512, 384, [4, 5, 6]), (896, 320, [7, 8, 9])]
            H2 = [('acts', k, b) for k in range(16) for b in range(1, 10)]

            def d2(s):
                return ring[:, s, :].rearrange("p (kk n) -> p kk n", kk=2)
            dstep = 0
            sg_i = 0
            pend_fin = []
            for g in range(11):
                gp = g % 2
                gs, us_, ds = [], [], []
                for u in range(2):
                    c0 = 512 * g + 256 * u
                    gs.append(load_unit(u16, w_gate[:, c0:c0 + 256].rearrange("(k p) c -> p k c", p=128)))
                    us_.append(load_unit(u16, w_up[:, c0:c0 + 256].rearrange("(k p) c -> p k c", p=128)))
                for c in range(4):
                    u, cc = c // 2, c % 2
                    for tt, (t0, n, blks) in enumerate(TT2):
                        for which, sl, bk in (('g', gs[u], tt), ('u', us_[u], 3 + tt)):
                            def mm(e, sl=sl, cc=cc, t0=t0, n=n, bk=bk):
                                for k in range(16):
                                    ins = e.matmul(ps[:, bk, :n], lhsT=u16(sl)[:, k, cc * 128:(cc + 1) * 128], rhs=acts[:, k, t0:t0 + n], start=(k == 0), stop=(k == 15))
                                return ins
                            last_gu = P.op('pe', mm, R=[('ring', sl)] + [('acts', k, bb) for k in range(16) for bb in blks], W=[('bank', bk)])
                        sq = sg_i % 3
                        sg_i += 1
                        P.op('act', lambda e, sq=sq, tt=tt, n=n: e.activation(out=sgt[sq][:, :n], in_=ps[:, tt, :n], func=AF.Silu), R=[('bank', tt)], W=[('sgt', sq)], reg='scr')
                        P.op('dve', lambda e, sq=sq, tt=tt, n=n, t0=t0, c=c, gp=gp: e.tensor_tensor(out=actT[gp][:, c, t0 - 128:t0 - 128 + n], in0=sgt[sq][:, :n], in1=ps[:, 3 + tt, :n], op=ALU.mult),
                             R=[('sgt', sq), ('bank', 3 + tt)], W=[('actT', gp, c, tt)], reg='scr')
                for u in range(2):
                    r0 = 512 * g + 256 * u
                    ds.append(load_unit(d2, w_down[r0:r0 + 256, :].rearrange("(kk p) n -> p kk n", p=128)))
                if g == 9:
                    ds9, gp9 = list(ds), gp
                    continue
                chunks = [(gp, c, ds[c // 2], c % 2) for c in range(4)]
                if g == 10:
                    chunks = [(gp9, c, ds9[c // 2], c % 2) for c in range(4)] + chunks
                    accf = acts[:].rearrange("p k t -> p (k t)").bitcast(F32)
                    gfin = accf[:, 0:2048]
                    ystg = [accf[:, 2048:4096], accf[:, 4096:6144]]
                    P.op('sp', lambda e: e.dma_start(out=gfin, in_=gfin_d[:, :]), W=['gfinA'], dma=True, extra=[last_gu])
                for b in range(1, 10):
                    rows = rows_of(b)
                    tcol = (b - 1) * 128
                    ttb = min((b - 1) // 3, 2)
                    for m in range(4):
                        bank = (dstep % 8) if g == 10 else 6 + dstep % 2
                        dstep += 1

                        def mm(e, chunks=tuple(chunks), tcol=tcol, rows=rows, m=m, bank=bank):
                            for i_, (gpp, c, slot, kk) in enumerate(chunks):
                                ins = e.matmul(ps[:rows, bank, :], lhsT=actT[gpp][:, c, tcol:tcol + rows], rhs=d2(slot)[:, kk, 512 * m:512 * (m + 1)],
                                               start=(i_ == 0), stop=(i_ == len(chunks) - 1))
                            return ins
                        P.op('pe', mm, R=sorted(set(('ring', sl_) for (_, _, sl_, _) in chunks)) + [('actT', gpp, c, ttb) for (gpp, c, _, _) in chunks], W=[('bank', bank)], reg='scr')
                        P.op('dve', lambda e, b=b, m=m, rows=rows, bank=bank: e.tensor_tensor(out=xres[:rows, b - 1, 512 * m:512 * (m + 1)], in0=ps[:rows, bank, :],
                                                                                              in1=xres[:rows, b - 1, 512 * m:512 * (m + 1)], op=ALU.add),
                             R=[('bank', bank), ('xres', b, m)], W=[('xres', b, m)])
                    if g == 10:
                        while pend_fin:
                            pend_fin.pop(0)()
                    if g == 10:
                      def fin(b=b, rows=rows):
                        sc = 60 + b
                        q = b % 2
                        xk = [('xres', b, m) for m in range(4)]
                        yk = ('ystgA', q)
                        P.op('act', lambda e, b=b, rows=rows, sc=sc, q=q: e.activation(out=ystg[q][:rows, :], in_=xres[:rows, b - 1, :], func=AF.Square, accum_out=stat[:rows, sc:sc + 1]),
                             R=xk, W=[yk, ('st', sc)], extra=[last_gu])
                        P.op('act', lambda e, rows=rows, sc=sc: e.activation(out=stat[:rows, sc + 10:sc + 11], in_=stat[:rows, sc:sc + 1], func=AF.Sqrt, bias=epst[:rows], scale=1.0 / D),
                             R=[('st', sc), 'eps'], W=[('st', sc + 10)])
                        P.op('dve', lambda e, rows=rows, sc=sc: e.reciprocal(out=stat[:rows, sc + 20:sc + 21], in_=stat[:rows, sc + 10:sc + 11]), R=[('st', sc + 10)], W=[('st', sc + 20)])
                        P.op('dve', lambda e, b=b, rows=rows, sc=sc, q=q: e.scalar_tensor_tensor(out=ystg[q][:rows, :], in0=xres[:rows, b - 1, :], scalar=stat[:rows, sc + 20:sc + 21],
                                                                                                 in1=gfin[:rows, :], op0=ALU.mult, op1=ALU.mult),
                             R=xk + [('st', sc + 20), 'gfinA'], W=[yk])
                        dst = ys[:, :] if b == 9 else yp[(b - 1) * 128:b * 128, :]
                        P.op('sp', lambda e, dst=dst, rows=rows, q=q: e.dma_start(out=dst, in_=ystg[q][:rows, :]), R=[yk], W=[('o_y', b)], dma=True)
                      pend_fin.append(fin)
            while pend_fin:
                pend_fin.pop(0)()

            chk('FFN')
            allout = ['o_kp', 'o_vp', 'o_ks_c', 'o_vs_c', 'o_ps_c', 'o_pp'] + [('o_ks', t) for t in range(4)] + [('o_vs', t) for t in range(4)] + \
                     [('o_ps', t) for t in range(4)] + [('o_y', b) for b in range(1, 10)]
            P.op('sp', None, R=allout)

        except _Stop:
            P.op('sp', None, R=list(P.lw.keys()))
        P.resolve()
        P.emit(block, sems, dsems)
    return nc


_CACHE = {}


def _rope_tab(pos):
    inv = (np.float32(500000.0) ** (-np.arange(0, 16, 2, dtype=np.float32) / np.float32(16))).astype(np.float32)
    ang = pos.astype(np.float32)[:, None] * inv[None, :]
    cos = np.cos(ang).astype(np.float32)
    sin = np.sin(ang).astype(np.float32)
    return np.concatenate([cos, cos, sin, sin], axis=1)


def _prepare(x_prompt, x_sample, state_pool, cache_k_win, cache_v_win, g_mix, w_in, w_pool,
             pool_scale, attn_sinks, w_out, g_ffn, w_gate, w_up, w_down, g_final):
    f = lambda a: np.ascontiguousarray(np.asarray(a, dtype=np.float32))
    x_prompt, x_sample, state_pool = f(x_prompt), f(x_sample), f(state_pool)
    ck_all, cv_all = f(cache_k_win)[0].reshape(128, 128, 256), f(cache_v_win)[0].reshape(128, 128, 256)
    g_mix, g_ffn, g_final = f(g_mix)[0], f(g_ffn)[0], f(g_final)
    w_in0, w_pool0, w_out0 = f(w_in)[0], f(w_pool)[0], f(w_out)[0]
    w_gate0, w_up0, w_down0 = f(w_gate)[0], f(w_up)[0], f(w_down)[0]
    pscale, sinks = f(pool_scale)[0], f(attn_sinks)[0]
    qcols = np.concatenate([1024 + 64 * h + np.arange(64) for h in PERM])
    w_in_p = np.ascontiguousarray(np.concatenate([w_in0[:, :1024], w_in0[:, qcols], w_in0[:, 2048:]], axis=1))
    w_out_p = np.ascontiguousarray(np.concatenate([w_out0[:1024], w_out0[qcols]], axis=0))
    sinks_p = sinks[PERM]

    qi = np.arange(128)[:, None]
    kj = np.arange(256)[None, :]
    diff = qi + 128 - kj
    mn = np.where((diff >= 0) & (diff <= 128), 0.0, NEG).astype(np.float32)
    mf = mn.copy()
    mf[:, :128] = NEG
    r = np.arange(64)
    tr, sr = r // 16, r % 16
    ms = np.full((128, 256), NEG, np.float32)
    jj = np.arange(128)
    ms[:64, :128] = np.where(jj[None, :] >= tr[:, None], 0.0, NEG)
    n = np.arange(64)
    tn, sn = n // 16, n % 16
    ms[:64, 128:192] = np.where((sn[None, :] == sr[:, None]) & (tn[None, :] <= tr[:, None]), 0.0, NEG)
    bmf = np.zeros((16, 64), np.float32)
    bmf[sr, r] = 1.0
    bm = np.broadcast_to(bmf.reshape(1, 1024), (128, 1024)).astype(ml_dtypes.bfloat16)
    gfin_b = np.ascontiguousarray(np.broadcast_to(g_final[None, :], (128, D)))

    in_maps = []
    for c in range(NCORE):
        seq, ch = c // 4, c % 4
        t0 = ch * 1024
        xpc = x_prompt[seq, t0:t0 + 1024]
        xhc = x_prompt[seq, t0 - 128:t0] if ch > 0 else np.zeros((128, D), np.float32)
        xsc = np.ascontiguousarray(x_sample[16 * c:16 * c + 16].transpose(1, 0, 2).reshape(64, D))
        cst = np.zeros((128, CW), np.float32)
        cst[:, C_GMIX:C_GMIX + 16] = g_mix.reshape(16, 128).T
        cst[:, C_GFFN:C_GFFN + 16] = g_ffn.reshape(16, 128).T
        cst[:, C_PSC:C_PSC + 8] = pscale.reshape(8, 128).T
        cst[:, C_SINK:C_SINK + 16] = sinks_p[None, :]
        cst[:, C_NSINK:C_NSINK + 16] = np.negative(sinks_p)[None, :]
        cst[:, C_MN:C_MN + 256] = mn
        cst[:, C_MF:C_MF + 256] = mn if ch > 0 else mf
        cst[:, C_MS:C_MS + 256] = ms
        pos = np.zeros((10, 128), np.int64)
        pos[:9] = (t0 - 128 + np.arange(1152)).reshape(9, 128)
        pos[9, :64] = PAST + tr
        rt = _rope_tab(pos.reshape(-1)).reshape(10, 128, 32).transpose(1, 0, 2)
        cst[:, C_ROPE:C_ROPE + 320] = rt.reshape(128, 320)
        iv = np.zeros((8, 16), np.float32)
        for cch in range(8):
            w = 2 ** (cch // 2 + 1)
            p = np.arange(16) + (t0 if ch > 0 else 0)
            iv[cch] = 1.0 / np.minimum(p + 1, w).astype(np.float32)
        cst[:, C_INVC:C_INVC + 128] = iv.reshape(1, 128)
        in_maps.append({
            "xp": np.ascontiguousarray(xpc), "xh": np.ascontiguousarray(xhc), "xs": xsc,
            "sp": np.ascontiguousarray(state_pool[0, 16 * c:16 * c + 16]),
            "ck": np.ascontiguousarray(ck_all[16 * c:16 * c + 16]), "cv": np.ascontiguousarray(cv_all[16 * c:16 * c + 16]),
            "w_in": w_in_p, "w_pool": w_pool0, "w_out": w_out_p, "w_gate": w_gate0, "w_up": w_up0, "w_down": w_down0,
            "cst": cst, "gfin": gfin_b, "bm": bm,
        })
    return in_maps


def kernel(**inputs):
    in_maps = _prepare(**inputs)
    if 'nc' not in _CACHE:
        _CACHE['nc'] = build_program()
    nc = _CACHE['nc']
    res = run_bass_kernel_spmd(nc, in_maps, core_ids=list(range(NCORE)))
    return _assemble(res.results)


def _assemble(R):
    y_prompt = np.stack([np.concatenate([R[4 * s + ch]["yp"] for ch in range(4)], axis=0) for s in range(2)], axis=0)
    y_sample = np.concatenate([R[c]["ys"].reshape(4, 16, D).transpose(1, 0, 2) for c in range(NCORE)], axis=0)
    new_pool_p = np.stack([R[3]["pool_p"], R[7]["pool_p"]], axis=0)[None]
    new_k_p = np.stack([R[3]["k_p"], R[7]["k_p"]], axis=0).reshape(1, 2, 128, 4, 64)
    new_v_p = np.stack([R[3]["v_p"], R[7]["v_p"]], axis=0).reshape(1, 2, 128, 4, 64)
    new_pool_s = np.concatenate([R[c]["pool_s"] for c in range(NCORE)], axis=0)[None]
    new_k_s = np.concatenate([R[c]["k_s"] for c in range(NCORE)], axis=0).reshape(1, 128, 128, 4, 64)
    new_v_s = np.concatenate([R[c]["v_s"] for c in range(NCORE)], axis=0).reshape(1, 128, 128, 4, 64)
    asf = lambda a: np.ascontiguousarray(a, dtype=np.float32)
    return (asf(y_prompt), asf(y_sample), asf(new_pool_p), asf(new_k_p), asf(new_v_p), asf(new_pool_s), asf(new_k_s), asf(new_v_s))
```

```python
import contextlib
import numpy as np
import ml_dtypes
import concourse.bass as bass
import concourse.mybir as mybir
from concourse.bass_utils import run_bass_kernel_spmd

F32 = mybir.dt.float32
BF16 = mybir.dt.bfloat16
AF = mybir.ActivationFunctionType
ALU = mybir.AluOpType
AX = mybir.AxisListType

D = 2048
DFF = 5632
NCORE = 8
PAST = 16384
EPS = 1e-5
NEG = -30000.0
PERM = [0, 4, 1, 5, 2, 6, 3, 7, 8, 12, 9, 13, 10, 14, 11, 15]
ENGS = ['pe', 'act', 'dve', 'pool', 'sp']
NORM_ENG = 'pool'
NDSEM = {'sp': 12, 'pool': 6, 'act': 4, 'pe': 1, 'dve': 1}

C_GMIX, C_GFFN, C_PSC, C_SINK, C_NSINK = 0, 16, 32, 40, 56
C_MN, C_MF, C_MS, C_ROPE, C_INVC = 72, 328, 584, 840, 1160
CW = 1288


class Op:
    pass


class Planner:
    def __init__(self):
        self.ops = []
        self.lw = {}
        self.rd = {}
        self.cnt = {e: 0 for e in ENGS}
        self.rr = {e: 0 for e in ENGS}
        self.dlast = {}
        self.dcnt = {}
        self.reg_last = {}

    def op(self, eng, fn, R=(), W=(), dma=False, extra=(), reg=None):
        o = Op()
        o.eng, o.fn, o.dma = eng, fn, dma
        deps = {}
        for k in R:
            w = self.lw.get(k)
            if w is not None:
                deps[w] = True
        for k in W:
            w = self.lw.get(k)
            if w is not None:
                deps.setdefault(w, False)
            for r in self.rd.get(k, ()):
                deps.setdefault(r, False)
        for e in extra:
            if e is not None:
                deps[e] = True
        if dma:
            si = self.rr[eng]
            self.rr[eng] = (si + 1) % NDSEM[eng]
            prev = self.dlast.get((eng, si))
            if prev is not None:
                deps[prev] = True
            self.dlast[(eng, si)] = o
            n = self.dcnt.get((eng, si), 0) + 1
            self.dcnt[(eng, si)] = n
            o.dsem, o.dval = (eng, si), 16 * n
        o.deps = deps
        for k in W:
            self.lw[k] = o
            self.rd[k] = []
        for k in R:
            self.rd.setdefault(k, []).append(o)
        o.pos = self.cnt[eng]
        self.cnt[eng] += 1
        o.mark = False
        o.ms = 0
        self.ops.append(o)
        if reg is not None:
            for r in (reg if isinstance(reg, (list, tuple)) else [reg]):
                if dma:
                    self.reg_last.setdefault(r, {}).setdefault('dmas', []).append(o)
                else:
                    self.reg_last.setdefault(r, {})[eng] = o
        return o

    def fence(self, *regs):
        out = []
        for r in regs:
            for k, v in self.reg_last.get(r, {}).items():
                out += v if k == 'dmas' else [v]
        return out

    def all_last(self):
        last = {}
        dmas = []
        for o in self.ops:
            if o.dma:
                dmas.append(o)
            else:
                last[o.eng] = o
        return list(last.values()) + dmas

    def resolve(self):
        waited = {e: {} for e in ENGS}
        for o in self.ops:
            o.waits = []
            best = {}
            for a, raw in o.deps.items():
                if a is o:
                    continue
                if a.dma:
                    key = ('d',) + a.dsem
                    if waited[o.eng].get(key, 0) < a.dval:
                        waited[o.eng][key] = a.dval
                        o.waits.append(a)
                    continue
                if a.eng == o.eng and not o.dma:
                    if o.eng == 'pe':
                        continue
                if a.eng not in best or best[a.eng].pos < a.pos:
                    best[a.eng] = a
            for e, a in best.items():
                if waited[o.eng].get(e, -1) < a.pos:
                    waited[o.eng][e] = a.pos
                    a.mark = True
                    o.waits.append(a)
        c = {e: 0 for e in ENGS}
        for o in self.ops:
            if o.mark:
                c[o.eng] += 1
                o.ms = c[o.eng]

    def emit(self, block, sems, dsems):
        deco = {'pe': block.tensor, 'act': block.scalar, 'dve': block.vector,
                'pool': block.gpsimd, 'sp': block.sync}
        for eng in ENGS:
            ops = [o for o in self.ops if o.eng == eng]

            def body(e, ops=ops):
                for o in ops:
                    for a in o.waits:
                        if a.dma:
                            e.wait_ge(dsems[a.dsem], a.dval)
                        else:
                            e.wait_ge(sems[a.eng], a.ms)
                    if o.fn is None:
                        continue
                    ins = o.fn(e)
                    if o.dma:
                        ins.then_inc(dsems[o.dsem], 16)
                    elif o.mark:
                        ins.then_inc(sems[o.eng], 1)
            deco[eng](body)


class _Stop(Exception):
    pass


def build_program(stop_at=None):
    nc = bass.Bass("TRN2", target_bir_lowering=False)

    def din(name, shape, dt=F32):
        return nc.dram_tensor(name, list(shape), dt, kind="ExternalInput").ap()

    def dout(name, shape):
        return nc.dram_tensor(name, list(shape), F32, kind="ExternalOutput").ap()

    xp = din("xp", [1024, D]); xh = din("xh", [128, D]); xs = din("xs", [64, D])
    spd = din("sp", [16, 15, 1024]); ckd = din("ck", [16, 128, 256]); cvd = din("cv", [16, 128, 256])
    w_in = din("w_in", [D, 2560]); w_pool = din("w_pool", [4, 256, 256]); w_out = din("w_out", [D, D])
    w_gate = din("w_gate", [D, DFF]); w_up = din("w_up", [D, DFF]); w_down = din("w_down", [DFF, D])
    cst_d = din("cst", [128, CW]); gfin_d = din("gfin", [128, D]); bm_d = din("bm", [128, 1024], BF16)
    yp = dout("yp", [1024, D]); ys = dout("ys", [64, D])
    pool_p = dout("pool_p", [15, 1024]); k_p = dout("k_p", [128, 256]); v_p = dout("v_p", [128, 256])
    pool_s = dout("pool_s", [16, 15, 1024]); k_s = dout("k_s", [16, 128, 256]); v_s = dout("v_s", [16, 128, 256])

    P = Planner()
    stack = contextlib.ExitStack()
    with stack:
        def sb(name, shape, dt):
            return stack.enter_context(nc.sbuf_tensor(name, shape, dt))
        acts = sb("acts", [128, 16, 1216], BF16)
        ring = sb("ring", [128, 8, 4096], BF16)
        flex = sb("flex", [128, 18432], F32)
        scr = sb("scr", [128, 6144], F32)
        cst = sb("cst_sb", [128, CW], F32)
        bm = sb("bm_sb", [128, 16, 64], BF16)
        identf = sb("identf", [128, 128], F32)
        identb = sb("identb", [128, 128], BF16)
        stat = sb("stat", [128, 128], F32)
        small = sb("small", [128, 4, 64], F32)
        ps = stack.enter_context(nc.psum_tensor("ps", [128, 8, 512], F32))
        sems = {e: stack.enter_context(nc.semaphore("sem_" + e)) for e in ENGS}
        dsems = {}
        for e in ('sp', 'pool', 'act'):
            for i in range(NDSEM[e]):
                dsems[(e, i)] = stack.enter_context(nc.semaphore("dsem_%s_%d" % (e, i)))
        block = stack.enter_context(nc.Block())

        psb = ps[:].bitcast(BF16)
        ringf = ring[:].bitcast(F32)
        flexb = flex[:].bitcast(BF16)
        scrb = scr[:].bitcast(BF16)

        gmixT = cst[:, C_GMIX:C_GMIX + 16]; gffnT = cst[:, C_GFFN:C_GFFN + 16]
        pscT = cst[:, C_PSC:C_PSC + 8]
        sinkv = cst[:, C_SINK:C_SINK + 16]; nsinkv = cst[:, C_NSINK:C_NSINK + 16]
        mask_n = cst[:, C_MN:C_MN + 256]; mask_f = cst[:, C_MF:C_MF + 256]; mask_s = cst[:, C_MS:C_MS + 256]
        rope = cst[:, C_ROPE:C_ROPE + 320].rearrange("p (b c) -> p b c", b=10)
        invc = cst[:, C_INVC:C_INVC + 128].rearrange("p (c t) -> p c t", c=8)
        epst = stat[:, 127:128]
        ringhi = ring[:, 4:8, :].rearrange("p s e -> p (s e)")
        QT = ringhi[:, 0:8704].rearrange("p (i t) -> p i t", i=8)
        KT = ringhi[:, 8704:11136].rearrange("p (i t) -> p i t", i=2)
        Vb = ringhi[:, 11136:13696].rearrange("p (b c) -> p b c", b=10)
        kvf = ringf[:, 7, 800:1824].rearrange("p (a c) -> p a c", a=4)
        uT = flex[:, 0:8832].rearrange("p (c t) -> p c t", c=8)
        dT = flexb[:, 17664:26368].rearrange("p (c t) -> p c t", c=8)
        wsA = flex[:, 13184:14224]; wsB = flex[:, 14224:15264]
        usT = flex[:, 15264:17696].rearrange("p (c s h) -> p c s h", c=8, s=16)
        xres = flex[:, :].rearrange("p (b c) -> p b c", b=9)
        ckb = flexb[:, 0:4096].rearrange("p (s c) -> p s c", s=16)
        Vc = flexb[:, 4096:8192].rearrange("p (s c) -> p s c", s=16)
        KcT = flexb[:, 8192:12288].rearrange("p (k s j) -> p k s j", k=2, s=16)
        QTm = flexb[:, 12288:20480].rearrange("p (i s r) -> p i s r", i=8, s=16)
        PTm = [flexb[:, 20480 + 4096 * q:20480 + 4096 * (q + 1)].rearrange("p (i s r) -> p i s r", i=4, s=16) for q in range(2)]
        xa = [scr[:, 0:2048], scr[:, 2048:4096], flex[:, 8832:10880], flex[:, 10880:12928]]
        xnb = [scrb[:, 8192:10240], scrb[:, 10240:12288]]
        S_sb = [scr[:, 1024 * q:1024 * (q + 1)].rearrange("p (i k) -> p i k", i=4) for q in range(2)]
        Pb = [scrb[:, 4096 + 1024 * q:4096 + 1024 * (q + 1)].rearrange("p (i k) -> p i k", i=4) for q in range(2)]
        PTs = [scrb[:, 6144 + 1024 * q:6144 + 1024 * (q + 1)] for q in range(2)]
        atm = [scrb[:, 8192 + 1024 * q:8192 + 1024 * (q + 1)] for q in range(2)]
        ostg = scr[:, 5120:6144]
        atm_s = scrb[:, 10240:11264]
        qf = [scr[:, 2048 + 512 * q:2048 + 512 * (q + 1)] for q in range(2)]
        qb = [scrb[:, 6144 + 512 * q:6144 + 512 * (q + 1)] for q in range(2)]
        rtA = scr[:, 3584:3840].rearrange("p (h c) -> p h c", c=16)[:, 0:8, :]
        rtB = scr[:, 3840:4096].rearrange("p (h c) -> p h c", c=16)[:, 0:8, :]
        actT = [scrb[:, 4352 * q:4352 * (q + 1)].rearrange("p (c t) -> p c t", c=4) for q in range(2)]
        sgt = [scrb[:, 8704 + 512 * q:8704 + 512 * (q + 1)] for q in range(3)]

        def rows_of(b):
            return 64 if b == 9 else 128

        def chk(name):
            if stop_at == name:
                raise _Stop()

        try:
            P.op('sp', lambda e: e.dma_start(out=cst[:], in_=cst_d[:, :]), W=['cst'], dma=True)
            P.op('sp', lambda e: e.dma_start(out=bm[:].rearrange("p s r -> p (s r)"), in_=bm_d[:, :]), W=['bm'], dma=True)
            P.op('pool', lambda e: e.memset(identf[:], 0.0), W=['identf'])
            P.op('pool', lambda e: e.affine_select(out=identf[:], in_=identf[:], pattern=[[-1, 128]],
                                                   compare_op=ALU.not_equal, fill=1.0, base=0, channel_multiplier=1),
                 R=['identf'], W=['identf'])
            P.op('dve', lambda e: e.tensor_copy(out=identb[:], in_=identf[:]), R=['identf'], W=['identb'])
            P.op('dve', lambda e: e.memset(epst, EPS), W=['eps'])

            rs = {'n': 0, 'mod': 4}

            def next_slot():
                s = rs['n'] % rs['mod']
                rs['n'] += 1
                return s

            def load_unit(view_fn, src, eng='pool', extra=()):
                s = next_slot()
                o = P.op(eng, lambda e, s=s: e.dma_start(out=view_fn(s), in_=src), W=[('ring', s)], dma=True, extra=extra)
                return s

            def u16(s):
                return ring[:, s, :].rearrange("p (k c) -> p k c", k=16)

            def p16(s):
                return ring[:, s:s + 2, :].rearrange("p s e -> p (s e)").rearrange("p (k c) -> p k c", k=16)

            def load_pair(src, extra=()):
                if rs['n'] % 2:
                    rs['n'] += 1
                s0 = next_slot()
                s1 = next_slot()
                assert s1 == s0 + 1 and s0 % 2 == 0
                P.op('pool', lambda e: e.dma_start(out=p16(s0), in_=src), W=[('ring', s0), ('ring', s1)], dma=True, extra=extra)
                return s0

            def norm_T(b, src, srckeys, gT, sc, reg=None, extra=(), defer=None):
                rows = rows_of(b)
                q = b % 2
                c0 = b * 128
                P.op('act', lambda e: e.activation(out=xnb[q][:rows], in_=src, func=AF.Square, accum_out=stat[:rows, sc:sc + 1]),
                     R=srckeys, W=[('xnb', q), ('st', sc)], reg=reg, extra=extra)
                P.op('act', lambda e: e.activation(out=stat[:rows, sc + 10:sc + 11], in_=stat[:rows, sc:sc + 1], func=AF.Sqrt,
                                                   bias=epst[:rows], scale=1.0 / D),
                     R=[('st', sc), 'eps'], W=[('st', sc + 10)])
                P.op('dve', lambda e: e.reciprocal(out=stat[:rows, sc + 20:sc + 21], in_=stat[:rows, sc + 10:sc + 11]),
                     R=[('st', sc + 10)], W=[('st', sc + 20)])
                P.op('dve', lambda e: e.tensor_scalar(out=xnb[q][:rows], in0=src, scalar1=stat[:rows, sc + 20:sc + 21], scalar2=None, op0=ALU.mult),
                     R=srckeys + [('st', sc + 20)], W=[('xnb', q)], reg=reg)
                def back():
                  for h in range(2):
                    bank = 2 * q + h

                    def tp(e, h=h, bank=bank):
                        for j in range(8):
                            k = 8 * h + j
                            ins = e.transpose(out=psb[:, bank, j * 128:j * 128 + rows], in_=xnb[q][:rows, k * 128:(k + 1) * 128],
                                              identity=identb[:rows, :rows])
                        return ins
                    P.op('pe', tp, R=[('xnb', q), 'identb'], W=[('bank', bank)], reg=reg)
                    P.op('dve', lambda e, h=h, bank=bank: e.tensor_tensor(
                        out=acts[:, 8 * h:8 * h + 8, c0:c0 + rows],
                        in0=psb[:, bank, :].rearrange("p (j t) -> p j t", j=8)[:, :, :rows],
                        in1=gT[:, 8 * h:8 * h + 8].unsqueeze(2).to_broadcast([128, 8, rows]), op=ALU.mult),
                        R=[('bank', bank), 'cst'], W=[('acts', k, b) for k in range(8 * h, 8 * h + 8)], reg=reg)
                if defer is None:
                    back()
                else:
                    defer.append(back)

            xa_ops = {}
            for b in range(10):
                rows = rows_of(b)
                src = xh[:, :] if b == 0 else (xs[:, :] if b == 9 else xp[(b - 1) * 128:b * 128, :])
                q = b % 2
                q4 = b % 4
                xa_ops[b] = P.op('sp', lambda e, src=src, q4=q4, rows=rows: e.dma_start(out=xa[q4][:rows], in_=src), W=[('xa', q4)], dma=True, reg=['scr', 'flex', 'flexU'])
                norm_T(b, xa[q4][:rows], [('xa', q4)], gmixT, b, reg=['scr', 'flex', 'flexU'])

            ALLACT = [('acts', k, b) for k in range(16) for b in range(10)]
            chk('A')

            TT1 = [(112, 368), (480, 368), (848, 368)]
            for j in range(4):
                s = 4 + j
                P.op('pool', lambda e, s=s, j=j: e.dma_start(out=u16(s), in_=w_in[:, 256 * j:256 * j + 256].rearrange("(k p) c -> p k c", p=128)), W=[('ring', s)], dma=True, extra=([xa_ops[8]] if j == 0 else ()))
                for cc in range(2):
                    c = 2 * j + cc
                    b0 = 3 * (c % 2)
                    for tt, (t0, n) in enumerate(TT1):
                        def mm(e, s=s, cc=cc, tt=tt, t0=t0, n=n, b0=b0):
                            for k in range(16):
                                ins = e.matmul(ps[:, b0 + tt, :n], lhsT=u16(s)[:, k, cc * 128:(cc + 1) * 128], rhs=acts[:, k, t0:t0 + n],
                                               start=(k == 0), stop=(k == 15))
                            return ins
                        P.op('pe', mm, R=[('ring', s)] + [('acts', k, bb) for k in range(16) for bb in range(t0 // 128, (t0 + n - 1) // 128 + 1)], W=[('bank', b0 + tt)])
                        eng = 'act' if tt % 2 == 0 else 'dve'
                        if eng == 'act':
                            P.op('act', lambda e, c=c, tt=tt, t0=t0, n=n, b0=b0: e.copy(out=uT[:, c, t0 - 112:t0 - 112 + n], in_=ps[:, b0 + tt, :n]),
                                 R=[('bank', b0 + tt)], W=[('uT', c, tt)], reg=['flex', 'flexU'])
                        else:
                            P.op('dve', lambda e, c=c, tt=tt, t0=t0, n=n, b0=b0: e.tensor_copy(out=uT[:, c, t0 - 112:t0 - 112 + n], in_=ps[:, b0 + tt, :n]),
                                 R=[('bank', b0 + tt)], W=[('uT', c, tt)], reg=['flex', 'flexU'])
            UT = lambda c: [('uT', c, 0), ('uT', c, 1), ('uT', c, 2)]
            chk('B1')

            pool_work = []
            chk('B2o')
            s_sph = next_slot()
            sph = ringf[:, s_sph, :].rearrange("p (a c) -> p a c", a=2)
            P.op('sp', lambda e: e.dma_start(out=sph[:120], in_=spd.rearrange("(a s) h c -> (s h) a c", a=2)), W=[('ring', s_sph)], dma=True)
            for a in range(2):
                for g in range(2):
                    bank = 2 * a + g

                    def tp(e, a=a, g=g, bank=bank):
                        for cq in range(4):
                            c = 4 * g + cq
                            ins = e.transpose(out=ps[:, bank, cq * 120:(cq + 1) * 120], in_=sph[:120, a, c * 128:(c + 1) * 128], identity=identf[:120, :120])
                        return ins
                    P.op('pe', tp, R=[('ring', s_sph), 'identf'], W=[('bank', bank)])
                    P.op('act', lambda e, a=a, g=g, bank=bank: e.copy(
                        out=usT[:, 4 * g:4 * g + 4, 8 * a:8 * a + 8, 0:15],
                        in_=ps[:, bank, 0:480].rearrange("p (c s h) -> p c s h", c=4, s=8)),
                        R=[('bank', bank)], W=[('usT', 4 * g + cq, a) for cq in range(4)], reg='flex')
            P.op('dve', lambda e: e.tensor_copy(out=usT[:, :, :, 15:19], in_=uT[:, :, 1040:1104].rearrange("p c (t s) -> p c s t", t=4)),
                 R=[k for c in range(8) for k in UT(c)], W=[('usT', c, 2) for c in range(8)], reg=['flex', 'flexU'])
            USK = lambda c: [('usT', c, 0), ('usT', c, 1), ('usT', c, 2)]

            def tp_po(e):
                for c in range(8):
                    ins = e.transpose(out=ps[:16, 4 + c // 4, (c % 4) * 128:(c % 4 + 1) * 128], in_=uT[:, c, 1024:1040], identity=identf[:, :])
                return ins
            P.op('pe', tp_po, R=[k for c in range(8) for k in UT(c)] + ['identf'], W=[('bank', 4), ('bank', 5)], reg='flexU')
            P.op('act', lambda e: e.copy(out=ostg[:16, :], in_=ps[:16, 4:6, :].rearrange("p a c -> p (a c)")),
                 R=[('bank', 4), ('bank', 5)], W=['ostg'], reg='scr')
            P.op('sp', lambda e: e.dma_start(out=pool_p[:, :], in_=ostg[1:16, :]), R=['ostg'], W=['o_pp'], dma=True, reg='scr')

            def tp_pos(e):
                for c in range(8):
                    ins = e.transpose(out=ps[:64, 6 + c // 4, (c % 4) * 128:(c % 4 + 1) * 128], in_=uT[:, c, 1040:1104], identity=identf[:, :])
                return ins
            P.op('pe', tp_pos, R=[k for c in range(8) for k in UT(c)] + ['identf'], W=[('bank', 6), ('bank', 7)], reg='flexU')
            P.op('act', lambda e: e.copy(out=ostg[:64, :], in_=ps[:64, 6:8, :].rearrange("p a c -> p (a c)")),
                 R=[('bank', 6), ('bank', 7)], W=['ostg'], reg='scr')
            for t in range(4):
                P.op('sp', lambda e, t=t: e.dma_start(out=pool_s[:, 11 + t, :], in_=ostg[16 * t:16 * t + 16, :]), R=['ostg'], W=[('o_ps', t)], dma=True, reg='scr')

            L = 1040
            for c in range(8):
                gi = c // 2
                nlev = gi + 1
                w = 2 ** nlev
                ev = uT[:, c, 0:L]
                cur, curk = ev, None
                bufs = [(wsA, 'wsA'), (wsB, 'wsB')]
                for lev in range(nlev):
                    sh = 2 ** lev
                    lo = 2 * sh - 1
                    dst, dk = bufs[lev % 2]
                    pool_work.append((lambda cur=cur, dst=(dst if 'dst' in dir() else None), c=(c if 'c' in dir() else None), gi=(gi if 'gi' in dir() else None), w=(w if 'w' in dir() else None), sh=(sh if 'sh' in dir() else None), lo=(lo if 'lo' in dir() else None), curk=curk, dk=(dk if 'dk' in dir() else None), tmpb=(tmpb if 'tmpb' in dir() else None), tmpk=(tmpk if 'tmpk' in dir() else None), ci=(ci if 'ci' in dir() else None):
                        P.op('dve', lambda e, cur=cur, dst=dst, sh=sh, lo=lo: e.tensor_tensor(out=dst[:, lo:L], in0=cur[:, lo:L], in1=cur[:, lo - sh:L - sh], op=ALU.add),
                             R=(UT(c) if curk is None else [curk]), W=[dk], reg=['flex', 'flexU'])))
                    cur, curk = dst, dk
                pool_work.append((lambda cur=cur, dst=(dst if 'dst' in dir() else None), c=(c if 'c' in dir() else None), gi=(gi if 'gi' in dir() else None), w=(w if 'w' in dir() else None), sh=(sh if 'sh' in dir() else None), lo=(lo if 'lo' in dir() else None), curk=curk, dk=(dk if 'dk' in dir() else None), tmpb=(tmpb if 'tmpb' in dir() else None), tmpk=(tmpk if 'tmpk' in dir() else None), ci=(ci if 'ci' in dir() else None):
                    P.op('dve', lambda e, cur=cur, c=c, w=w: e.scalar_tensor_tensor(out=dT[:, c, 16:1024], in0=cur[:, 32:L], scalar=1.0 / w, in1=uT[:, c, 32:L],
                                                                                    op0=ALU.mult, op1=ALU.subtract),
                         R=[curk] + UT(c), W=[('dT', c, 1)], reg=['flex', 'flexU'])))
                tmpk = 'wsB' if curk == 'wsA' else 'wsA'
                tmpb = wsB if curk == 'wsA' else wsA
                pool_work.append((lambda cur=cur, dst=(dst if 'dst' in dir() else None), c=(c if 'c' in dir() else None), gi=(gi if 'gi' in dir() else None), w=(w if 'w' in dir() else None), sh=(sh if 'sh' in dir() else None), lo=(lo if 'lo' in dir() else None), curk=curk, dk=(dk if 'dk' in dir() else None), tmpb=(tmpb if 'tmpb' in dir() else None), tmpk=(tmpk if 'tmpk' in dir() else None), ci=(ci if 'ci' in dir() else None):
                    P.op('dve', lambda e, cur=cur, c=c, tmpb=tmpb: e.tensor_tensor(out=tmpb[:, 0:16], in0=cur[:, 16:32], in1=invc[:, c, :], op=ALU.mult),
                         R=[curk, 'cst'], W=[tmpk], reg=['flex', 'flexU'])))
                pool_work.append((lambda cur=cur, dst=(dst if 'dst' in dir() else None), c=(c if 'c' in dir() else None), gi=(gi if 'gi' in dir() else None), w=(w if 'w' in dir() else None), sh=(sh if 'sh' in dir() else None), lo=(lo if 'lo' in dir() else None), curk=curk, dk=(dk if 'dk' in dir() else None), tmpb=(tmpb if 'tmpb' in dir() else None), tmpk=(tmpk if 'tmpk' in dir() else None), ci=(ci if 'ci' in dir() else None):
                    P.op('dve', lambda e, c=c, tmpb=tmpb: e.tensor_tensor(out=dT[:, c, 0:16], in0=tmpb[:, 0:16], in1=uT[:, c, 16:32], op=ALU.subtract),
                         R=[tmpk] + UT(c), W=[('dT', c, 0)], reg=['flex', 'flexU'])))
            for gi in range(4):
                nlev = gi + 1
                w = 2 ** nlev
                ev = usT[:, 2 * gi:2 * gi + 2, :, :]
                vA = wsA[:, 0:608].rearrange("p (c s h) -> p c s h", c=2, s=16)
                vB = wsB[:, 0:608].rearrange("p (c s h) -> p c s h", c=2, s=16)
                bufs = [(vA, 'wsA'), (vB, 'wsB')]
                cur, curk = ev, None
                for lev in range(nlev):
                    sh = 2 ** lev
                    lo = 2 * sh - 1
                    dst, dk = bufs[lev % 2]
                    pool_work.append((lambda cur=cur, dst=(dst if 'dst' in dir() else None), c=(c if 'c' in dir() else None), gi=(gi if 'gi' in dir() else None), w=(w if 'w' in dir() else None), sh=(sh if 'sh' in dir() else None), lo=(lo if 'lo' in dir() else None), curk=curk, dk=(dk if 'dk' in dir() else None), tmpb=(tmpb if 'tmpb' in dir() else None), tmpk=(tmpk if 'tmpk' in dir() else None), ci=(ci if 'ci' in dir() else None):
                        P.op('dve', lambda e, cur=cur, dst=dst, sh=sh, lo=lo: e.tensor_tensor(out=dst[:, :, :, lo:19], in0=cur[:, :, :, lo:19], in1=cur[:, :, :, lo - sh:19 - sh], op=ALU.add),
                             R=(USK(2 * gi) + USK(2 * gi + 1) if curk is None else [curk]), W=[dk], reg=['flex', 'flexU'])))
                    cur, curk = dst, dk
                for ci in range(2):
                    pool_work.append((lambda cur=cur, dst=(dst if 'dst' in dir() else None), c=(c if 'c' in dir() else None), gi=(gi if 'gi' in dir() else None), w=(w if 'w' in dir() else None), sh=(sh if 'sh' in dir() else None), lo=(lo if 'lo' in dir() else None), curk=curk, dk=(dk if 'dk' in dir() else None), tmpb=(tmpb if 'tmpb' in dir() else None), tmpk=(tmpk if 'tmpk' in dir() else None), ci=(ci if 'ci' in dir() else None):
                        P.op('dve', lambda e, cur=cur, gi=gi, w=w, ci=ci: e.scalar_tensor_tensor(
                            out=dT[:, 2 * gi + ci, 1024:1088].rearrange("p (t s) -> p s t", t=4),
                            in0=cur[:, ci, :, 15:19], scalar=1.0 / w, in1=usT[:, 2 * gi + ci, :, 15:19], op0=ALU.mult, op1=ALU.subtract),
                            R=[curk] + USK(2 * gi + ci), W=[('dT', 2 * gi + ci, 2)], reg=['flex', 'flexU'])))
            def rope_ops(q, b, rows, nh):
                x = qf[q][:rows, 0:nh * 64].rearrange("p (h d) -> p h d", d=64)
                cc_ = rope[:rows, b, 0:16].unsqueeze(1).to_broadcast([rows, nh, 16])
                ss_ = rope[:rows, b, 16:32].unsqueeze(1).to_broadcast([rows, nh, 16])
                A = rtA[:rows, 0:nh, :]
                B = rtB[:rows, 0:nh, :]
                P.op('dve', lambda e: e.tensor_tensor(out=A, in0=x[:, :, 0:16], in1=cc_, op=ALU.mult), R=[('qf', q), 'cst'], W=['rtA'], reg='scr')
                P.op('dve', lambda e: e.tensor_tensor(out=B, in0=x[:, :, 0:16], in1=ss_, op=ALU.mult), R=[('qf', q), 'cst'], W=['rtB'], reg='scr')
                P.op('dve', lambda e: e.tensor_tensor(out=x[:, :, 0:8], in0=A[:, :, 0:8], in1=B[:, :, 8:16], op=ALU.subtract),
                     R=['rtA', 'rtB'], W=[('qf', q)], reg='scr')
                P.op('dve', lambda e: e.tensor_tensor(out=x[:, :, 8:16], in0=A[:, :, 8:16], in1=B[:, :, 0:8], op=ALU.add),
                     R=['rtA', 'rtB'], W=[('qf', q)], reg='scr')

            step = 0
            pend_b2 = []
            for pair in range(3):
                c0 = 2048 if pair == 0 else 1024 + 512 * (pair - 1)
                s0 = load_pair(w_in[:, c0:c0 + 512].rearrange("(k p) c -> p k c", p=128))
                for b in range(0 if pair == 0 else 1, 10):
                    rows = rows_of(b)
                    bank = step % 4
                    tb = 4 + step % 2
                    q = step % 2
                    step += 1

                    def mm(e, s0=s0, b=b, rows=rows, bank=bank):
                        for k in range(16):
                            ins = e.matmul(ps[:rows, bank, :], lhsT=acts[:, k, b * 128:b * 128 + rows],
                                           rhs=p16(s0)[:, k, :], start=(k == 0), stop=(k == 15))
                        return ins
                    P.op('pe', mm, R=[('ring', s0), ('ring', s0 + 1)] + [('acts', k, b) for k in range(16)], W=[('bank', bank)])
                    while pend_b2:
                        pend_b2.pop(0)()
                    if pair == 0:
                        P.op('act', lambda e, q=q, rows=rows, bank=bank: e.copy(out=qf[q][:rows, 0:256], in_=ps[:rows, bank, 0:256]),
                             R=[('bank', bank)], W=[('qf', q)], reg='scr')
                        P.op('act', lambda e, b=b, rows=rows, bank=bank: e.copy(out=Vb[:rows, b, :], in_=ps[:rows, bank, 256:512]),
                             R=[('bank', bank)], W=[('V', b)], reg='ringhi')
                        if b >= 8:
                            P.op('act', lambda e, b=b, rows=rows, bank=bank: e.copy(out=kvf[:rows, 2 + b - 8, :], in_=ps[:rows, bank, 256:512]),
                                 R=[('bank', bank)], W=[('vf', b)], reg='ringhi')
                        rope_ops(q, b, rows, 4)
                        for _ in range(3):
                            if pool_work:
                                pool_work.pop(0)()
                        if b >= 8:
                            P.op('act', lambda e, b=b, rows=rows, q=q: e.copy(out=kvf[:rows, b - 8, :], in_=qf[q][:rows, 0:256]),
                                 R=[('qf', q)], W=[('kf', b)], reg='ringhi')
                        P.op('act', lambda e, q=q, rows=rows: e.copy(out=qb[q][:rows, 0:256], in_=qf[q][:rows, 0:256]),
                             R=[('qf', q)], W=[('qb', q)], reg='scr')
                        nchunk = 2
                    else:
                        P.op('act', lambda e, q=q, rows=rows, bank=bank: e.copy(out=qf[q][:rows, :], in_=ps[:rows, bank, :]),
                             R=[('bank', bank)], W=[('qf', q)], reg='scr')
                        rope_ops(q, b, rows, 8)
                        for _ in range(3):
                            if pool_work:
                                pool_work.pop(0)()
                        P.op('act', lambda e, q=q, rows=rows: e.activation(out=qb[q][:rows, :], in_=qf[q][:rows, :], func=AF.Copy, scale=0.125),
                             R=[('qf', q)], W=[('qb', q)], reg='scr')
                        nchunk = 4

                    def finish(q=q, rows=rows, tb=tb, nchunk=nchunk, pair=pair, b=b):
                        def tp(e):
                            for j in range(nchunk):
                                ins = e.transpose(out=psb[:, tb, j * 128:j * 128 + rows], in_=qb[q][:rows, j * 128:(j + 1) * 128],
                                                  identity=identb[:rows, :rows])
                            return ins
                        P.op('pe', tp, R=[('qb', q), 'identb'], W=[('bank', tb)], reg='scr')
                        srcv = psb[:, tb, :].rearrange("p (j t) -> p j t", t=128)
                        if pair == 0:
                            P.op('dve', lambda e: e.tensor_copy(out=KT[:, :, b * 128:b * 128 + rows], in_=srcv[:, 0:2, :rows]),
                                 R=[('bank', tb)], W=[('KT', 0, b), ('KT', 1, b)], reg='ringhi')
                        else:
                            i0 = 4 * (pair - 1)
                            qc = (b - 1) * 128
                            P.op('dve', lambda e: e.tensor_copy(out=QT[:, i0:i0 + 4, qc:qc + rows], in_=srcv[:, 0:4, :rows]),
                                 R=[('bank', tb)], W=[('QT', i0 + ii, b) for ii in range(4)], reg='ringhi')
                    pend_b2.append(finish)

            while pend_b2:
                pend_b2.pop(0)()
            while pool_work:
                pool_work.pop(0)()
            chk('B2')
            P.op('sp', lambda e: e.dma_start(out=k_p[:, :], in_=kvf[:, 0, :]), R=[('kf', 8)], W=['o_kp'], dma=True, reg='ringhi')
            P.op('sp', lambda e: e.dma_start(out=v_p[:, :], in_=kvf[:, 2, :]), R=[('vf', 8)], W=['o_vp'], dma=True, reg='ringhi')
            for t in range(4):
                P.op('sp', lambda e, t=t: e.dma_start(out=k_s[:, 124 + t, :], in_=kvf[16 * t:16 * t + 16, 1, :]), R=[('kf', 9)], W=[('o_ks', t)], dma=True, reg='ringhi')
                P.op('sp', lambda e, t=t: e.dma_start(out=v_s[:, 124 + t, :], in_=kvf[16 * t:16 * t + 16, 3, :]), R=[('vf', 9)], W=[('o_vs', t)], dma=True, reg='ringhi')
            P.op('sp', lambda e: e.dma_start(out=k_s[:, 0:124, :], in_=ckd[:, 4:128, :]), W=['o_ks_c'], dma=True)
            P.op('sp', lambda e: e.dma_start(out=v_s[:, 0:124, :], in_=cvd[:, 4:128, :]), W=['o_vs_c'], dma=True)
            P.op('sp', lambda e: e.dma_start(out=pool_s[:, 0:11, :], in_=spd[:, 4:15, :]), W=['o_ps_c'], dma=True)

            s_wp = next_slot()
            wpv = ring[:, s_wp, 0:2048].rearrange("p (g d) -> p g d", g=8)
            P.op('pool', lambda e: e.dma_start(out=wpv, in_=w_pool.rearrange("g (kk p) d -> p (g kk) d", p=128)), W=[('ring', s_wp)], dma=True)
            TTP = [(0, 512, [1, 2, 3, 4]), (512, 512, [5, 6, 7, 8]), (1024, 64, [9])]
            pstep = 0
            for gi in range(4):
                for oc in range(2):
                    c = 2 * gi + oc
                    for (t0, n, blks) in TTP:
                        bank = pstep % 4
                        pstep += 1

                        def mm(e, gi=gi, oc=oc, t0=t0, n=n, bank=bank):
                            for kk in range(2):
                                ins = e.matmul(ps[:, bank, :n], lhsT=wpv[:, 2 * gi + kk, oc * 128:(oc + 1) * 128], rhs=dT[:, 2 * gi + kk, t0:t0 + n],
                                               start=(kk == 0), stop=(kk == 1))
                            return ins
                        P.op('pe', mm, R=[('ring', s_wp)] + [('dT', 2 * gi + kk, z) for kk in range(2) for z in range(3)], W=[('bank', bank)], reg='flex')
                        if pstep % 2 == 0:
                            P.op('act', lambda e, c=c, t0=t0, n=n, bank=bank: e.activation(out=acts[:, c, 128 + t0:128 + t0 + n], in_=ps[:, bank, :n], func=AF.Copy,
                                                                                          scale=pscT[:, c:c + 1]),
                                 R=[('bank', bank), 'cst'], W=[('acts', c, bb) for bb in blks])
                        else:
                            P.op('dve', lambda e, c=c, t0=t0, n=n, bank=bank: e.tensor_tensor(out=acts[:, c, 128 + t0:128 + t0 + n], in0=ps[:, bank, :n],
                                                                                             in1=pscT[:, c:c + 1].to_broadcast([128, n]), op=ALU.mult),
                                 R=[('bank', bank), 'cst'], W=[('acts', c, bb) for bb in blks])

            chk('POOL')

            CUR = {'sq': 0}

            def sm_of(par, rows):
                return small[:rows, CUR['sq'], :]

            def st_s1dve(par, rows, nk, maskv, hs, sbank):
                Sps = ps[:rows, sbank:sbank + 2, :].rearrange("p a (i k) -> p (a i) k", i=2)[:, :, :nk]
                Ss = S_sb[par][:rows, :, :nk]
                sm = sm_of(par, rows)
                nsinkh = nsinkv[:rows, hs]
                kS, km = ('Ssb', par), ('sm', CUR['sq'])
                P.op('dve', lambda e: e.tensor_tensor(out=Ss, in0=Sps, in1=maskv[:rows, :nk].unsqueeze(1).to_broadcast([rows, 4, nk]), op=ALU.add),
                     R=[('bank', sbank), ('bank', sbank + 1), 'cst'], W=[kS], reg='scr')
                P.op('dve', lambda e: e.tensor_reduce(out=sm[:, 0:4], in_=Ss, axis=AX.X, op=ALU.max), R=[kS], W=[(km, 0)])
                P.op('dve', lambda e: e.scalar_tensor_tensor(out=sm[:, 4:8], in0=sm[:, 0:4], scalar=-1.0, in1=nsinkh, op0=ALU.mult, op1=ALU.min),
                     R=[(km, 0), 'cst'], W=[(km, 1)])

            def st_s2a(par, rows, nk, hs, sbank):
                Ss = S_sb[par][:rows, :, :nk]
                Pv = Pb[par][:rows, :, :nk]
                sm = sm_of(par, rows)
                sinkh = sinkv[:rows, hs]
                kP, km = ('Pb', par), ('sm', CUR['sq'])

                def ex(e):
                    for ii in range(4):
                        ins = e.activation(out=Pv[:, ii, :], in_=Ss[:, ii, :], func=AF.Exp, bias=sm[:, 4 + ii:5 + ii], scale=1.0, accum_out=sm[:, 8 + ii:9 + ii])
                    return ins
                P.op('act', ex, R=[('Ssb', par), (km, 1)], W=[kP, (km, 2)], reg='scr')
                P.op('pool', lambda e: e.tensor_tensor(out=sm[:, 12:16], in0=sinkh, in1=sm[:, 4:8], op=ALU.add), R=[(km, 1), 'cst'], W=[(km, 3)])
                P.op('act', lambda e: e.activation(out=sm[:, 16:20], in_=sm[:, 12:16], func=AF.Exp), R=[(km, 3)], W=[(km, 4)])

            def st_s2b(par, rows, nk):
                Pv = Pb[par][:rows, :, :nk]
                sm = sm_of(par, rows)
                kP, km = ('Pb', par), ('sm', CUR['sq'])
                P.op('pool', lambda e: e.tensor_tensor(out=sm[:, 20:24], in0=sm[:, 8:12], in1=sm[:, 16:20], op=ALU.add), R=[(km, 2), (km, 4)], W=[(km, 5)])
                P.op('dve', lambda e: e.reciprocal(out=sm[:, 24:28], in_=sm[:, 20:24]), R=[(km, 5)], W=[(km, 6)])

            def make_prompt_item(idx, b, kc, j):
                par = idx % 2
                sbank, ptb = 2 * par, 4 + par
                bp = b % 2
                qc = (b - 1) * 128
                maskv = mask_f if b == 1 else mask_n
                kvh = 2 * kc + j
                hs = slice(8 * kc + j, 8 * kc + 8, 2)
                it = {}

                def s1pe():
                    def qk(e):
                        for ii in range(4):
                            i = 4 * kc + ii
                            ins = e.matmul(ps[:, sbank + ii // 2, (ii % 2) * 256:(ii % 2) * 256 + 256],
                                     lhsT=QT[64 * j:64 * j + 64, i, qc:qc + 128], rhs=KT[64 * j:64 * j + 64, kc, (b - 1) * 128:(b + 1) * 128],
                                     start=True, stop=True)
                        return ins
                    P.op('pe', qk, R=[('QT', 4 * kc + ii, b) for ii in range(4)] + [('KT', kc, b - 1), ('KT', kc, b)], W=[('bank', sbank), ('bank', sbank + 1)], reg='ringhi')
                it['s1pe'] = s1pe
                it['s1dve'] = lambda: st_s1dve(par, 128, 256, maskv, hs, sbank)
                it['s2a'] = lambda: st_s2a(par, 128, 256, hs, sbank)
                it['s2b'] = lambda: st_s2b(par, 128, 256)

                def s3():
                    def tpp(e):
                        for ii in range(4):
                            for kb in range(2):
                                ins = e.transpose(out=psb[:, ptb, (2 * ii + kb) * 128:(2 * ii + kb + 1) * 128], in_=Pb[par][:, ii, kb * 128:(kb + 1) * 128], identity=identb[:, :])
                        return ins
                    P.op('pe', tpp, R=[('Pb', par), 'identb'], W=[('bank', ptb)], reg='scr')
                    P.op('act', lambda e: e.copy(out=PTs[par][:, :], in_=psb[:, ptb, :]), R=[('bank', ptb)], W=[('PTs', par)], reg='scr')
                it['s3'] = s3

                def s4():
                    def pv(e):
                        for ii in range(4):
                            for kb in range(2):
                                ins = e.matmul(ps[:, 6, par * 256 + ii * 64:par * 256 + (ii + 1) * 64],
                                               lhsT=PTs[par][:, (2 * ii + kb) * 128:(2 * ii + kb + 1) * 128], rhs=Vb[:, b - 1 + kb, kvh * 64:(kvh + 1) * 64],
                                               start=(kb == 0), stop=(kb == 1))
                        return ins
                    P.op('pe', pv, R=[('PTs', par), ('V', b - 1), ('V', b)], W=[('bank', 6)], reg=['scr', 'ringhi'])
                    sq_ = idx % 4
                    P.op('dve', lambda e: e.tensor_tensor(
                        out=atm[bp][:, :].rearrange("p (i jj d) -> p i jj d", i=8, jj=2)[:, 4 * kc:4 * kc + 4, j, :],
                        in0=ps[:, 6, par * 256:(par + 1) * 256].rearrange("p (i d) -> p i d", i=4),
                        in1=small[:, sq_, 24:28].unsqueeze(2).to_broadcast([128, 4, 64]), op=ALU.mult),
                        R=[('bank', 6), (('sm', sq_), 6)], W=[('atm', bp, kc, j)], reg='scr')
                    if kc == 1 and j == 1:
                        def tpa(e):
                            for i in range(8):
                                ins = e.transpose(out=psb[:, 7, i * 128:(i + 1) * 128], in_=atm[bp][:, i * 128:(i + 1) * 128], identity=identb[:, :])
                            return ins
                        P.op('pe', tpa, R=[('atm', bp, kk, jj) for kk in range(2) for jj in range(2)] + ['identb'], W=[('bank', 7)], reg='scr')
                        P.op('dve', lambda e: e.tensor_copy(out=acts[:, 8:16, b * 128:(b + 1) * 128], in_=psb[:, 7, :].rearrange("p (i t) -> p i t", i=8)),
                             R=[('bank', 7)], W=[('acts', 8 + i, b) for i in range(8)])
                it['s4'] = s4
                return it

            def make_sample_item(idx, kc, j):
                par = idx % 2
                sbank, ptb = 2 * par, 4 + par
                bp = 1
                kvh = 2 * kc + j
                hs = slice(8 * kc + j, 8 * kc + 8, 2)
                it = {}

                def s1pe():
                    def qk(e):
                        for ii in range(4):
                            i = 4 * kc + ii
                            o0 = (ii % 2) * 256
                            for s in range(16):
                                e.matmul(ps[:64, sbank + ii // 2, o0:o0 + 128], lhsT=QTm[64 * j:64 * j + 64, i, s, :], rhs=KcT[64 * j:64 * j + 64, kc, s, :],
                                         start=(s == 0), stop=(s == 15))
                            ins = e.matmul(ps[:64, sbank + ii // 2, o0 + 128:o0 + 192], lhsT=QT[64 * j:64 * j + 64, i, 1024:1088], rhs=KT[64 * j:64 * j + 64, kc, 1152:1216],
                                           start=True, stop=True)
                        return ins
                    P.op('pe', qk, R=[('QTm', 4 * kc + ii) for ii in range(4)] + [('QT', 4 * kc + ii, 9) for ii in range(4)] + [('KcT', kc, 0), ('KcT', kc, 1), ('KT', kc, 9)],
                         W=[('bank', sbank), ('bank', sbank + 1)], reg=['flex', 'ringhi'])
                it['s1pe'] = s1pe
                it['s1dve'] = lambda: st_s1dve(par, 64, 192, mask_s, hs, sbank)
                it['s2a'] = lambda: st_s2a(par, 64, 192, hs, sbank)
                it['s2b'] = lambda: st_s2b(par, 64, 192)

                def s3():
                    def tpp(e):
                        for ii in range(4):
                            e.transpose(out=psb[:, ptb, ii * 64:(ii + 1) * 64], in_=Pb[par][:64, ii, 0:128], identity=identb[:64, :64])
                            ins = e.transpose(out=psb[:64, ptb, 256 + ii * 64:256 + (ii + 1) * 64], in_=Pb[par][:64, ii, 128:192], identity=identb[:64, :64])
                        return ins
                    P.op('pe', tpp, R=[('Pb', par), 'identb'], W=[('bank', ptb)], reg='scr')
                    P.op('act', lambda e: e.copy(out=PTs[par][:, 0:512], in_=psb[:, ptb, 0:512]), R=[('bank', ptb)], W=[('PTs', par)], reg='scr')
                    P.op('dve', lambda e: e.tensor_tensor(
                        out=PTm[par][:, :, :, :], in0=PTs[par][:, 0:256].rearrange("p (i r) -> p i r", i=4).unsqueeze(2).to_broadcast([128, 4, 16, 64]),
                        in1=bm[:, :, :].unsqueeze(1).to_broadcast([128, 4, 16, 64]), op=ALU.mult),
                        R=[('PTs', par), 'bm'], W=[('PTm', par)], reg=['flex', 'scr'])
                it['s3'] = s3

                def s4():
                    def pv(e):
                        for ii in range(4):
                            oo = par * 256 + ii * 64
                            for s in range(16):
                                e.matmul(ps[:64, 6, oo:oo + 64], lhsT=PTm[par][:, ii, s, :], rhs=Vc[:, s, kvh * 64:(kvh + 1) * 64], start=(s == 0), stop=False)
                            ins = e.matmul(ps[:64, 6, oo:oo + 64], lhsT=PTs[par][:64, 256 + ii * 64:256 + (ii + 1) * 64], rhs=Vb[:64, 9, kvh * 64:(kvh + 1) * 64],
                                           start=False, stop=True)
                        return ins
                    P.op('pe', pv, R=[('PTm', par), ('PTs', par), 'Vc', ('V', 9)], W=[('bank', 6)], reg=['flex', 'scr', 'ringhi'])
                    sq_ = idx % 4
                    P.op('dve', lambda e: e.tensor_tensor(
                        out=atm_s[:64, :].rearrange("p (i jj d) -> p i jj d", i=8, jj=2)[:, 4 * kc:4 * kc + 4, j, :],
                        in0=ps[:64, 6, par * 256:(par + 1) * 256].rearrange("p (i d) -> p i d", i=4),
                        in1=small[:64, sq_, 24:28].unsqueeze(2).to_broadcast([64, 4, 64]), op=ALU.mult),
                        R=[('bank', 6), (('sm', sq_), 6)], W=[('atm', 's', kc, j)], reg='scr', extra=P.fence('scr'))
                    if kc == 1 and j == 1:
                        def tpa9(e):
                            for i in range(8):
                                ins = e.transpose(out=psb[:, 7, i * 128:i * 128 + 64], in_=atm_s[:64, i * 128:(i + 1) * 128], identity=identb[:64, :64])
                            return ins
                        P.op('pe', tpa9, R=[('atm', 's', kk, jj) for kk in range(2) for jj in range(2)] + ['identb'], W=[('bank', 7)], reg='scr')
                        P.op('dve', lambda e: e.tensor_copy(out=acts[:, 8:16, 1152:1216], in_=psb[:, 7, :].rearrange("p (i t) -> p i t", i=8)[:, :, 0:64]),
                             R=[('bank', 7)], W=[('acts', 8 + i, 9) for i in range(8)], reg='flex')
                it['s4'] = s4
                return it

            fl = P.fence('flex')
            flU = P.fence('flexU')
            P.op('pool', lambda e: e.dma_start(out=ckb[:, :, :], in_=ckd.rearrange("s j c -> j s c")), W=['ckb'], dma=True, extra=flU, reg='flex')
            P.op('pool', lambda e: e.dma_start(out=Vc[:, :, :], in_=cvd.rearrange("s j c -> j s c")), W=['Vc'], dma=True, extra=flU, reg='flex')
            for kc in range(2):
                for hf in range(2):
                    bank = 2 * kc + hf

                    def tpk(e, kc=kc, hf=hf, bank=bank):
                        for sl in range(8):
                            s = 8 * hf + sl
                            ins = e.transpose(out=psb[:, bank, sl * 128:(sl + 1) * 128], in_=ckb[:, s, kc * 128:(kc + 1) * 128], identity=identb[:, :])
                        return ins
                    P.op('pe', tpk, R=['ckb', 'identb'], W=[('bank', bank)], reg='flex')
                    P.op('dve', lambda e, kc=kc, hf=hf, bank=bank: e.tensor_copy(out=KcT[:, kc, 8 * hf:8 * hf + 8, :], in_=psb[:, bank, :].rearrange("p (s j) -> p s j", s=8)),
                         R=[('bank', bank)], W=[('KcT', kc, hf)], extra=flU, reg='flex')
            for i in range(8):
                P.op('dve', lambda e, i=i: e.tensor_tensor(out=QTm[:, i, :, :], in0=QT[:, i, 1024:1088].unsqueeze(1).to_broadcast([128, 16, 64]), in1=bm[:, :, :], op=ALU.mult),
                     R=[('QT', i, 9), 'bm'], W=[('QTm', i)], extra=fl, reg=['flex', 'ringhi'])

            rs['mod'] = 8
            rs['n'] = 0
            wout_s = [load_pair(w_out[:, 512 * m:512 * m + 512].rearrange("(k p) c -> p k c", p=128)) for m in range(2)]
            items = []
            order = [('p', b, kc, j) for b in range(1, 9) for kc in range(2) for j in range(2)]
            samp = [('s', 0, kc, j) for kc in range(2) for j in range(2)]
            for pos_, sd in zip((1, 5, 10, 15), samp):
                order.insert(pos_, sd)
            for (kind_, b, kc, j) in order:
                if kind_ == 's':
                    items.append(make_sample_item(len(items), kc, j))
                else:
                    items.append(make_prompt_item(len(items), b, kc, j))
            NI = len(items)
            for t in range(NI + 4):
                def g(d):
                    n = t - d
                    if 0 <= n < NI:
                        return {k: (lambda f=f, n=n: (CUR.__setitem__('sq', n % 4), f())) for k, f in items[n].items()}
                    return None
                if g(0): g(0)['s1pe']()
                if g(4): g(4)['s4']()
                if g(3): g(3)['s3']()
                if g(2): g(2)['s2b']()
                if g(1): g(1)['s2a']()
                if g(0): g(0)['s1dve']()
                if t == 8:
                    chk('SATT')
            chk('ATT')
            flx = P.fence('flex')
            pend_nt = []
            scf = P.fence('scr')
            for b in range(1, 10):
                rows = rows_of(b)
                src = xs[:, :] if b == 9 else xp[(b - 1) * 128:b * 128, :]
                P.op('sp', lambda e, src=src, b=b, rows=rows: e.dma_start(out=xres[:rows, b - 1, :], in_=src), W=[('xres', b, m) for m in range(4)], dma=True, extra=flx)
            ostep = 0
            for m in range(4):
                s0 = wout_s[m] if m < 2 else load_pair(w_out[:, 512 * m:512 * m + 512].rearrange("(k p) c -> p k c", p=128), extra=P.fence('ringhi'))
                for b in range(1, 10):
                    rows = rows_of(b)
                    bank = 4 + ostep % 4
                    ostep += 1

                    def mm(e, s0=s0, b=b, rows=rows, bank=bank):
                        for k in range(16):
                            ins = e.matmul(ps[:rows, bank, :], lhsT=acts[:, k, b * 128:b * 128 + rows],
                                           rhs=p16(s0)[:, k, :], start=(k == 0), stop=(k == 15))
                        return ins
                    P.op('pe', mm, R=[('ring', s0), ('ring', s0 + 1)] + [('acts', k, b) for k in range(16)], W=[('bank', bank)])
                    P.op('dve', lambda e, b=b, m=m, rows=rows, bank=bank: e.tensor_tensor(out=xres[:rows, b - 1, 512 * m:512 * (m + 1)], in0=ps[:rows, bank, :],
                                                                                          in1=xres[:rows, b - 1, 512 * m:512 * (m + 1)], op=ALU.add),
                         R=[('bank', bank), ('xres', b, m)], W=[('xres', b, m)])
                    if m == 3 and b >= 2:
                        bb = b - 1
                        while pend_nt:
                            pend_nt.pop(0)()
                        norm_T(bb, xres[:rows_of(bb), bb - 1, :], [('xres', bb, mm_) for mm_ in range(4)], gffnT, 30 + bb, extra=scf, defer=pend_nt)
            while pend_nt:
                pend_nt.pop(0)()
            norm_T(9, xres[:64, 8, :], [('xres', 9, mm_) for mm_ in range(4)], gffnT, 39, extra=scf)
            chk('D')
            chk('D2')
            chk('D2')
            TT2 = [(128, 384, [1, 2, 3]), (512, 384, [4, 5, 6]), (896, 320, [7, 8, 9])]
            H2 = [('acts', k, b) for k in range(16) for b in range(1, 10)]

            def d2(s):
                return ring[:, s, :].rearrange("p (kk n) -> p kk n", kk=2)
            dstep = 0
            sg_i = 0
            pend_fin = []
            for g in range(11):
                gp = g % 2
                gs, us_, ds = [], [], []
                for u in range(2):
                    c0 = 512 * g + 256 * u
                    gs.append(load_unit(u16, w_gate[:, c0:c0 + 256].rearrange("(k p) c -> p k c", p=128)))
                    us_.append(load_unit(u16, w_up[:, c0:c0 + 256].rearrange("(k p) c -> p k c", p=128)))
                for c in range(4):
                    u, cc = c // 2, c % 2
                    for tt, (t0, n, blks) in enumerate(TT2):
                        for which, sl, bk in (('g', gs[u], tt), ('u', us_[u], 3 + tt)):
                            def mm(e, sl=sl, cc=cc, t0=t0, n=n, bk=bk):
                                for k in range(16):
                                    ins = e.matmul(ps[:, bk, :n], lhsT=u16(sl)[:, k, cc * 128:(cc + 1) * 128], rhs=acts[:, k, t0:t0 + n], start=(k == 0), stop=(k == 15))
                                return ins
                            last_gu = P.op('pe', mm, R=[('ring', sl)] + [('acts', k, bb) for k in range(16) for bb in blks], W=[('bank', bk)])
                        sq = sg_i % 3
                        sg_i += 1
                        P.op('act', lambda e, sq=sq, tt=tt, n=n: e.activation(out=sgt[sq][:, :n], in_=ps[:, tt, :n], func=AF.Silu), R=[('bank', tt)], W=[('sgt', sq)], reg='scr')
                        P.op('dve', lambda e, sq=sq, tt=tt, n=n, t0=t0, c=c, gp=gp: e.tensor_tensor(out=actT[gp][:, c, t0 - 128:t0 - 128 + n], in0=sgt[sq][:, :n], in1=ps[:, 3 + tt, :n], op=ALU.mult),
                             R=[('sgt', sq), ('bank', 3 + tt)], W=[('actT', gp, c, tt)], reg='scr')
                for u in range(2):
                    r0 = 512 * g + 256 * u
                    ds.append(load_unit(d2, w_down[r0:r0 + 256, :].rearrange("(kk p) n -> p kk n", p=128)))
                if g == 9:
                    ds9, gp9 = list(ds), gp
                    continue
                chunks = [(gp, c, ds[c // 2], c % 2) for c in range(4)]
                if g == 10:
                    chunks = [(gp9, c, ds9[c // 2], c % 2) for c in range(4)] + chunks
                    accf = acts[:].rearrange("p k t -> p (k t)").bitcast(F32)
                    gfin = accf[:, 0:2048]
                    ystg = [accf[:, 2048:4096], accf[:, 4096:6144]]
                    P.op('sp', lambda e: e.dma_start(out=gfin, in_=gfin_d[:, :]), W=['gfinA'], dma=True, extra=[last_gu])
                for b in range(1, 10):
                    rows = rows_of(b)
                    tcol = (b - 1) * 128
                    ttb = min((b - 1) // 3, 2)
                    for m in range(4):
                        bank = (dstep % 8) if g == 10 else 6 + dstep % 2
                        dstep += 1

                        def mm(e, chunks=tuple(chunks), tcol=tcol, rows=rows, m=m, bank=bank):
                            for i_, (gpp, c, slot, kk) in enumerate(chunks):
                                ins = e.matmul(ps[:rows, bank, :], lhsT=actT[gpp][:, c, tcol:tcol + rows], rhs=d2(slot)[:, kk, 512 * m:512 * (m + 1)],
                                               start=(i_ == 0), stop=(i_ == len(chunks) - 1))
                            return ins
                        P.op('pe', mm, R=sorted(set(('ring', sl_) for (_, _, sl_, _) in chunks)) + [('actT', gpp, c, ttb) for (gpp, c, _, _) in chunks], W=[('bank', bank)], reg='scr')
                        P.op('dve', lambda e, b=b, m=m, rows=rows, bank=bank: e.tensor_tensor(out=xres[:rows, b - 1, 512 * m:512 * (m + 1)], in0=ps[:rows, bank, :],
                                                                                              in1=xres[:rows, b - 1, 512 * m:512 * (m + 1)], op=ALU.add),
                             R=[('bank', bank), ('xres', b, m)], W=[('xres', b, m)])
                    if g == 10:
                        while pend_fin:
                            pend_fin.pop(0)()
                    if g == 10:
                      def fin(b=b, rows=rows):
                        sc = 60 + b
                        q = b % 2
                        xk = [('xres', b, m) for m in range(4)]
                        yk = ('ystgA', q)
                        P.op('act', lambda e, b=b, rows=rows, sc=sc, q=q: e.activation(out=ystg[q][:rows, :], in_=xres[:rows, b - 1, :], func=AF.Square, accum_out=stat[:rows, sc:sc + 1]),
                             R=xk, W=[yk, ('st', sc)], extra=[last_gu])
                        P.op('act', lambda e, rows=rows, sc=sc: e.activation(out=stat[:rows, sc + 10:sc + 11], in_=stat[:rows, sc:sc + 1], func=AF.Sqrt, bias=epst[:rows], scale=1.0 / D),
                             R=[('st', sc), 'eps'], W=[('st', sc + 10)])
                        P.op('dve', lambda e, rows=rows, sc=sc: e.reciprocal(out=stat[:rows, sc + 20:sc + 21], in_=stat[:rows, sc + 10:sc + 11]), R=[('st', sc + 10)], W=[('st', sc + 20)])
                        P.op('dve', lambda e, b=b, rows=rows, sc=sc, q=q: e.scalar_tensor_tensor(out=ystg[q][:rows, :], in0=xres[:rows, b - 1, :], scalar=stat[:rows, sc + 20:sc + 21],
                                                                                                 in1=gfin[:rows, :], op0=ALU.mult, op1=ALU.mult),
                             R=xk + [('st', sc + 20), 'gfinA'], W=[yk])
                        dst = ys[:, :] if b == 9 else yp[(b - 1) * 128:b * 128, :]
                        P.op('sp', lambda e, dst=dst, rows=rows, q=q: e.dma_start(out=dst, in_=ystg[q][:rows, :]), R=[yk], W=[('o_y', b)], dma=True)
                      pend_fin.append(fin)
            while pend_fin:
                pend_fin.pop(0)()

            chk('FFN')
            allout = ['o_kp', 'o_vp', 'o_ks_c', 'o_vs_c', 'o_ps_c', 'o_pp'] + [('o_ks', t) for t in range(4)] + [('o_vs', t) for t in range(4)] + \
                     [('o_ps', t) for t in range(4)] + [('o_y', b) for b in range(1, 10)]
            P.op('sp', None, R=allout)

        except _Stop:
            P.op('sp', None, R=list(P.lw.keys()))
        P.resolve()
        P.emit(block, sems, dsems)
    return nc


_CACHE = {}


def _rope_tab(pos):
    inv = (np.float32(500000.0) ** (-np.arange(0, 16, 2, dtype=np.float32) / np.float32(16))).astype(np.float32)
    ang = pos.astype(np.float32)[:, None] * inv[None, :]
    cos = np.cos(ang).astype(np.float32)
    sin = np.sin(ang).astype(np.float32)
    return np.concatenate([cos, cos, sin, sin], axis=1)


def _prepare(x_prompt, x_sample, state_pool, cache_k_win, cache_v_win, g_mix, w_in, w_pool,
             pool_scale, attn_sinks, w_out, g_ffn, w_gate, w_up, w_down, g_final):
    f = lambda a: np.ascontiguousarray(np.asarray(a, dtype=np.float32))
    x_prompt, x_sample, state_pool = f(x_prompt), f(x_sample), f(state_pool)
    ck_all, cv_all = f(cache_k_win)[0].reshape(128, 128, 256), f(cache_v_win)[0].reshape(128, 128, 256)
    g_mix, g_ffn, g_final = f(g_mix)[0], f(g_ffn)[0], f(g_final)
    w_in0, w_pool0, w_out0 = f(w_in)[0], f(w_pool)[0], f(w_out)[0]
    w_gate0, w_up0, w_down0 = f(w_gate)[0], f(w_up)[0], f(w_down)[0]
    pscale, sinks = f(pool_scale)[0], f(attn_sinks)[0]
    qcols = np.concatenate([1024 + 64 * h + np.arange(64) for h in PERM])
    w_in_p = np.ascontiguousarray(np.concatenate([w_in0[:, :1024], w_in0[:, qcols], w_in0[:, 2048:]], axis=1))
    w_out_p = np.ascontiguousarray(np.concatenate([w_out0[:1024], w_out0[qcols]], axis=0))
    sinks_p = sinks[PERM]

    qi = np.arange(128)[:, None]
    kj = np.arange(256)[None, :]
    diff = qi + 128 - kj
    mn = np.where((diff >= 0) & (diff <= 128), 0.0, NEG).astype(np.float32)
    mf = mn.copy()
    mf[:, :128] = NEG
    r = np.arange(64)
    tr, sr = r // 16, r % 16
    ms = np.full((128, 256), NEG, np.float32)
    jj = np.arange(128)
    ms[:64, :128] = np.where(jj[None, :] >= tr[:, None], 0.0, NEG)
    n = np.arange(64)
    tn, sn = n // 16, n % 16
    ms[:64, 128:192] = np.where((sn[None, :] == sr[:, None]) & (tn[None, :] <= tr[:, None]), 0.0, NEG)
    bmf = np.zeros((16, 64), np.float32)
    bmf[sr, r] = 1.0
    bm = np.broadcast_to(bmf.reshape(1, 1024), (128, 1024)).astype(ml_dtypes.bfloat16)
    gfin_b = np.ascontiguousarray(np.broadcast_to(g_final[None, :], (128, D)))

    in_maps = []
    for c in range(NCORE):
        seq, ch = c // 4, c % 4
        t0 = ch * 1024
        xpc = x_prompt[seq, t0:t0 + 1024]
        xhc = x_prompt[seq, t0 - 128:t0] if ch > 0 else np.zeros((128, D), np.float32)
        xsc = np.ascontiguousarray(x_sample[16 * c:16 * c + 16].transpose(1, 0, 2).reshape(64, D))
        cst = np.zeros((128, CW), np.float32)
        cst[:, C_GMIX:C_GMIX + 16] = g_mix.reshape(16, 128).T
        cst[:, C_GFFN:C_GFFN + 16] = g_ffn.reshape(16, 128).T
        cst[:, C_PSC:C_PSC + 8] = pscale.reshape(8, 128).T
        cst[:, C_SINK:C_SINK + 16] = sinks_p[None, :]
        cst[:, C_NSINK:C_NSINK + 16] = np.negative(sinks_p)[None, :]
        cst[:, C_MN:C_MN + 256] = mn
        cst[:, C_MF:C_MF + 256] = mn if ch > 0 else mf
        cst[:, C_MS:C_MS + 256] = ms
        pos = np.zeros((10, 128), np.int64)
        pos[:9] = (t0 - 128 + np.arange(1152)).reshape(9, 128)
        pos[9, :64] = PAST + tr
        rt = _rope_tab(pos.reshape(-1)).reshape(10, 128, 32).transpose(1, 0, 2)
        cst[:, C_ROPE:C_ROPE + 320] = rt.reshape(128, 320)
        iv = np.zeros((8, 16), np.float32)
        for cch in range(8):
            w = 2 ** (cch // 2 + 1)
            p = np.arange(16) + (t0 if ch > 0 else 0)
            iv[cch] = 1.0 / np.minimum(p + 1, w).astype(np.float32)
        cst[:, C_INVC:C_INVC + 128] = iv.reshape(1, 128)
        in_maps.append({
            "xp": np.ascontiguousarray(xpc), "xh": np.ascontiguousarray(xhc), "xs": xsc,
            "sp": np.ascontiguousarray(state_pool[0, 16 * c:16 * c + 16]),
            "ck": np.ascontiguousarray(ck_all[16 * c:16 * c + 16]), "cv": np.ascontiguousarray(cv_all[16 * c:16 * c + 16]),
            "w_in": w_in_p, "w_pool": w_pool0, "w_out": w_out_p, "w_gate": w_gate0, "w_up": w_up0, "w_down": w_down0,
            "cst": cst, "gfin": gfin_b, "bm": bm,
        })
    return in_maps


def kernel(**inputs):
    in_maps = _prepare(**inputs)
    if 'nc' not in _CACHE:
        _CACHE['nc'] = build_program()
    nc = _CACHE['nc']
    res = run_bass_kernel_spmd(nc, in_maps, core_ids=list(range(NCORE)))
    return _assemble(res.results)


def _assemble(R):
    y_prompt = np.stack([np.concatenate([R[4 * s + ch]["yp"] for ch in range(4)], axis=0) for s in range(2)], axis=0)
    y_sample = np.concatenate([R[c]["ys"].reshape(4, 16, D).transpose(1, 0, 2) for c in range(NCORE)], axis=0)
    new_pool_p = np.stack([R[3]["pool_p"], R[7]["pool_p"]], axis=0)[None]
    new_k_p = np.stack([R[3]["k_p"], R[7]["k_p"]], axis=0).reshape(1, 2, 128, 4, 64)
    new_v_p = np.stack([R[3]["v_p"], R[7]["v_p"]], axis=0).reshape(1, 2, 128, 4, 64)
    new_pool_s = np.concatenate([R[c]["pool_s"] for c in range(NCORE)], axis=0)[None]
    new_k_s = np.concatenate([R[c]["k_s"] for c in range(NCORE)], axis=0).reshape(1, 128, 128, 4, 64)
    new_v_s = np.concatenate([R[c]["v_s"] for c in range(NCORE)], axis=0).reshape(1, 128, 128, 4, 64)
    asf = lambda a: np.ascontiguousarray(a, dtype=np.float32)
    return (asf(y_prompt), asf(y_sample), asf(new_pool_p), asf(new_k_p), asf(new_v_p), asf(new_pool_s), asf(new_k_s), asf(new_v_s))
```

```python
import contextlib
import numpy as np
import ml_dtypes
import concourse.bass as bass
import concourse.mybir as mybir
from concourse.bass_utils import run_bass_kernel_spmd

F32 = mybir.dt.float32
BF16 = mybir.dt.bfloat16
AF = mybir.ActivationFunctionType
ALU = mybir.AluOpType
AX = mybir.AxisListType

D = 2048
DFF = 5632
NCORE = 8
PAST = 16384
EPS = 1e-5
NEG = -30000.0
PERM = [0, 4, 1, 5, 2, 6, 3, 7, 8, 12, 9, 13, 10, 14, 11, 15]
ENGS = ['pe', 'act', 'dve', 'pool', 'sp']
NORM_ENG = 'pool'
NDSEM = {'sp': 12, 'pool': 6, 'act': 4, 'pe': 1, 'dve': 1}

C_GMIX, C_GFFN, C_PSC, C_SINK, C_NSINK = 0, 16, 32, 40, 56
C_MN, C_MF, C_MS, C_ROPE, C_INVC = 72, 328, 584, 840, 1160
CW = 1288


class Op:
    pass


class Planner:
    def __init__(self):
        self.ops = []
        self.lw = {}
        self.rd = {}
        self.cnt = {e: 0 for e in ENGS}
        self.rr = {e: 0 for e in ENGS}
        self.dlast = {}
        self.dcnt = {}
        self.reg_last = {}

    def op(self, eng, fn, R=(), W=(), dma=False, extra=(), reg=None):
        o = Op()
        o.eng, o.fn, o.dma = eng, fn, dma
        deps = {}
        for k in R:
            w = self.lw.get(k)
            if w is not None:
                deps[w] = True
        for k in W:
            w = self.lw.get(k)
            if w is not None:
                deps.setdefault(w, False)
            for r in self.rd.get(k, ()):
                deps.setdefault(r, False)
        for e in extra:
            if e is not None:
                deps[e] = True
        if dma:
            si = self.rr[eng]
            self.rr[eng] = (si + 1) % NDSEM[eng]
            prev = self.dlast.get((eng, si))
            if prev is not None:
                deps[prev] = True
            self.dlast[(eng, si)] = o
            n = self.dcnt.get((eng, si), 0) + 1
            self.dcnt[(eng, si)] = n
            o.dsem, o.dval = (eng, si), 16 * n
        o.deps = deps
        for k in W:
            self.lw[k] = o
            self.rd[k] = []
        for k in R:
            self.rd.setdefault(k, []).append(o)
        o.pos = self.cnt[eng]
        self.cnt[eng] += 1
        o.mark = False
        o.ms = 0
        self.ops.append(o)
        if reg is not None:
            for r in (reg if isinstance(reg, (list, tuple)) else [reg]):
                if dma:
                    self.reg_last.setdefault(r, {}).setdefault('dmas', []).append(o)
                else:
                    self.reg_last.setdefault(r, {})[eng] = o
        return o

    def fence(self, *regs):
        out = []
        for r in regs:
            for k, v in self.reg_last.get(r, {}).items():
                out += v if k == 'dmas' else [v]
        return out

    def all_last(self):
        last = {}
        dmas = []
        for o in self.ops:
            if o.dma:
                dmas.append(o)
            else:
                last[o.eng] = o
        return list(last.values()) + dmas

    def resolve(self):
        waited = {e: {} for e in ENGS}
        for o in self.ops:
            o.waits = []
            best = {}
            for a, raw in o.deps.items():
                if a is o:
                    continue
                if a.dma:
                    key = ('d',) + a.dsem
                    if waited[o.eng].get(key, 0) < a.dval:
                        waited[o.eng][key] = a.dval
                        o.waits.append(a)
                    continue
                if a.eng == o.eng and not o.dma:
                    if o.eng == 'pe':
                        continue
                if a.eng not in best or best[a.eng].pos < a.pos:
                    best[a.eng] = a
            for e, a in best.items():
                if waited[o.eng].get(e, -1) < a.pos:
                    waited[o.eng][e] = a.pos
                    a.mark = True
                    o.waits.append(a)
        c = {e: 0 for e in ENGS}
        for o in self.ops:
            if o.mark:
                c[o.eng] += 1
                o.ms = c[o.eng]

    def emit(self, block, sems, dsems):
        deco = {'pe': block.tensor, 'act': block.scalar, 'dve': block.vector,
                'pool': block.gpsimd, 'sp': block.sync}
        for eng in ENGS:
            ops = [o for o in self.ops if o.eng == eng]

            def body(e, ops=ops):
                for o in ops:
                    for a in o.waits:
                        if a.dma:
                            e.wait_ge(dsems[a.dsem], a.dval)
                        else:
                            e.wait_ge(sems[a.eng], a.ms)
                    if o.fn is None:
                        continue
                    ins = o.fn(e)
                    if o.dma:
                        ins.then_inc(dsems[o.dsem], 16)
                    elif o.mark:
                        ins.then_inc(sems[o.eng], 1)
            deco[eng](body)


class _Stop(Exception):
    pass


def build_program(stop_at=None):
    nc = bass.Bass("TRN2", target_bir_lowering=False)

    def din(name, shape, dt=F32):
        return nc.dram_tensor(name, list(shape), dt, kind="ExternalInput").ap()

    def dout(name, shape):
        return nc.dram_tensor(name, list(shape), F32, kind="ExternalOutput").ap()

    xp = din("xp", [1024, D]); xh = din("xh", [128, D]); xs = din("xs", [64, D])
    spd = din("sp", [16, 15, 1024]); ckd = din("ck", [16, 128, 256]); cvd = din("cv", [16, 128, 256])
    w_in = din("w_in", [D, 2560]); w_pool = din("w_pool", [4, 256, 256]); w_out = din("w_out", [D, D])
    w_gate = din("w_gate", [D, DFF]); w_up = din("w_up", [D, DFF]); w_down = din("w_down", [DFF, D])
    cst_d = din("cst", [128, CW]); gfin_d = din("gfin", [128, D]); bm_d = din("bm", [128, 1024], BF16)
    yp = dout("yp", [1024, D]); ys = dout("ys", [64, D])
    pool_p = dout("pool_p", [15, 1024]); k_p = dout("k_p", [128, 256]); v_p = dout("v_p", [128, 256])
    pool_s = dout("pool_s", [16, 15, 1024]); k_s = dout("k_s", [16, 128, 256]); v_s = dout("v_s", [16, 128, 256])

    P = Planner()
    stack = contextlib.ExitStack()
    with stack:
        def sb(name, shape, dt):
            return stack.enter_context(nc.sbuf_tensor(name, shape, dt))
        acts = sb("acts", [128, 16, 1216], BF16)
        ring = sb("ring", [128, 8, 4096], BF16)
        flex = sb("flex", [128, 18432], F32)
        scr = sb("scr", [128, 6144], F32)
        cst = sb("cst_sb", [128, CW], F32)
        bm = sb("bm_sb", [128, 16, 64], BF16)
        identf = sb("identf", [128, 128], F32)
        identb = sb("identb", [128, 128], BF16)
        stat = sb("stat", [128, 128], F32)
        small = sb("small", [128, 4, 64], F32)
        ps = stack.enter_context(nc.psum_tensor("ps", [128, 8, 512], F32))
        sems = {e: stack.enter_context(nc.semaphore("sem_" + e)) for e in ENGS}
        dsems = {}
        for e in ('sp', 'pool', 'act'):
            for i in range(NDSEM[e]):
                dsems[(e, i)] = stack.enter_context(nc.semaphore("dsem_%s_%d" % (e, i)))
        block = stack.enter_context(nc.Block())

        psb = ps[:].bitcast(BF16)
        ringf = ring[:].bitcast(F32)
        flexb = flex[:].bitcast(BF16)
        scrb = scr[:].bitcast(BF16)

        gmixT = cst[:, C_GMIX:C_GMIX + 16]; gffnT = cst[:, C_GFFN:C_GFFN + 16]
        pscT = cst[:, C_PSC:C_PSC + 8]
        sinkv = cst[:, C_SINK:C_SINK + 16]; nsinkv = cst[:, C_NSINK:C_NSINK + 16]
        mask_n = cst[:, C_MN:C_MN + 256]; mask_f = cst[:, C_MF:C_MF + 256]; mask_s = cst[:, C_MS:C_MS + 256]
        rope = cst[:, C_ROPE:C_ROPE + 320].rearrange("p (b c) -> p b c", b=10)
        invc = cst[:, C_INVC:C_INVC + 128].rearrange("p (c t) -> p c t", c=8)
        epst = stat[:, 127:128]
        ringhi = ring[:, 4:8, :].rearrange("p s e -> p (s e)")
        QT = ringhi[:, 0:8704].rearrange("p (i t) -> p i t", i=8)
        KT = ringhi[:, 8704:11136].rearrange("p (i t) -> p i t", i=2)
        Vb = ringhi[:, 11136:13696].rearrange("p (b c) -> p b c", b=10)
        kvf = ringf[:, 7, 800:1824].rearrange("p (a c) -> p a c", a=4)
        uT = flex[:, 0:8832].rearrange("p (c t) -> p c t", c=8)
        dT = flexb[:, 17664:26368].rearrange("p (c t) -> p c t", c=8)
        wsA = flex[:, 13184:14224]; wsB = flex[:, 14224:15264]
        usT = flex[:, 15264:17696].rearrange("p (c s h) -> p c s h", c=8, s=16)
        xres = flex[:, :].rearrange("p (b c) -> p b c", b=9)
        ckb = flexb[:, 0:4096].rearrange("p (s c) -> p s c", s=16)
        Vc = flexb[:, 4096:8192].rearrange("p (s c) -> p s c", s=16)
        KcT = flexb[:, 8192:12288].rearrange("p (k s j) -> p k s j", k=2, s=16)
        QTm = flexb[:, 12288:20480].rearrange("p (i s r) -> p i s r", i=8, s=16)
        PTm = [flexb[:, 20480 + 4096 * q:20480 + 4096 * (q + 1)].rearrange("p (i s r) -> p i s r", i=4, s=16) for q in range(2)]
        xa = [scr[:, 0:2048], scr[:, 2048:4096], flex[:, 8832:10880], flex[:, 10880:12928]]
        xnb = [scrb[:, 8192:10240], scrb[:, 10240:12288]]
        S_sb = [scr[:, 1024 * q:1024 * (q + 1)].rearrange("p (i k) -> p i k", i=4) for q in range(2)]
        Pb = [scrb[:, 4096 + 1024 * q:4096 + 1024 * (q + 1)].rearrange("p (i k) -> p i k", i=4) for q in range(2)]
        PTs = [scrb[:, 6144 + 1024 * q:6144 + 1024 * (q + 1)] for q in range(2)]
        atm = [scrb[:, 8192 + 1024 * q:8192 + 1024 * (q + 1)] for q in range(2)]
        ostg = scr[:, 5120:6144]
        atm_s = scrb[:, 10240:11264]
        qf = [scr[:, 2048 + 512 * q:2048 + 512 * (q + 1)] for q in range(2)]
        qb = [scrb[:, 6144 + 512 * q:6144 + 512 * (q + 1)] for q in range(2)]
        rtA = scr[:, 3584:3840].rearrange("p (h c) -> p h c", c=16)[:, 0:8, :]
        rtB = scr[:, 3840:4096].rearrange("p (h c) -> p h c", c=16)[:, 0:8, :]
        actT = [scrb[:, 4352 * q:4352 * (q + 1)].rearrange("p (c t) -> p c t", c=4) for q in range(2)]
        sgt = [scrb[:, 8704 + 512 * q:8704 + 512 * (q + 1)] for q in range(3)]

        def rows_of(b):
            return 64 if b == 9 else 128

        def chk(name):
            if stop_at == name:
                raise _Stop()

        try:
            P.op('sp', lambda e: e.dma_start(out=cst[:], in_=cst_d[:, :]), W=['cst'], dma=True)
            P.op('sp', lambda e: e.dma_start(out=bm[:].rearrange("p s r -> p (s r)"), in_=bm_d[:, :]), W=['bm'], dma=True)
            P.op('pool', lambda e: e.memset(identf[:], 0.0), W=['identf'])
            P.op('pool', lambda e: e.affine_select(out=identf[:], in_=identf[:], pattern=[[-1, 128]],
                                                   compare_op=ALU.not_equal, fill=1.0, base=0, channel_multiplier=1),
                 R=['identf'], W=['identf'])
            P.op('dve', lambda e: e.tensor_copy(out=identb[:], in_=identf[:]), R=['identf'], W=['identb'])
            P.op('dve', lambda e: e.memset(epst, EPS), W=['eps'])

            rs = {'n': 0, 'mod': 4}

            def next_slot():
                s = rs['n'] % rs['mod']
                rs['n'] += 1
                return s

            def load_unit(view_fn, src, eng='pool', extra=()):
                s = next_slot()
                o = P.op(eng, lambda e, s=s: e.dma_start(out=view_fn(s), in_=src), W=[('ring', s)], dma=True, extra=extra)
                return s

            def u16(s):
                return ring[:, s, :].rearrange("p (k c) -> p k c", k=16)

            def p16(s):
                return ring[:, s:s + 2, :].rearrange("p s e -> p (s e)").rearrange("p (k c) -> p k c", k=16)

            def load_pair(src, extra=()):
                if rs['n'] % 2:
                    rs['n'] += 1
                s0 = next_slot()
                s1 = next_slot()
                assert s1 == s0 + 1 and s0 % 2 == 0
                P.op('pool', lambda e: e.dma_start(out=p16(s0), in_=src), W=[('ring', s0), ('ring', s1)], dma=True, extra=extra)
                return s0

            def norm_T(b, src, srckeys, gT, sc, reg=None, extra=(), defer=None):
                rows = rows_of(b)
                q = b % 2
                c0 = b * 128
                P.op('act', lambda e: e.activation(out=xnb[q][:rows], in_=src, func=AF.Square, accum_out=stat[:rows, sc:sc + 1]),
                     R=srckeys, W=[('xnb', q), ('st', sc)], reg=reg, extra=extra)
                P.op('act', lambda e: e.activation(out=stat[:rows, sc + 10:sc + 11], in_=stat[:rows, sc:sc + 1], func=AF.Sqrt,
                                                   bias=epst[:rows], scale=1.0 / D),
                     R=[('st', sc), 'eps'], W=[('st', sc + 10)])
                P.op('dve', lambda e: e.reciprocal(out=stat[:rows, sc + 20:sc + 21], in_=stat[:rows, sc + 10:sc + 11]),
                     R=[('st', sc + 10)], W=[('st', sc + 20)])
                P.op('dve', lambda e: e.tensor_scalar(out=xnb[q][:rows], in0=src, scalar1=stat[:rows, sc + 20:sc + 21], scalar2=None, op0=ALU.mult),
                     R=srckeys + [('st', sc + 20)], W=[('xnb', q)], reg=reg)
                def back():
                  for h in range(2):
                    bank = 2 * q + h

                    def tp(e, h=h, bank=bank):
                        for j in range(8):
                            k = 8 * h + j
                            ins = e.transpose(out=psb[:, bank, j * 128:j * 128 + rows], in_=xnb[q][:rows, k * 128:(k + 1) * 128],
                                              identity=identb[:rows, :rows])
                        return ins
                    P.op('pe', tp, R=[('xnb', q), 'identb'], W=[('bank', bank)], reg=reg)
                    P.op('dve', lambda e, h=h, bank=bank: e.tensor_tensor(
                        out=acts[:, 8 * h:8 * h + 8, c0:c0 + rows],
                        in0=psb[:, bank, :].rearrange("p (j t) -> p j t", j=8)[:, :, :rows],
                        in1=gT[:, 8 * h:8 * h + 8].unsqueeze(2).to_broadcast([128, 8, rows]), op=ALU.mult),
                        R=[('bank', bank), 'cst'], W=[('acts', k, b) for k in range(8 * h, 8 * h + 8)], reg=reg)
                if defer is None:
                    back()
                else:
                    defer.append(back)

            xa_ops = {}
            for b in range(10):
                rows = rows_of(b)
                src = xh[:, :] if b == 0 else (xs[:, :] if b == 9 else xp[(b - 1) * 128:b * 128, :])
                q = b % 2
                q4 = b % 4
                xa_ops[b] = P.op('sp', lambda e, src=src, q4=q4, rows=rows: e.dma_start(out=xa[q4][:rows], in_=src), W=[('xa', q4)], dma=True, reg=['scr', 'flex', 'flexU'])
                norm_T(b, xa[q4][:rows], [('xa', q4)], gmixT, b, reg=['scr', 'flex', 'flexU'])

            ALLACT = [('acts', k, b) for k in range(16) for b in range(10)]
            chk('A')

            TT1 = [(112, 368), (480, 368), (848, 368)]
            for j in range(4):
                s = 4 + j
                P.op('pool', lambda e, s=s, j=j: e.dma_start(out=u16(s), in_=w_in[:, 256 * j:256 * j + 256].rearrange("(k p) c -> p k c", p=128)), W=[('ring', s)], dma=True, extra=([xa_ops[8]] if j == 0 else ()))
                for cc in range(2):
                    c = 2 * j + cc
                    b0 = 3 * (c % 2)
                    for tt, (t0, n) in enumerate(TT1):
                        def mm(e, s=s, cc=cc, tt=tt, t0=t0, n=n, b0=b0):
                            for k in range(16):
                                ins = e.matmul(ps[:, b0 + tt, :n], lhsT=u16(s)[:, k, cc * 128:(cc + 1) * 128], rhs=acts[:, k, t0:t0 + n],
                                               start=(k == 0), stop=(k == 15))
                            return ins
                        P.op('pe', mm, R=[('ring', s)] + [('acts', k, bb) for k in range(16) for bb in range(t0 // 128, (t0 + n - 1) // 128 + 1)], W=[('bank', b0 + tt)])
                        eng = 'act' if tt % 2 == 0 else 'dve'
                        if eng == 'act':
                            P.op('act', lambda e, c=c, tt=tt, t0=t0, n=n, b0=b0: e.copy(out=uT[:, c, t0 - 112:t0 - 112 + n], in_=ps[:, b0 + tt, :n]),
                                 R=[('bank', b0 + tt)], W=[('uT', c, tt)], reg=['flex', 'flexU'])
                        else:
                            P.op('dve', lambda e, c=c, tt=tt, t0=t0, n=n, b0=b0: e.tensor_copy(out=uT[:, c, t0 - 112:t0 - 112 + n], in_=ps[:, b0 + tt, :n]),
                                 R=[('bank', b0 + tt)], W=[('uT', c, tt)], reg=['flex', 'flexU'])
            UT = lambda c: [('uT', c, 0), ('uT', c, 1), ('uT', c, 2)]
            chk('B1')

            pool_work = []
            chk('B2o')
            s_sph = next_slot()
            sph = ringf[:, s_sph, :].rearrange("p (a c) -> p a c", a=2)
            P.op('sp', lambda e: e.dma_start(out=sph[:120], in_=spd.rearrange("(a s) h c -> (s h) a c", a=2)), W=[('ring', s_sph)], dma=True)
            for a in range(2):
                for g in range(2):
                    bank = 2 * a + g

                    def tp(e, a=a, g=g, bank=bank):
                        for cq in range(4):
                            c = 4 * g + cq
                            ins = e.transpose(out=ps[:, bank, cq * 120:(cq + 1) * 120], in_=sph[:120, a, c * 128:(c + 1) * 128], identity=identf[:120, :120])
                        return ins
                    P.op('pe', tp, R=[('ring', s_sph), 'identf'], W=[('bank', bank)])
                    P.op('act', lambda e, a=a, g=g, bank=bank: e.copy(
                        out=usT[:, 4 * g:4 * g + 4, 8 * a:8 * a + 8, 0:15],
                        in_=ps[:, bank, 0:480].rearrange("p (c s h) -> p c s h", c=4, s=8)),
                        R=[('bank', bank)], W=[('usT', 4 * g + cq, a) for cq in range(4)], reg='flex')
            P.op('dve', lambda e: e.tensor_copy(out=usT[:, :, :, 15:19], in_=uT[:, :, 1040:1104].rearrange("p c (t s) -> p c s t", t=4)),
                 R=[k for c in range(8) for k in UT(c)], W=[('usT', c, 2) for c in range(8)], reg=['flex', 'flexU'])
            USK = lambda c: [('usT', c, 0), ('usT', c, 1), ('usT', c, 2)]

            def tp_po(e):
                for c in range(8):
                    ins = e.transpose(out=ps[:16, 4 + c // 4, (c % 4) * 128:(c % 4 + 1) * 128], in_=uT[:, c, 1024:1040], identity=identf[:, :])
                return ins
            P.op('pe', tp_po, R=[k for c in range(8) for k in UT(c)] + ['identf'], W=[('bank', 4), ('bank', 5)], reg='flexU')
            P.op('act', lambda e: e.copy(out=ostg[:16, :], in_=ps[:16, 4:6, :].rearrange("p a c -> p (a c)")),
                 R=[('bank', 4), ('bank', 5)], W=['ostg'], reg='scr')
            P.op('sp', lambda e: e.dma_start(out=pool_p[:, :], in_=ostg[1:16, :]), R=['ostg'], W=['o_pp'], dma=True, reg='scr')

            def tp_pos(e):
                for c in range(8):
                    ins = e.transpose(out=ps[:64, 6 + c // 4, (c % 4) * 128:(c % 4 + 1) * 128], in_=uT[:, c, 1040:1104], identity=identf[:, :])
                return ins
            P.op('pe', tp_pos, R=[k for c in range(8) for k in UT(c)] + ['identf'], W=[('bank', 6), ('bank', 7)], reg='flexU')
            P.op('act', lambda e: e.copy(out=ostg[:64, :], in_=ps[:64, 6:8, :].rearrange("p a c -> p (a c)")),
                 R=[('bank', 6), ('bank', 7)], W=['ostg'], reg='scr')
            for t in range(4):
                P.op('sp', lambda e, t=t: e.dma_start(out=pool_s[:, 11 + t, :], in_=ostg[16 * t:16 * t + 16, :]), R=['ostg'], W=[('o_ps', t)], dma=True, reg='scr')

            L = 1040
            for c in range(8):
                gi = c // 2
                nlev = gi + 1
                w = 2 ** nlev
                ev = uT[:, c, 0:L]
                cur, curk = ev, None
                bufs = [(wsA, 'wsA'), (wsB, 'wsB')]
                for lev in range(nlev):
                    sh = 2 ** lev
                    lo = 2 * sh - 1
                    dst, dk = bufs[lev % 2]
                    pool_work.append((lambda cur=cur, dst=(dst if 'dst' in dir() else None), c=(c if 'c' in dir() else None), gi=(gi if 'gi' in dir() else None), w=(w if 'w' in dir() else None), sh=(sh if 'sh' in dir() else None), lo=(lo if 'lo' in dir() else None), curk=curk, dk=(dk if 'dk' in dir() else None), tmpb=(tmpb if 'tmpb' in dir() else None), tmpk=(tmpk if 'tmpk' in dir() else None), ci=(ci if 'ci' in dir() else None):
                        P.op('dve', lambda e, cur=cur, dst=dst, sh=sh, lo=lo: e.tensor_tensor(out=dst[:, lo:L], in0=cur[:, lo:L], in1=cur[:, lo - sh:L - sh], op=ALU.add),
                             R=(UT(c) if curk is None else [curk]), W=[dk], reg=['flex', 'flexU'])))
                    cur, curk = dst, dk
                pool_work.append((lambda cur=cur, dst=(dst if 'dst' in dir() else None), c=(c if 'c' in dir() else None), gi=(gi if 'gi' in dir() else None), w=(w if 'w' in dir() else None), sh=(sh if 'sh' in dir() else None), lo=(lo if 'lo' in dir() else None), curk=curk, dk=(dk if 'dk' in dir() else None), tmpb=(tmpb if 'tmpb' in dir() else None), tmpk=(tmpk if 'tmpk' in dir() else None), ci=(ci if 'ci' in dir() else None):
                    P.op('dve', lambda e, cur=cur, c=c, w=w: e.scalar_tensor_tensor(out=dT[:, c, 16:1024], in0=cur[:, 32:L], scalar=1.0 / w, in1=uT[:, c, 32:L],
                                                                                    op0=ALU.mult, op1=ALU.subtract),
                         R=[curk] + UT(c), W=[('dT', c, 1)], reg=['flex', 'flexU'])))
                tmpk = 'wsB' if curk == 'wsA' else 'wsA'
                tmpb = wsB if curk == 'wsA' else wsA
                pool_work.append((lambda cur=cur, dst=(dst if 'dst' in dir() else None), c=(c if 'c' in dir() else None), gi=(gi if 'gi' in dir() else None), w=(w if 'w' in dir() else None), sh=(sh if 'sh' in dir() else None), lo=(lo if 'lo' in dir() else None), curk=curk, dk=(dk if 'dk' in dir() else None), tmpb=(tmpb if 'tmpb' in dir() else None), tmpk=(tmpk if 'tmpk' in dir() else None), ci=(ci if 'ci' in dir() else None):
                    P.op('dve', lambda e, cur=cur, c=c, tmpb=tmpb: e.tensor_tensor(out=tmpb[:, 0:16], in0=cur[:, 16:32], in1=invc[:, c, :], op=ALU.mult),
                         R=[curk, 'cst'], W=[tmpk], reg=['flex', 'flexU'])))
                pool_work.append((lambda cur=cur, dst=(dst if 'dst' in dir() else None), c=(c if 'c' in dir() else None), gi=(gi if 'gi' in dir() else None), w=(w if 'w' in dir() else None), sh=(sh if 'sh' in dir() else None), lo=(lo if 'lo' in dir() else None), curk=curk, dk=(dk if 'dk' in dir() else None), tmpb=(tmpb if 'tmpb' in dir() else None), tmpk=(tmpk if 'tmpk' in dir() else None), ci=(ci if 'ci' in dir() else None):
                    P.op('dve', lambda e, c=c, tmpb=tmpb: e.tensor_tensor(out=dT[:, c, 0:16], in0=tmpb[:, 0:16], in1=uT[:, c, 16:32], op=ALU.subtract),
                         R=[tmpk] + UT(c), W=[('dT', c, 0)], reg=['flex', 'flexU'])))
            for gi in range(4):
                nlev = gi + 1
                w = 2 ** nlev
                ev = usT[:, 2 * gi:2 * gi + 2, :, :]
                vA = wsA[:, 0:608].rearrange("p (c s h) -> p c s h", c=2, s=16)
                vB = wsB[:, 0:608].rearrange("p (c s h) -> p c s h", c=2, s=16)
                bufs = [(vA, 'wsA'), (vB, 'wsB')]
                cur, curk = ev, None
                for lev in range(nlev):
                    sh = 2 ** lev
                    lo = 2 * sh - 1
                    dst, dk = bufs[lev % 2]
                    pool_work.append((lambda cur=cur, dst=(dst if 'dst' in dir() else None), c=(c if 'c' in dir() else None), gi=(gi if 'gi' in dir() else None), w=(w if 'w' in dir() else None), sh=(sh if 'sh' in dir() else None), lo=(lo if 'lo' in dir() else None), curk=curk, dk=(dk if 'dk' in dir() else None), tmpb=(tmpb if 'tmpb' in dir() else None), tmpk=(tmpk if 'tmpk' in dir() else None), ci=(ci if 'ci' in dir() else None):
                        P.op('dve', lambda e, cur=cur, dst=dst, sh=sh, lo=lo: e.tensor_tensor(out=dst[:, :, :, lo:19], in0=cur[:, :, :, lo:19], in1=cur[:, :, :, lo - sh:19 - sh], op=ALU.add),
                             R=(USK(2 * gi) + USK(2 * gi + 1) if curk is None else [curk]), W=[dk], reg=['flex', 'flexU'])))
                    cur, curk = dst, dk
                for ci in range(2):
                    pool_work.append((lambda cur=cur, dst=(dst if 'dst' in dir() else None), c=(c if 'c' in dir() else None), gi=(gi if 'gi' in dir() else None), w=(w if 'w' in dir() else None), sh=(sh if 'sh' in dir() else None), lo=(lo if 'lo' in dir() else None), curk=curk, dk=(dk if 'dk' in dir() else None), tmpb=(tmpb if 'tmpb' in dir() else None), tmpk=(tmpk if 'tmpk' in dir() else None), ci=(ci if 'ci' in dir() else None):
                        P.op('dve', lambda e, cur=cur, gi=gi, w=w, ci=ci: e.scalar_tensor_tensor(
                            out=dT[:, 2 * gi + ci, 1024:1088].rearrange("p (t s) -> p s t", t=4),
                            in0=cur[:, ci, :, 15:19], scalar=1.0 / w, in1=usT[:, 2 * gi + ci, :, 15:19], op0=ALU.mult, op1=ALU.subtract),
                            R=[curk] + USK(2 * gi + ci), W=[('dT', 2 * gi + ci, 2)], reg=['flex', 'flexU'])))
            def rope_ops(q, b, rows, nh):
                x = qf[q][:rows, 0:nh * 64].rearrange("p (h d) -> p h d", d=64)
                cc_ = rope[:rows, b, 0:16].unsqueeze(1).to_broadcast([rows, nh, 16])
                ss_ = rope[:rows, b, 16:32].unsqueeze(1).to_broadcast([rows, nh, 16])
                A = rtA[:rows, 0:nh, :]
                B = rtB[:rows, 0:nh, :]
                P.op('dve', lambda e: e.tensor_tensor(out=A, in0=x[:, :, 0:16], in1=cc_, op=ALU.mult), R=[('qf', q), 'cst'], W=['rtA'], reg='scr')
                P.op('dve', lambda e: e.tensor_tensor(out=B, in0=x[:, :, 0:16], in1=ss_, op=ALU.mult), R=[('qf', q), 'cst'], W=['rtB'], reg='scr')
                P.op('dve', lambda e: e.tensor_tensor(out=x[:, :, 0:8], in0=A[:, :, 0:8], in1=B[:, :, 8:16], op=ALU.subtract),
                     R=['rtA', 'rtB'], W=[('qf', q)], reg='scr')
                P.op('dve', lambda e: e.tensor_tensor(out=x[:, :, 8:16], in0=A[:, :, 8:16], in1=B[:, :, 0:8], op=ALU.add),
                     R=['rtA', 'rtB'], W=[('qf', q)], reg='scr')

            step = 0
            pend_b2 = []
            for pair in range(3):
                c0 = 2048 if pair == 0 else 1024 + 512 * (pair - 1)
                s0 = load_pair(w_in[:, c0:c0 + 512].rearrange("(k p) c -> p k c", p=128))
                for b in range(0 if pair == 0 else 1, 10):
                    rows = rows_of(b)
                    bank = step % 4
                    tb = 4 + step % 2
                    q = step % 2
                    step += 1

                    def mm(e, s0=s0, b=b, rows=rows, bank=bank):
                        for k in range(16):
                            ins = e.matmul(ps[:rows, bank, :], lhsT=acts[:, k, b * 128:b * 128 + rows],
                                           rhs=p16(s0)[:, k, :], start=(k == 0), stop=(k == 15))
                        return ins
                    P.op('pe', mm, R=[('ring', s0), ('ring', s0 + 1)] + [('acts', k, b) for k in range(16)], W=[('bank', bank)])
                    while pend_b2:
                        pend_b2.pop(0)()
                    if pair == 0:
                        P.op('act', lambda e, q=q, rows=rows, bank=bank: e.copy(out=qf[q][:rows, 0:256], in_=ps[:rows, bank, 0:256]),
                             R=[('bank', bank)], W=[('qf', q)], reg='scr')
                        P.op('act', lambda e, b=b, rows=rows, bank=bank: e.copy(out=Vb[:rows, b, :], in_=ps[:rows, bank, 256:512]),
                             R=[('bank', bank)], W=[('V', b)], reg='ringhi')
                        if b >= 8:
                            P.op('act', lambda e, b=b, rows=rows, bank=bank: e.copy(out=kvf[:rows, 2 + b - 8, :], in_=ps[:rows, bank, 256:512]),
                                 R=[('bank', bank)], W=[('vf', b)], reg='ringhi')
                        rope_ops(q, b, rows, 4)
                        for _ in range(3):
                            if pool_work:
                                pool_work.pop(0)()
                        if b >= 8:
                            P.op('act', lambda e, b=b, rows=rows, q=q: e.copy(out=kvf[:rows, b - 8, :], in_=qf[q][:rows, 0:256]),
                                 R=[('qf', q)], W=[('kf', b)], reg='ringhi')
                        P.op('act', lambda e, q=q, rows=rows: e.copy(out=qb[q][:rows, 0:256], in_=qf[q][:rows, 0:256]),
                             R=[('qf', q)], W=[('qb', q)], reg='scr')
                        nchunk = 2
                    else:
                        P.op('act', lambda e, q=q, rows=rows, bank=bank: e.copy(out=qf[q][:rows, :], in_=ps[:rows, bank, :]),
                             R=[('bank', bank)], W=[('qf', q)], reg='scr')
                        rope_ops(q, b, rows, 8)
                        for _ in range(3):
                            if pool_work:
                                pool_work.pop(0)()
                        P.op('act', lambda e, q=q, rows=rows: e.activation(out=qb[q][:rows, :], in_=qf[q][:rows, :], func=AF.Copy, scale=0.125),
                             R=[('qf', q)], W=[('qb', q)], reg='scr')
                        nchunk = 4

                    def finish(q=q, rows=rows, tb=tb, nchunk=nchunk, pair=pair, b=b):
                        def tp(e):
                            for j in range(nchunk):
                                ins = e.transpose(out=psb[:, tb, j * 128:j * 128 + rows], in_=qb[q][:rows, j * 128:(j + 1) * 128],
                                                  identity=identb[:rows, :rows])
                            return ins
                        P.op('pe', tp, R=[('qb', q), 'identb'], W=[('bank', tb)], reg='scr')
                        srcv = psb[:, tb, :].rearrange("p (j t) -> p j t", t=128)
                        if pair == 0:
                            P.op('dve', lambda e: e.tensor_copy(out=KT[:, :, b * 128:b * 128 + rows], in_=srcv[:, 0:2, :rows]),
                                 R=[('bank', tb)], W=[('KT', 0, b), ('KT', 1, b)], reg='ringhi')
                        else:
                            i0 = 4 * (pair - 1)
                            qc = (b - 1) * 128
                            P.op('dve', lambda e: e.tensor_copy(out=QT[:, i0:i0 + 4, qc:qc + rows], in_=srcv[:, 0:4, :rows]),
                                 R=[('bank', tb)], W=[('QT', i0 + ii, b) for ii in range(4)], reg='ringhi')
                    pend_b2.append(finish)

            while pend_b2:
                pend_b2.pop(0)()
            while pool_work:
                pool_work.pop(0)()
            chk('B2')
            P.op('sp', lambda e: e.dma_start(out=k_p[:, :], in_=kvf[:, 0, :]), R=[('kf', 8)], W=['o_kp'], dma=True, reg='ringhi')
            P.op('sp', lambda e: e.dma_start(out=v_p[:, :], in_=kvf[:, 2, :]), R=[('vf', 8)], W=['o_vp'], dma=True, reg='ringhi')
            for t in range(4):
                P.op('sp', lambda e, t=t: e.dma_start(out=k_s[:, 124 + t, :], in_=kvf[16 * t:16 * t + 16, 1, :]), R=[('kf', 9)], W=[('o_ks', t)], dma=True, reg='ringhi')
                P.op('sp', lambda e, t=t: e.dma_start(out=v_s[:, 124 + t, :], in_=kvf[16 * t:16 * t + 16, 3, :]), R=[('vf', 9)], W=[('o_vs', t)], dma=True, reg='ringhi')
            P.op('sp', lambda e: e.dma_start(out=k_s[:, 0:124, :], in_=ckd[:, 4:128, :]), W=['o_ks_c'], dma=True)
            P.op('sp', lambda e: e.dma_start(out=v_s[:, 0:124, :], in_=cvd[:, 4:128, :]), W=['o_vs_c'], dma=True)
            P.op('sp', lambda e: e.dma_start(out=pool_s[:, 0:11, :], in_=spd[:, 4:15, :]), W=['o_ps_c'], dma=True)

            s_wp = next_slot()
            wpv = ring[:, s_wp, 0:2048].rearrange("p (g d) -> p g d", g=8)
            P.op('pool', lambda e: e.dma_start(out=wpv, in_=w_pool.rearrange("g (kk p) d -> p (g kk) d", p=128)), W=[('ring', s_wp)], dma=True)
            TTP = [(0, 512, [1, 2, 3, 4]), (512, 512, [5, 6, 7, 8]), (1024, 64, [9])]
            pstep = 0
            for gi in range(4):
                for oc in range(2):
                    c = 2 * gi + oc
                    for (t0, n, blks) in TTP:
                        bank = pstep % 4
                        pstep += 1

                        def mm(e, gi=gi, oc=oc, t0=t0, n=n, bank=bank):
                            for kk in range(2):
                                ins = e.matmul(ps[:, bank, :n], lhsT=wpv[:, 2 * gi + kk, oc * 128:(oc + 1) * 128], rhs=dT[:, 2 * gi + kk, t0:t0 + n],
                                               start=(kk == 0), stop=(kk == 1))
                            return ins
                        P.op('pe', mm, R=[('ring', s_wp)] + [('dT', 2 * gi + kk, z) for kk in range(2) for z in range(3)], W=[('bank', bank)], reg='flex')
                        if pstep % 2 == 0:
                            P.op('act', lambda e, c=c, t0=t0, n=n, bank=bank: e.activation(out=acts[:, c, 128 + t0:128 + t0 + n], in_=ps[:, bank, :n], func=AF.Copy,
                                                                                          scale=pscT[:, c:c + 1]),
                                 R=[('bank', bank), 'cst'], W=[('acts', c, bb) for bb in blks])
                        else:
                            P.op('dve', lambda e, c=c, t0=t0, n=n, bank=bank: e.tensor_tensor(out=acts[:, c, 128 + t0:128 + t0 + n], in0=ps[:, bank, :n],
                                                                                             in1=pscT[:, c:c + 1].to_broadcast([128, n]), op=ALU.mult),
                                 R=[('bank', bank), 'cst'], W=[('acts', c, bb) for bb in blks])

            chk('POOL')

            CUR = {'sq': 0}

            def sm_of(par, rows):
                return small[:rows, CUR['sq'], :]

            def st_s1dve(par, rows, nk, maskv, hs, sbank):
                Sps = ps[:rows, sbank:sbank + 2, :].rearrange("p a (i k) -> p (a i) k", i=2)[:, :, :nk]
                Ss = S_sb[par][:rows, :, :nk]
                sm = sm_of(par, rows)
                nsinkh = nsinkv[:rows, hs]
                kS, km = ('Ssb', par), ('sm', CUR['sq'])
                P.op('dve', lambda e: e.tensor_tensor(out=Ss, in0=Sps, in1=maskv[:rows, :nk].unsqueeze(1).to_broadcast([rows, 4, nk]), op=ALU.add),
                     R=[('bank', sbank), ('bank', sbank + 1), 'cst'], W=[kS], reg='scr')
                P.op('dve', lambda e: e.tensor_reduce(out=sm[:, 0:4], in_=Ss, axis=AX.X, op=ALU.max), R=[kS], W=[(km, 0)])
                P.op('dve', lambda e: e.scalar_tensor_tensor(out=sm[:, 4:8], in0=sm[:, 0:4], scalar=-1.0, in1=nsinkh, op0=ALU.mult, op1=ALU.min),
                     R=[(km, 0), 'cst'], W=[(km, 1)])

            def st_s2a(par, rows, nk, hs, sbank):
                Ss = S_sb[par][:rows, :, :nk]
                Pv = Pb[par][:rows, :, :nk]
                sm = sm_of(par, rows)
                sinkh = sinkv[:rows, hs]
                kP, km = ('Pb', par), ('sm', CUR['sq'])

                def ex(e):
                    for ii in range(4):
                        ins = e.activation(out=Pv[:, ii, :], in_=Ss[:, ii, :], func=AF.Exp, bias=sm[:, 4 + ii:5 + ii], scale=1.0, accum_out=sm[:, 8 + ii:9 + ii])
                    return ins
                P.op('act', ex, R=[('Ssb', par), (km, 1)], W=[kP, (km, 2)], reg='scr')
                P.op('pool', lambda e: e.tensor_tensor(out=sm[:, 12:16], in0=sinkh, in1=sm[:, 4:8], op=ALU.add), R=[(km, 1), 'cst'], W=[(km, 3)])
                P.op('act', lambda e: e.activation(out=sm[:, 16:20], in_=sm[:, 12:16], func=AF.Exp), R=[(km, 3)], W=[(km, 4)])

            def st_s2b(par, rows, nk):
                Pv = Pb[par][:rows, :, :nk]
                sm = sm_of(par, rows)
                kP, km = ('Pb', par), ('sm', CUR['sq'])
                P.op('pool', lambda e: e.tensor_tensor(out=sm[:, 20:24], in0=sm[:, 8:12], in1=sm[:, 16:20], op=ALU.add), R=[(km, 2), (km, 4)], W=[(km, 5)])
                P.op('dve', lambda e: e.reciprocal(out=sm[:, 24:28], in_=sm[:, 20:24]), R=[(km, 5)], W=[(km, 6)])

            def make_prompt_item(idx, b, kc, j):
                par = idx % 2
                sbank, ptb = 2 * par, 4 + par
                bp = b % 2
                qc = (b - 1) * 128
                maskv = mask_f if b == 1 else mask_n
                kvh = 2 * kc + j
                hs = slice(8 * kc + j, 8 * kc + 8, 2)
                it = {}

                def s1pe():
                    def qk(e):
                        for ii in range(4):
                            i = 4 * kc + ii
                            ins = e.matmul(ps[:, sbank + ii // 2, (ii % 2) * 256:(ii % 2) * 256 + 256],
                                     lhsT=QT[64 * j:64 * j + 64, i, qc:qc + 128], rhs=KT[64 * j:64 * j + 64, kc, (b - 1) * 128:(b + 1) * 128],
                                     start=True, stop=True)
                        return ins
                    P.op('pe', qk, R=[('QT', 4 * kc + ii, b) for ii in range(4)] + [('KT', kc, b - 1), ('KT', kc, b)], W=[('bank', sbank), ('bank', sbank + 1)], reg='ringhi')
                it['s1pe'] = s1pe
                it['s1dve'] = lambda: st_s1dve(par, 128, 256, maskv, hs, sbank)
                it['s2a'] = lambda: st_s2a(par, 128, 256, hs, sbank)
                it['s2b'] = lambda: st_s2b(par, 128, 256)

                def s3():
                    def tpp(e):
                        for ii in range(4):
                            for kb in range(2):
                                ins = e.transpose(out=psb[:, ptb, (2 * ii + kb) * 128:(2 * ii + kb + 1) * 128], in_=Pb[par][:, ii, kb * 128:(kb + 1) * 128], identity=identb[:, :])
                        return ins
                    P.op('pe', tpp, R=[('Pb', par), 'identb'], W=[('bank', ptb)], reg='scr')
                    P.op('act', lambda e: e.copy(out=PTs[par][:, :], in_=psb[:, ptb, :]), R=[('bank', ptb)], W=[('PTs', par)], reg='scr')
                it['s3'] = s3

                def s4():
                    def pv(e):
                        for ii in range(4):
                            for kb in range(2):
                                ins = e.matmul(ps[:, 6, par * 256 + ii * 64:par * 256 + (ii + 1) * 64],
                                               lhsT=PTs[par][:, (2 * ii + kb) * 128:(2 * ii + kb + 1) * 128], rhs=Vb[:, b - 1 + kb, kvh * 64:(kvh + 1) * 64],
                                               start=(kb == 0), stop=(kb == 1))
                        return ins
                    P.op('pe', pv, R=[('PTs', par), ('V', b - 1), ('V', b)], W=[('bank', 6)], reg=['scr', 'ringhi'])
                    sq_ = idx % 4
                    P.op('dve', lambda e: e.tensor_tensor(
                        out=atm[bp][:, :].rearrange("p (i jj d) -> p i jj d", i=8, jj=2)[:, 4 * kc:4 * kc + 4, j, :],
                        in0=ps[:, 6, par * 256:(par + 1) * 256].rearrange("p (i d) -> p i d", i=4),
                        in1=small[:, sq_, 24:28].unsqueeze(2).to_broadcast([128, 4, 64]), op=ALU.mult),
                        R=[('bank', 6), (('sm', sq_), 6)], W=[('atm', bp, kc, j)], reg='scr')
                    if kc == 1 and j == 1:
                        def tpa(e):
                            for i in range(8):
                                ins = e.transpose(out=psb[:, 7, i * 128:(i + 1) * 128], in_=atm[bp][:, i * 128:(i + 1) * 128], identity=identb[:, :])
                            return ins
                        P.op('pe', tpa, R=[('atm', bp, kk, jj) for kk in range(2) for jj in range(2)] + ['identb'], W=[('bank', 7)], reg='scr')
                        P.op('dve', lambda e: e.tensor_copy(out=acts[:, 8:16, b * 128:(b + 1) * 128], in_=psb[:, 7, :].rearrange("p (i t) -> p i t", i=8)),
                             R=[('bank', 7)], W=[('acts', 8 + i, b) for i in range(8)])
                it['s4'] = s4
                return it

            def make_sample_item(idx, kc, j):
                par = idx % 2
                sbank, ptb = 2 * par, 4 + par
                bp = 1
                kvh = 2 * kc + j
                hs = slice(8 * kc + j, 8 * kc + 8, 2)
                it = {}

                def s1pe():
                    def qk(e):
                        for ii in range(4):
                            i = 4 * kc + ii
                            o0 = (ii % 2) * 256
                            for s in range(16):
                                e.matmul(ps[:64, sbank + ii // 2, o0:o0 + 128], lhsT=QTm[64 * j:64 * j + 64, i, s, :], rhs=KcT[64 * j:64 * j + 64, kc, s, :],
                                         start=(s == 0), stop=(s == 15))
                            ins = e.matmul(ps[:64, sbank + ii // 2, o0 + 128:o0 + 192], lhsT=QT[64 * j:64 * j + 64, i, 1024:1088], rhs=KT[64 * j:64 * j + 64, kc, 1152:1216],
                                           start=True, stop=True)
                        return ins
                    P.op('pe', qk, R=[('QTm', 4 * kc + ii) for ii in range(4)] + [('QT', 4 * kc + ii, 9) for ii in range(4)] + [('KcT', kc, 0), ('KcT', kc, 1), ('KT', kc, 9)],
                         W=[('bank', sbank), ('bank', sbank + 1)], reg=['flex', 'ringhi'])
                it['s1pe'] = s1pe
                it['s1dve'] = lambda: st_s1dve(par, 64, 192, mask_s, hs, sbank)
                it['s2a'] = lambda: st_s2a(par, 64, 192, hs, sbank)
                it['s2b'] = lambda: st_s2b(par, 64, 192)

                def s3():
                    def tpp(e):
                        for ii in range(4):
                            e.transpose(out=psb[:, ptb, ii * 64:(ii + 1) * 64], in_=Pb[par][:64, ii, 0:128], identity=identb[:64, :64])
                            ins = e.transpose(out=psb[:64, ptb, 256 + ii * 64:256 + (ii + 1) * 64], in_=Pb[par][:64, ii, 128:192], identity=identb[:64, :64])
                        return ins
                    P.op('pe', tpp, R=[('Pb', par), 'identb'], W=[('bank', ptb)], reg='scr')
                    P.op('act', lambda e: e.copy(out=PTs[par][:, 0:512], in_=psb[:, ptb, 0:512]), R=[('bank', ptb)], W=[('PTs', par)], reg='scr')
                    P.op('dve', lambda e: e.tensor_tensor(
                        out=PTm[par][:, :, :, :], in0=PTs[par][:, 0:256].rearrange("p (i r) -> p i r", i=4).unsqueeze(2).to_broadcast([128, 4, 16, 64]),
                        in1=bm[:, :, :].unsqueeze(1).to_broadcast([128, 4, 16, 64]), op=ALU.mult),
                        R=[('PTs', par), 'bm'], W=[('PTm', par)], reg=['flex', 'scr'])
                it['s3'] = s3

                def s4():
                    def pv(e):
                        for ii in range(4):
                            oo = par * 256 + ii * 64
                            for s in range(16):
                                e.matmul(ps[:64, 6, oo:oo + 64], lhsT=PTm[par][:, ii, s, :], rhs=Vc[:, s, kvh * 64:(kvh + 1) * 64], start=(s == 0), stop=False)
                            ins = e.matmul(ps[:64, 6, oo:oo + 64], lhsT=PTs[par][:64, 256 + ii * 64:256 + (ii + 1) * 64], rhs=Vb[:64, 9, kvh * 64:(kvh + 1) * 64],
                                           start=False, stop=True)
                        return ins
                    P.op('pe', pv, R=[('PTm', par), ('PTs', par), 'Vc', ('V', 9)], W=[('bank', 6)], reg=['flex', 'scr', 'ringhi'])
                    sq_ = idx % 4
                    P.op('dve', lambda e: e.tensor_tensor(
                        out=atm_s[:64, :].rearrange("p (i jj d) -> p i jj d", i=8, jj=2)[:, 4 * kc:4 * kc + 4, j, :],
                        in0=ps[:64, 6, par * 256:(par + 1) * 256].rearrange("p (i d) -> p i d", i=4),
                        in1=small[:64, sq_, 24:28].unsqueeze(2).to_broadcast([64, 4, 64]), op=ALU.mult),
                        R=[('bank', 6), (('sm', sq_), 6)], W=[('atm', 's', kc, j)], reg='scr', extra=P.fence('scr'))
                    if kc == 1 and j == 1:
                        def tpa9(e):
                            for i in range(8):
                                ins = e.transpose(out=psb[:, 7, i * 128:i * 128 + 64], in_=atm_s[:64, i * 128:(i + 1) * 128], identity=identb[:64, :64])
                            return ins
                        P.op('pe', tpa9, R=[('atm', 's', kk, jj) for kk in range(2) for jj in range(2)] + ['identb'], W=[('bank', 7)], reg='scr')
                        P.op('dve', lambda e: e.tensor_copy(out=acts[:, 8:16, 1152:1216], in_=psb[:, 7, :].rearrange("p (i t) -> p i t", i=8)[:, :, 0:64]),
                             R=[('bank', 7)], W=[('acts', 8 + i, 9) for i in range(8)], reg='flex')
                it['s4'] = s4
                return it

            fl = P.fence('flex')
            flU = P.fence('flexU')
            P.op('pool', lambda e: e.dma_start(out=ckb[:, :, :], in_=ckd.rearrange("s j c -> j s c")), W=['ckb'], dma=True, extra=flU, reg='flex')
            P.op('pool', lambda e: e.dma_start(out=Vc[:, :, :], in_=cvd.rearrange("s j c -> j s c")), W=['Vc'], dma=True, extra=flU, reg='flex')
            for kc in range(2):
                for hf in range(2):
                    bank = 2 * kc + hf

                    def tpk(e, kc=kc, hf=hf, bank=bank):
                        for sl in range(8):
                            s = 8 * hf + sl
                            ins = e.transpose(out=psb[:, bank, sl * 128:(sl + 1) * 128], in_=ckb[:, s, kc * 128:(kc + 1) * 128], identity=identb[:, :])
                        return ins
                    P.op('pe', tpk, R=['ckb', 'identb'], W=[('bank', bank)], reg='flex')
                    P.op('dve', lambda e, kc=kc, hf=hf, bank=bank: e.tensor_copy(out=KcT[:, kc, 8 * hf:8 * hf + 8, :], in_=psb[:, bank, :].rearrange("p (s j) -> p s j", s=8)),
                         R=[('bank', bank)], W=[('KcT', kc, hf)], extra=flU, reg='flex')
            for i in range(8):
                P.op('dve', lambda e, i=i: e.tensor_tensor(out=QTm[:, i, :, :], in0=QT[:, i, 1024:1088].unsqueeze(1).to_broadcast([128, 16, 64]), in1=bm[:, :, :], op=ALU.mult),
                     R=[('QT', i, 9), 'bm'], W=[('QTm', i)], extra=fl, reg=['flex', 'ringhi'])

            rs['mod'] = 8
            rs['n'] = 0
            wout_s = [load_pair(w_out[:, 512 * m:512 * m + 512].rearrange("(k p) c -> p k c", p=128)) for m in range(2)]
            items = []
            order = [('p', b, kc, j) for b in range(1, 9) for kc in range(2) for j in range(2)]
            samp = [('s', 0, kc, j) for kc in range(2) for j in range(2)]
            for pos_, sd in zip((1, 5, 9, 14), samp):
                order.insert(pos_, sd)
            for (kind_, b, kc, j) in order:
                if kind_ == 's':
                    items.append(make_sample_item(len(items), kc, j))
                else:
                    items.append(make_prompt_item(len(items), b, kc, j))
            NI = len(items)
            for t in range(NI + 4):
                def g(d):
                    n = t - d
                    if 0 <= n < NI:
                        return {k: (lambda f=f, n=n: (CUR.__setitem__('sq', n % 4), f())) for k, f in items[n].items()}
                    return None
                if g(0): g(0)['s1pe']()
                if g(4): g(4)['s4']()
                if g(3): g(3)['s3']()
                if g(2): g(2)['s2b']()
                if g(1): g(1)['s2a']()
                if g(0): g(0)['s1dve']()
                if t == 8:
                    chk('SATT')
            chk('ATT')
            flx = P.fence('flex')
            pend_nt = []
            scf = P.fence('scr')
            for b in range(1, 10):
                rows = rows_of(b)
                src = xs[:, :] if b == 9 else xp[(b - 1) * 128:b * 128, :]
                P.op('sp', lambda e, src=src, b=b, rows=rows: e.dma_start(out=xres[:rows, b - 1, :], in_=src), W=[('xres', b, m) for m in range(4)], dma=True, extra=flx)
            ostep = 0
            for m in range(4):
                s0 = wout_s[m] if m < 2 else load_pair(w_out[:, 512 * m:512 * m + 512].rearrange("(k p) c -> p k c", p=128), extra=P.fence('ringhi'))
                for b in range(1, 10):
                    rows = rows_of(b)
                    bank = 4 + ostep % 4
                    ostep += 1

                    def mm(e, s0=s0, b=b, rows=rows, bank=bank):
                        for k in range(16):
                            ins = e.matmul(ps[:rows, bank, :], lhsT=acts[:, k, b * 128:b * 128 + rows],
                                           rhs=p16(s0)[:, k, :], start=(k == 0), stop=(k == 15))
                        return ins
                    P.op('pe', mm, R=[('ring', s0), ('ring', s0 + 1)] + [('acts', k, b) for k in range(16)], W=[('bank', bank)])
                    P.op('dve', lambda e, b=b, m=m, rows=rows, bank=bank: e.tensor_tensor(out=xres[:rows, b - 1, 512 * m:512 * (m + 1)], in0=ps[:rows, bank, :],
                                                                                          in1=xres[:rows, b - 1, 512 * m:512 * (m + 1)], op=ALU.add),
                         R=[('bank', bank), ('xres', b, m)], W=[('xres', b, m)])
                    if m == 3 and b >= 2:
                        bb = b - 1
                        while pend_nt:
                            pend_nt.pop(0)()
                        norm_T(bb, xres[:rows_of(bb), bb - 1, :], [('xres', bb, mm_) for mm_ in range(4)], gffnT, 30 + bb, extra=scf, defer=pend_nt)
            while pend_nt:
                pend_nt.pop(0)()
            norm_T(9, xres[:64, 8, :], [('xres', 9, mm_) for mm_ in range(4)], gffnT, 39, extra=scf)
            chk('D')
            chk('D2')
            chk('D2')
            TT2 = [(128, 384, [1, 2, 3]), (512, 384, [4, 5, 6]), (896, 320, [7, 8, 9])]
            H2 = [('acts', k, b) for k in range(16) for b in range(1, 10)]

            def d2(s):
                return ring[:, s, :].rearrange("p (kk n) -> p kk n", kk=2)
            dstep = 0
            sg_i = 0
            pend_fin = []
            for g in range(11):
                gp = g % 2
                gs, us_, ds = [], [], []
                for u in range(2):
                    c0 = 512 * g + 256 * u
                    gs.append(load_unit(u16, w_gate[:, c0:c0 + 256].rearrange("(k p) c -> p k c", p=128)))
                    us_.append(load_unit(u16, w_up[:, c0:c0 + 256].rearrange("(k p) c -> p k c", p=128)))
                for c in range(4):
                    u, cc = c // 2, c % 2
                    for tt, (t0, n, blks) in enumerate(TT2):
                        for which, sl, bk in (('g', gs[u], tt), ('u', us_[u], 3 + tt)):
                            def mm(e, sl=sl, cc=cc, t0=t0, n=n, bk=bk):
                                for k in range(16):
                                    ins = e.matmul(ps[:, bk, :n], lhsT=u16(sl)[:, k, cc * 128:(cc + 1) * 128], rhs=acts[:, k, t0:t0 + n], start=(k == 0), stop=(k == 15))
                                return ins
                            last_gu = P.op('pe', mm, R=[('ring', sl)] + [('acts', k, bb) for k in range(16) for bb in blks], W=[('bank', bk)])
                        sq = sg_i % 3
                        sg_i += 1
                        P.op('act', lambda e, sq=sq, tt=tt, n=n: e.activation(out=sgt[sq][:, :n], in_=ps[:, tt, :n], func=AF.Silu), R=[('bank', tt)], W=[('sgt', sq)], reg='scr')
                        P.op('dve', lambda e, sq=sq, tt=tt, n=n, t0=t0, c=c, gp=gp: e.tensor_tensor(out=actT[gp][:, c, t0 - 128:t0 - 128 + n], in0=sgt[sq][:, :n], in1=ps[:, 3 + tt, :n], op=ALU.mult),
                             R=[('sgt', sq), ('bank', 3 + tt)], W=[('actT', gp, c, tt)], reg='scr')
                for u in range(2):
                    r0 = 512 * g + 256 * u
                    ds.append(load_unit(d2, w_down[r0:r0 + 256, :].rearrange("(kk p) n -> p kk n", p=128)))
                if g == 9:
                    ds9, gp9 = list(ds), gp
                    continue
                chunks = [(gp, c, ds[c // 2], c % 2) for c in range(4)]
                if g == 10:
                    chunks = [(gp9, c, ds9[c // 2], c % 2) for c in range(4)] + chunks
                    accf = acts[:].rearrange("p k t -> p (k t)").bitcast(F32)
                    gfin = accf[:, 0:2048]
                    ystg = [accf[:, 2048:4096], accf[:, 4096:6144]]
                    P.op('sp', lambda e: e.dma_start(out=gfin, in_=gfin_d[:, :]), W=['gfinA'], dma=True, extra=[last_gu])
                for b in range(1, 10):
                    rows = rows_of(b)
                    tcol = (b - 1) * 128
                    ttb = min((b - 1) // 3, 2)
                    for m in range(4):
                        bank = (dstep % 8) if g == 10 else 6 + dstep % 2
                        dstep += 1

                        def mm(e, chunks=tuple(chunks), tcol=tcol, rows=rows, m=m, bank=bank):
                            for i_, (gpp, c, slot, kk) in enumerate(chunks):
                                ins = e.matmul(ps[:rows, bank, :], lhsT=actT[gpp][:, c, tcol:tcol + rows], rhs=d2(slot)[:, kk, 512 * m:512 * (m + 1)],
                                               start=(i_ == 0), stop=(i_ == len(chunks) - 1))
                            return ins
                        P.op('pe', mm, R=sorted(set(('ring', sl_) for (_, _, sl_, _) in chunks)) + [('actT', gpp, c, ttb) for (gpp, c, _, _) in chunks], W=[('bank', bank)], reg='scr')
                        P.op('dve', lambda e, b=b, m=m, rows=rows, bank=bank: e.tensor_tensor(out=xres[:rows, b - 1, 512 * m:512 * (m + 1)], in0=ps[:rows, bank, :],
                                                                                              in1=xres[:rows, b - 1, 512 * m:512 * (m + 1)], op=ALU.add),
                             R=[('bank', bank), ('xres', b, m)], W=[('xres', b, m)])
                    if g == 10:
                        while pend_fin:
                            pend_fin.pop(0)()
                    if g == 10:
                      def fin(b=b, rows=rows):
                        sc = 60 + b
                        q = b % 2
                        xk = [('xres', b, m) for m in range(4)]
                        yk = ('ystgA', q)
                        P.op('act', lambda e, b=b, rows=rows, sc=sc, q=q: e.activation(out=ystg[q][:rows, :], in_=xres[:rows, b - 1, :], func=AF.Square, accum_out=stat[:rows, sc:sc + 1]),
                             R=xk, W=[yk, ('st', sc)], extra=[last_gu])
                        P.op('act', lambda e, rows=rows, sc=sc: e.activation(out=stat[:rows, sc + 10:sc + 11], in_=stat[:rows, sc:sc + 1], func=AF.Sqrt, bias=epst[:rows], scale=1.0 / D),
                             R=[('st', sc), 'eps'], W=[('st', sc + 10)])
                        P.op('dve', lambda e, rows=rows, sc=sc: e.reciprocal(out=stat[:rows, sc + 20:sc + 21], in_=stat[:rows, sc + 10:sc + 11]), R=[('st', sc + 10)], W=[('st', sc + 20)])
                        P.op('dve', lambda e, b=b, rows=rows, sc=sc, q=q: e.scalar_tensor_tensor(out=ystg[q][:rows, :], in0=xres[:rows, b - 1, :], scalar=stat[:rows, sc + 20:sc + 21],
                                                                                                 in1=gfin[:rows, :], op0=ALU.mult, op1=ALU.mult),
                             R=xk + [('st', sc + 20), 'gfinA'], W=[yk])
                        dst = ys[:, :] if b == 9 else yp[(b - 1) * 128:b * 128, :]
                        P.op('sp', lambda e, dst=dst, rows=rows, q=q: e.dma_start(out=dst, in_=ystg[q][:rows, :]), R=[yk], W=[('o_y', b)], dma=True)
                      pend_fin.append(fin)
            while pend_fin:
                pend_fin.pop(0)()

            chk('FFN')
            allout = ['o_kp', 'o_vp', 'o_ks_c', 'o_vs_c', 'o_ps_c', 'o_pp'] + [('o_ks', t) for t in range(4)] + [('o_vs', t) for t in range(4)] + \
                     [('o_ps', t) for t in range(4)] + [('o_y', b) for b in range(1, 10)]
            P.op('sp', None, R=allout)

        except _Stop:
            P.op('sp', None, R=list(P.lw.keys()))
        P.resolve()
        P.emit(block, sems, dsems)
    return nc


_CACHE = {}


def _rope_tab(pos):
    inv = (np.float32(500000.0) ** (-np.arange(0, 16, 2, dtype=np.float32) / np.float32(16))).astype(np.float32)
    ang = pos.astype(np.float32)[:, None] * inv[None, :]
    cos = np.cos(ang).astype(np.float32)
    sin = np.sin(ang).astype(np.float32)
    return np.concatenate([cos, cos, sin, sin], axis=1)


def _prepare(x_prompt, x_sample, state_pool, cache_k_win, cache_v_win, g_mix, w_in, w_pool,
             pool_scale, attn_sinks, w_out, g_ffn, w_gate, w_up, w_down, g_final):
    f = lambda a: np.ascontiguousarray(np.asarray(a, dtype=np.float32))
    x_prompt, x_sample, state_pool = f(x_prompt), f(x_sample), f(state_pool)
    ck_all, cv_all = f(cache_k_win)[0].reshape(128, 128, 256), f(cache_v_win)[0].reshape(128, 128, 256)
    g_mix, g_ffn, g_final = f(g_mix)[0], f(g_ffn)[0], f(g_final)
    w_in0, w_pool0, w_out0 = f(w_in)[0], f(w_pool)[0], f(w_out)[0]
    w_gate0, w_up0, w_down0 = f(w_gate)[0], f(w_up)[0], f(w_down)[0]
    pscale, sinks = f(pool_scale)[0], f(attn_sinks)[0]
    qcols = np.concatenate([1024 + 64 * h + np.arange(64) for h in PERM])
    w_in_p = np.ascontiguousarray(np.concatenate([w_in0[:, :1024], w_in0[:, qcols], w_in0[:, 2048:]], axis=1))
    w_out_p = np.ascontiguousarray(np.concatenate([w_out0[:1024], w_out0[qcols]], axis=0))
    sinks_p = sinks[PERM]

    qi = np.arange(128)[:, None]
    kj = np.arange(256)[None, :]
    diff = qi + 128 - kj
    mn = np.where((diff >= 0) & (diff <= 128), 0.0, NEG).astype(np.float32)
    mf = mn.copy()
    mf[:, :128] = NEG
    r = np.arange(64)
    tr, sr = r // 16, r % 16
    ms = np.full((128, 256), NEG, np.float32)
    jj = np.arange(128)
    ms[:64, :128] = np.where(jj[None, :] >= tr[:, None], 0.0, NEG)
    n = np.arange(64)
    tn, sn = n // 16, n % 16
    ms[:64, 128:192] = np.where((sn[None, :] == sr[:, None]) & (tn[None, :] <= tr[:, None]), 0.0, NEG)
    bmf = np.zeros((16, 64), np.float32)
    bmf[sr, r] = 1.0
    bm = np.broadcast_to(bmf.reshape(1, 1024), (128, 1024)).astype(ml_dtypes.bfloat16)
    gfin_b = np.ascontiguousarray(np.broadcast_to(g_final[None, :], (128, D)))

    in_maps = []
    for c in range(NCORE):
        seq, ch = c // 4, c % 4
        t0 = ch * 1024
        xpc = x_prompt[seq, t0:t0 + 1024]
        xhc = x_prompt[seq, t0 - 128:t0] if ch > 0 else np.zeros((128, D), np.float32)
        xsc = np.ascontiguousarray(x_sample[16 * c:16 * c + 16].transpose(1, 0, 2).reshape(64, D))
        cst = np.zeros((128, CW), np.float32)
        cst[:, C_GMIX:C_GMIX + 16] = g_mix.reshape(16, 128).T
        cst[:, C_GFFN:C_GFFN + 16] = g_ffn.reshape(16, 128).T
        cst[:, C_PSC:C_PSC + 8] = pscale.reshape(8, 128).T
        cst[:, C_SINK:C_SINK + 16] = sinks_p[None, :]
        cst[:, C_NSINK:C_NSINK + 16] = np.negative(sinks_p)[None, :]
        cst[:, C_MN:C_MN + 256] = mn
        cst[:, C_MF:C_MF + 256] = mn if ch > 0 else mf
        cst[:, C_MS:C_MS + 256] = ms
        pos = np.zeros((10, 128), np.int64)
        pos[:9] = (t0 - 128 + np.arange(1152)).reshape(9, 128)
        pos[9, :64] = PAST + tr
        rt = _rope_tab(pos.reshape(-1)).reshape(10, 128, 32).transpose(1, 0, 2)
        cst[:, C_ROPE:C_ROPE + 320] = rt.reshape(128, 320)
        iv = np.zeros((8, 16), np.float32)
        for cch in range(8):
            w = 2 ** (cch // 2 + 1)
            p = np.arange(16) + (t0 if ch > 0 else 0)
            iv[cch] = 1.0 / np.minimum(p + 1, w).astype(np.float32)
        cst[:, C_INVC:C_INVC + 128] = iv.reshape(1, 128)
        in_maps.append({
            "xp": np.ascontiguousarray(xpc), "xh": np.ascontiguousarray(xhc), "xs": xsc,
            "sp": np.ascontiguousarray(state_pool[0, 16 * c:16 * c + 16]),
            "ck": np.ascontiguousarray(ck_all[16 * c:16 * c + 16]), "cv": np.ascontiguousarray(cv_all[16 * c:16 * c + 16]),
            "w_in": w_in_p, "w_pool": w_pool0, "w_out": w_out_p, "w_gate": w_gate0, "w_up": w_up0, "w_down": w_down0,
            "cst": cst, "gfin": gfin_b, "bm": bm,
        })
    return in_maps


def kernel(**inputs):
    in_maps = _prepare(**inputs)
    if 'nc' not in _CACHE:
        _CACHE['nc'] = build_program()
    nc = _CACHE['nc']
    res = run_bass_kernel_spmd(nc, in_maps, core_ids=list(range(NCORE)))
    return _assemble(res.results)


def _assemble(R):
    y_prompt = np.stack([np.concatenate([R[4 * s + ch]["yp"] for ch in range(4)], axis=0) for s in range(2)], axis=0)
    y_sample = np.concatenate([R[c]["ys"].reshape(4, 16, D).transpose(1, 0, 2) for c in range(NCORE)], axis=0)
    new_pool_p = np.stack([R[3]["pool_p"], R[7]["pool_p"]], axis=0)[None]
    new_k_p = np.stack([R[3]["k_p"], R[7]["k_p"]], axis=0).reshape(1, 2, 128, 4, 64)
    new_v_p = np.stack([R[3]["v_p"], R[7]["v_p"]], axis=0).reshape(1, 2, 128, 4, 64)
    new_pool_s = np.concatenate([R[c]["pool_s"] for c in range(NCORE)], axis=0)[None]
    new_k_s = np.concatenate([R[c]["k_s"] for c in range(NCORE)], axis=0).reshape(1, 128, 128, 4, 64)
    new_v_s = np.concatenate([R[c]["v_s"] for c in range(NCORE)], axis=0).reshape(1, 128, 128, 4, 64)
    asf = lambda a: np.ascontiguousarray(a, dtype=np.float32)
    return (asf(y_prompt), asf(y_sample), asf(new_pool_p), asf(new_k_p), asf(new_v_p), asf(new_pool_s), asf(new_k_s), asf(new_v_s))
```
